# Optimizing a Trainium2 kernel written in Bass

```python
import jax, jax.numpy as jnp
from jax import lax
import numpy as np

D_MODEL = 2048
BATCH = 2
SEQ = 8192
DEPTH = 4

N_BRANCH = 4
BRANCH_W = D_MODEL // 4
FOX_HEADS = 8
FOX_HD = BRANCH_W // FOX_HEADS
Q_BLOCK = 128
CONV_W = BRANCH_W
CONV_K = 31
GLA_HEADS = 4
GLA_DK = BRANCH_W // 2
GLA_DV = BRANCH_W
GLA_HK = GLA_DK // GLA_HEADS
GLA_HV = GLA_DV // GLA_HEADS
GLA_RANK = 16
GLA_TEMP = 16.0
GLA_CHUNK = 16
POOL_GROUPS = 4
POOL_GW = BRANCH_W // POOL_GROUPS
POOL_WINDOWS = (2, 4, 8, 16)
D_FF = 5632
FFN_K = 3
EPS = 1e-6

IN_SIZES = (BRANCH_W, BRANCH_W, BRANCH_W, FOX_HEADS,
            2 * CONV_W,
            GLA_DK, GLA_DK, GLA_DV, GLA_RANK, GLA_DV,
            BRANCH_W,
            N_BRANCH * D_MODEL)
D_IN = sum(IN_SIZES)

kernel_name = "hybrid_fox_conformer_gla_pool_block"


def rmsnorm(x, g):
    xf = x.astype(jnp.float32)
    y = xf * lax.rsqrt(jnp.mean(xf * xf, axis=-1, keepdims=True) + EPS)
    return (y * g.astype(jnp.float32)).astype(x.dtype)


def layernorm(x, g, b):
    xf = x.astype(jnp.float32)
    mu = jnp.mean(xf, axis=-1, keepdims=True)
    var = jnp.mean(jnp.square(xf - mu), axis=-1, keepdims=True)
    y = (xf - mu) * lax.rsqrt(var + EPS)
    return (y * g.astype(jnp.float32) + b.astype(jnp.float32)).astype(x.dtype)


def causal_dwconv(x, w, b):
    K, C = w.shape
    y = lax.conv_general_dilated(x, w[:, None, :].astype(x.dtype), window_strides=(1,),
                                 padding=[(K - 1, 0)], dimension_numbers=('NWC', 'WIO', 'NWC'),
                                 feature_group_count=C)
    return y + b.astype(x.dtype)


def fox_mixer(q, k, v, f_logit, f_b, q_g, k_g):
    B, S, _ = q.shape
    dt = q.dtype
    def heads(t):
        return t.reshape(B, S, FOX_HEADS, FOX_HD).transpose(0, 2, 1, 3)
    qh = rmsnorm(heads(q), q_g)
    kh = rmsnorm(heads(k), k_g)
    vh = heads(v)
    logf = jax.nn.log_sigmoid((f_logit + f_b).astype(jnp.float32))
    c = jnp.cumsum(logf, axis=1).transpose(0, 2, 1)
    scale = FOX_HD ** -0.5
    outs = []
    for i in range(S // Q_BLOCK):
        lo, hi = i * Q_BLOCK, (i + 1) * Q_BLOCK
        s = jnp.einsum('bhtd,bhsd->bhts', qh[:, :, lo:hi], kh[:, :, :hi]).astype(jnp.float32) * scale
        s = s + c[:, :, lo:hi, None] - c[:, :, None, :hi]
        mask = jnp.arange(hi)[None, :] <= jnp.arange(lo, hi)[:, None]
        p = jax.nn.softmax(jnp.where(mask, s, -jnp.inf), axis=-1).astype(dt)
        outs.append(jnp.einsum('bhts,bhsd->bhtd', p, vh[:, :, :hi]))
    o = jnp.concatenate(outs, axis=2)
    return o.transpose(0, 2, 1, 3).reshape(B, S, BRANCH_W)


def conformer_conv(z, dw, db, ln_g, ln_b):
    a, gate = jnp.split(z, 2, axis=-1)
    u = a * jax.nn.sigmoid(gate)
    u = causal_dwconv(u, dw, db)
    u = layernorm(u, ln_g, ln_b)
    return jax.nn.silu(u)


def gla_mixer(q, k, v, a_low, r, wa, ba, og):
    B, S, _ = q.shape
    dt = q.dtype
    C = GLA_CHUNK
    N = S // C
    loga = jax.nn.log_sigmoid((a_low @ wa + ba).astype(jnp.float32)) / GLA_TEMP
    def chunks(t, d):
        return t.astype(jnp.float32).reshape(B, N, C, GLA_HEADS, d).transpose(0, 3, 1, 2, 4)
    qc = chunks(q, GLA_HK) * (GLA_HK ** -0.5)
    kc = chunks(k, GLA_HK)
    vc = chunks(v, GLA_HV)
    bc = jnp.cumsum(chunks(loga, GLA_HK), axis=3)
    b_last = bc[:, :, :, -1:, :]
    diff = bc[:, :, :, :, None, :] - bc[:, :, :, None, :, :]
    mask = jnp.tril(jnp.ones((C, C), dtype=bool))[:, :, None]
    attn = jnp.sum(qc[:, :, :, :, None, :] * kc[:, :, :, None, :, :]
                   * jnp.exp(jnp.where(mask, diff, -jnp.inf)), axis=-1)
    o_intra = jnp.einsum('bhnts,bhnsv->bhntv', attn, vc)
    q_dec = qc * jnp.exp(bc)
    k_dec = kc * jnp.exp(b_last - bc)
    decay = jnp.exp(b_last[:, :, :, 0, :])
    def step(state, xs):
        qd, kd, vv, dec = xs
        o = jnp.einsum('bhcd,bhdv->bhcv', qd, state)
        state = dec[..., None] * state + jnp.einsum('bhcd,bhcv->bhdv', kd, vv)
        return state, o
    xs = (jnp.moveaxis(q_dec, 2, 0), jnp.moveaxis(k_dec, 2, 0), jnp.moveaxis(vc, 2, 0), jnp.moveaxis(decay, 2, 0))
    state0 = jnp.zeros((B, GLA_HEADS, GLA_HK, GLA_HV), jnp.float32)
    _, o_inter = lax.scan(step, state0, xs)
    o = o_intra + jnp.moveaxis(o_inter, 0, 2)
    o = o.transpose(0, 2, 3, 1, 4).reshape(B, S, GLA_HEADS, GLA_HV)
    o = o * lax.rsqrt(jnp.mean(o * o, axis=-1, keepdims=True) + EPS)
    o = o * og.astype(jnp.float32).reshape(GLA_HEADS, GLA_HV)
    o = o.reshape(B, S, GLA_DV) * jax.nn.silu(r.astype(jnp.float32))
    return o.astype(dt)


def pool_mixer(u, pw, scale):
    B, S, _ = u.shape
    uf = u.astype(jnp.float32)
    cs = jnp.concatenate([jnp.zeros((B, 1, BRANCH_W), jnp.float32), jnp.cumsum(uf, axis=1)], axis=1)
    hi = jnp.arange(1, S + 1)
    outs = []
    for gi, w in enumerate(POOL_WINDOWS):
        sl = slice(gi * POOL_GW, (gi + 1) * POOL_GW)
        lo = jnp.maximum(hi - w, 0)
        csg = cs[:, :, sl]
        cnt = (hi - lo).astype(jnp.float32)[None, :, None]
        mixed = (csg[:, hi] - csg[:, lo]) / cnt - uf[:, :, sl]
        outs.append(mixed.astype(u.dtype) @ pw[gi])
    return jnp.concatenate(outs, axis=-1) * scale


def hybrid_layer(x, norm1_g, w_in, fox_fb, fox_qg, fox_kg, conv_dw, conv_db, conv_ln_g, conv_ln_b,
                 gla_wa, gla_ba, gla_og, pool_w, pool_scale, gate_b, w_branch, w_out,
                 norm2_g, ffn_up, ffn_dw, ffn_db, ffn_down):
    B, S, D = x.shape
    h = rmsnorm(x, norm1_g)
    z = h @ w_in
    split_at = [int(v) for v in np.cumsum(IN_SIZES)[:-1]]
    fq, fk, fv, ff, cz, gq, gk, gv, ga, gr, pz, gt = jnp.split(z, split_at, axis=-1)
    o_a = fox_mixer(fq, fk, fv, ff, fox_fb, fox_qg, fox_kg)
    o_b = conformer_conv(cz, conv_dw, conv_db, conv_ln_g, conv_ln_b)
    o_c = gla_mixer(gq, gk, gv, ga, gr, gla_wa, gla_ba, gla_og)
    o_d = pool_mixer(pz, pool_w, pool_scale)
    o = jnp.stack([o_a, o_b, o_c, o_d], axis=2)
    br = jnp.einsum('bsnw,nwd->bsnd', o, w_branch)
    gates = jax.nn.sigmoid(gt + gate_b).reshape(B, S, N_BRANCH, D)
    y = jnp.sum(gates * br, axis=2)
    x = x + y @ w_out
    h2 = rmsnorm(x, norm2_g)
    u = causal_dwconv(h2 @ ffn_up, ffn_dw, ffn_db)
    a, v = jnp.split(u, 2, axis=-1)
    return x + (jax.nn.silu(a) * v) @ ffn_down


def setup_inputs(seed: int = 0) -> dict:
    key = jax.random.key(seed)
    ks = jax.random.split(key, 24)
    L, D, f32 = DEPTH, D_MODEL, jnp.float32
    nrm = lambda k, shape, s: jax.random.normal(k, shape, f32) * s
    res = (2.0 * DEPTH) ** -0.5
    return {
        "x": nrm(ks[0], (BATCH, SEQ, D), 1.0),
        "norm1_g": 1.0 + nrm(ks[1], (L, D), 0.02),
        "w_in": nrm(ks[2], (L, D, D_IN), D ** -0.5),
        "fox_fb": 3.0 + nrm(ks[3], (L, FOX_HEADS), 0.1),
        "fox_qg": 1.0 + nrm(ks[4], (L, FOX_HD), 0.02),
        "fox_kg": 1.0 + nrm(ks[5], (L, FOX_HD), 0.02),
        "conv_dw": nrm(ks[6], (L, CONV_K, CONV_W), CONV_K ** -0.5),
        "conv_db": nrm(ks[7], (L, CONV_W), 0.01),
        "conv_ln_g": 1.0 + nrm(ks[8], (L, CONV_W), 0.02),
        "conv_ln_b": nrm(ks[9], (L, CONV_W), 0.01),
        "gla_wa": nrm(ks[10], (L, GLA_RANK, GLA_DK), GLA_RANK ** -0.5),
        "gla_ba": nrm(ks[11], (L, GLA_DK), 0.01),
        "gla_og": 1.0 + nrm(ks[12], (L, GLA_DV), 0.02),
        "pool_w": nrm(ks[13], (L, POOL_GROUPS, POOL_GW, POOL_GW), POOL_GW ** -0.5),
        "pool_scale": 1.0 + nrm(ks[14], (L, BRANCH_W), 0.1),
        "gate_b": nrm(ks[15], (L, N_BRANCH * D), 0.01),
        "w_branch": nrm(ks[16], (L, N_BRANCH, BRANCH_W, D), BRANCH_W ** -0.5),
        "w_out": nrm(ks[17], (L, D, D), D ** -0.5 * res),
        "norm2_g": 1.0 + nrm(ks[18], (L, D), 0.02),
        "ffn_up": nrm(ks[19], (L, D, 2 * D_FF), D ** -0.5),
        "ffn_dw": nrm(ks[20], (L, FFN_K, 2 * D_FF), FFN_K ** -0.5),
        "ffn_db": nrm(ks[21], (L, 2 * D_FF), 0.01),
        "ffn_down": nrm(ks[22], (L, D_FF, D), D_FF ** -0.5 * res),
    }


def reference(x, norm1_g, w_in, fox_fb, fox_qg, fox_kg, conv_dw, conv_db, conv_ln_g, conv_ln_b,
              gla_wa, gla_ba, gla_og, pool_w, pool_scale, gate_b, w_branch, w_out,
              norm2_g, ffn_up, ffn_dw, ffn_db, ffn_down):
    for l in range(DEPTH):
        x = hybrid_layer(x, norm1_g[l], w_in[l], fox_fb[l], fox_qg[l], fox_kg[l],
                         conv_dw[l], conv_db[l], conv_ln_g[l], conv_ln_b[l],
                         gla_wa[l], gla_ba[l], gla_og[l], pool_w[l], pool_scale[l],
                         gate_b[l], w_branch[l], w_out[l], norm2_g[l],
                         ffn_up[l], ffn_dw[l], ffn_db[l], ffn_down[l])
    return x
```

```python
import contextlib
import numpy as np
import concourse.bass as bass
import concourse.mybir as mybir
from concourse.bass_utils import run_bass_kernel_spmd

F32 = mybir.dt.float32
BF16 = mybir.dt.bfloat16
ALU = mybir.AluOpType
AF = mybir.ActivationFunctionType

NCORES = 2
DEPTH = 4
D = 2048
T = 8192
SG = 2048
NSG = T // SG
D_IN = 12824
DFF = 5632
EPS = 1e-6
NEG = -30000.0

C_FQ, C_FK, C_FV, C_FF, C_CA, C_CG, C_GQ, C_GK, C_GV, C_GA, C_GR, C_PZ, C_GT = (
    0, 512, 1024, 1536, 1544, 2056, 2568, 2824, 3080, 3592, 3608, 4120, 4632)
ZROWS = 4632


class DSem:
    __slots__ = ("sem", "count", "key")

    def __init__(self, sem, key):
        self.sem = sem
        self.count = 0
        self.key = key


class Buf:
    __slots__ = ("name", "w", "r")

    def __init__(self, name=""):
        self.name = name
        self.w = {}
        self.r = {}


class Stream:
    def __init__(self, fw, key, eng, sem):
        self.fw = fw
        self.key = key
        self.eng = eng
        self.sem = sem
        self.cnt = 0
        self.seen = {}

    def need(self, deps):
        for (k, sem, val) in deps:
            if k == self.key and self.key == "pe":
                continue
            d = self.fw.dkey.get(k)
            if d is not None:
                val = max(val, d.count * 16)
            if val > self.seen.get(k, 0):
                self.seen[k] = val
                self.eng.wait_ge(sem, val)


class FW:
    def __init__(self, nc, esems, dsems):
        self.nc = nc
        self.s = {}
        for key, eng in (("pe", nc.tensor), ("act", nc.scalar), ("dve", nc.vector),
                         ("pool", nc.gpsimd), ("sp", nc.sync)):
            self.s[key] = Stream(self, key, eng, esems[key])
        self.dsems = [DSem(s, "d%d" % i) for i, s in enumerate(dsems)]
        self.dkey = {d.key: d for d in self.dsems}
        self.dnext = 0

    def dsem(self):
        d = self.dsems[self.dnext % len(self.dsems)]
        self.dnext += 1
        return d

    @staticmethod
    def _deps(reads, writes):
        deps = []
        for b in reads:
            deps.extend(b.w.values())
        for b in writes:
            deps.extend(b.w.values())
            deps.extend(b.r.values())
        return deps

    @staticmethod
    def _mark(tok, reads, writes):
        for b in reads:
            b.r[tok[0]] = tok
        for b in writes:
            b.w[tok[0]] = tok
            b.r = {}

    def op(self, key, fn, reads=(), writes=()):
        st = self.s[key]
        st.need(self._deps(reads, writes))
        inst = fn(st.eng)
        st.cnt += 1
        inst.then_inc(st.sem, 1)
        self._mark((key, st.sem, st.cnt), reads, writes)
        return inst

    def dma(self, qkey, ds, out, in_, reads=(), writes=(), **kw):
        st = self.s[qkey]
        deps = self._deps(reads, writes)
        st.need(deps)
        inst = st.eng.dma_start(out=out, in_=in_, **kw)
        ds.count += 1
        inst.then_inc(ds.sem, 16)
        self._mark((ds.key, ds.sem, ds.count * 16), reads, writes)
        return inst

    def barrier(self):
        toks = []
        for k, st in self.s.items():
            if st.cnt:
                toks.append((k, st.sem, st.cnt))
        for d in self.dsems:
            if d.count:
                toks.append((d.key, d.sem, d.count * 16))
        for k, st in self.s.items():
            st.need(toks)
        self.dnext = 0


def _slices(total, step):
    return [(i, min(step, total - i)) for i in range(0, total, step)]


class Kern:
    def __init__(self, nl=DEPTH, debug=(), stop=None):
        self.NL = nl
        self.debug = set(debug)
        self.stop = stop

    def dram(self, name, shape, dt, kind=None):
        if kind is None and name in self.debug:
            kind = "ExternalOutput"
        if kind:
            return self.nc.dram_tensor(name, list(shape), dt, kind=kind)
        return self.nc.dram_tensor(name, list(shape), dt)

    def sb(self, st, name, shape, dt):
        self.uid += 1
        return st.enter_context(self.nc.sbuf_tensor("%s_%d" % (name, self.uid), list(shape), dt))

    def psum_tiles(self, st):
        self.ps = [st.enter_context(self.nc.psum_tensor("ps%d" % i, [128, 512], F32)) for i in range(8)]
        self.psb = [Buf("ps%d" % i) for i in range(8)]
        self.psn = 0

    def pst(self):
        i = self.psn % 4
        self.psn += 1
        return self.ps[i], self.psb[i]

    def dump(self, name, ap, shape, dt, buf):
        if ("dump_" + name) not in self.debug:
            return
        t = self.nc.dram_tensor("dump_" + name, list(shape), dt, kind="ExternalOutput")
        self.fw.dma("sp", self.fw.dsem(), t[tuple(slice(None) for _ in shape)], ap, reads=[buf], writes=[Buf()])

    def load_w(self, W, wb, c0, n, wt, wtb, ds, q="sp"):
        src = W[:, c0:c0 + n].rearrange("(k p) c -> p k c", p=128)
        self.fw.dma(q, ds, wt[:, :, 0:n], src, reads=[wb], writes=[wtb])

    def load_cols(self, dst, dstb, src_vec, n, ds=None):
        fw = self.fw
        ds = ds or fw.dsem()
        fw.dma("sp", ds, dst[:, 0:n], src_vec.rearrange("(k p) -> p k", p=128), writes=[dstb],
               allow_slow_non_contiguous=True)

    def build(self):
        nc = bass.Bass("TRN2", target_bir_lowering=False)
        self.nc = nc
        self.uid = 0
        NL = self.NL
        inp = lambda name, shape: nc.dram_tensor(name, list(shape), F32, kind="ExternalInput")
        self.x_in = inp("x", [T, D])
        self.p = {}
        for name, shape in (("w_in", [NL, D, D_IN]), ("w_branch", [NL, D, D]), ("w_out", [NL, D, D]),
                            ("ffn_up", [NL, D, 2 * DFF]), ("ffn_down", [NL, DFF, D]),
                            ("norm1_g", [NL, D]), ("fox_fb", [NL, 8]), ("fox_qg", [NL, 64]), ("fox_kg", [NL, 64]),
                            ("conv_dw", [NL, 31, 512]), ("conv_db", [NL, 512]), ("conv_ln_g", [NL, 512]),
                            ("conv_ln_b", [NL, 512]), ("gla_wa", [NL, 16, 256]), ("gla_ba", [NL, 256]),
                            ("gla_og", [NL, 512]), ("pool_w", [NL, 4, 128, 128]), ("pool_scale", [NL, 512]),
                            ("gate_b", [NL, 8192]), ("norm2_g", [NL, D]), ("ffn_dw", [NL, 3, 2 * DFF]),
                            ("ffn_db", [NL, 2 * DFF]), ("invcnt", [4, 16])):
            self.p[name] = inp(name, shape)
        self.y_out = nc.dram_tensor("y", [T, D], F32, kind="ExternalOutput")

        self.W = []
        for l in range(NL):
            self.W.append({
                "w_in": self.dram("b_w_in%d" % l, [D, D_IN], BF16), "w_branch": self.dram("b_w_br%d" % l, [D, D], BF16),
                "w_out": self.dram("b_w_out%d" % l, [D, D], BF16), "ffn_up": self.dram("b_f_up%d" % l, [D, 2 * DFF], BF16),
                "ffn_down": self.dram("b_f_dn%d" % l, [DFF, D], BF16)})
        self.Wb = [Buf("W%d" % l) for l in range(NL)]
        self.xT = self.dram("xT", [D, T], F32)
        self.zT = self.dram("zT", [ZROWS, T], F32)
        self.vtok = self.dram("vtok", [T, 1024], BF16)
        self.gatesT = self.dram("gatesT", [8192, T], BF16)
        self.oT = self.dram("oT", [D, T], BF16)
        self.rrow = self.dram("rrow", [8, T], BF16)
        self.b_xT, self.b_zT, self.b_vtok, self.b_gates, self.b_oT, self.b_rrow, self.b_y = (
            Buf("xT"), Buf("zT"), Buf("vtok"), Buf("gatesT"), Buf("oT"), Buf("rrow"), Buf("y"))

        with contextlib.ExitStack() as st:
            esems = {k: st.enter_context(nc.semaphore("s_" + k)) for k in ("pe", "act", "dve", "pool", "sp")}
            dsems = [st.enter_context(nc.semaphore("d%d" % i)) for i in range(40)]
            self.fw = FW(nc, esems, dsems)
            fw = self.fw
            self.psum_tiles(st)
            self.consts(st)
            fw.barrier()
            self.weights_cast()
            self.x_to_xT()
            for l in range(NL):
                self.layer(l)
            self.xT_to_y()
            fw.barrier()
        return nc

    def consts(self, st):
        nc, fw = self.nc, self.fw
        self.ident_f = self.sb(st, "ident_f", [128, 128], F32)
        self.ident_b = self.sb(st, "ident_b", [128, 128], BF16)
        self.ones_f = self.sb(st, "ones_f", [128, 128], F32)
        self.ones_b = self.sb(st, "ones_b", [128, 128], BF16)
        self.bo64 = self.sb(st, "bo64", [128, 128], F32)
        self.sel127 = self.sb(st, "sel127", [128, 128], F32)
        self.U_b = self.sb(st, "U_b", [128, 128], BF16)
        self.maskneg = self.sb(st, "maskneg", [128, 128], BF16)
        self.eps_col = self.sb(st, "eps_col", [128, 1], F32)
        self.zero_col = self.sb(st, "zero_col", [128, 1], F32)
        self.ones_big = self.sb(st, "ones_big", [128, 2048], F32)
        self.b_const = Buf("const")
        cb = [self.b_const]
        fw.op("pool", lambda e: e.memset(self.ones_f[:, :], 1.0), writes=cb)
        fw.op("pool", lambda e: e.memset(self.ones_b[:, :], 1.0), writes=cb)
        fw.op("pool", lambda e: e.memset(self.eps_col[:, :], EPS), writes=cb)
        fw.op("pool", lambda e: e.memset(self.zero_col[:, :], 0.0), writes=cb)
        fw.op("pool", lambda e: e.memset(self.ones_big[:, :], 1.0), writes=cb)
        fw.op("pool", lambda e: e.memset(self.bo64[:, :], 0.0), writes=cb)
        fw.op("pool", lambda e: e.memset(self.bo64[0:64, 0:64], 1.0), writes=cb)
        fw.op("pool", lambda e: e.memset(self.bo64[64:128, 64:128], 1.0), writes=cb)
        sel = lambda out, in_, op, base, cm, pat: fw.op("pool", lambda e: e.affine_select(
            out=out, in_=in_, pattern=pat, compare_op=op, fill=0.0, base=base, channel_multiplier=cm), reads=cb, writes=cb)
        sel(self.ident_f[:, :], self.ones_f[:, :], ALU.is_equal, 0, -1, [[1, 128]])
        sel(self.ident_b[:, :], self.ones_b[:, :], ALU.is_equal, 0, -1, [[1, 128]])
        sel(self.U_b[:, :], self.ones_b[:, :], ALU.is_ge, 0, -1, [[1, 128]])
        sel(self.sel127[:, :], self.ones_f[:, :], ALU.is_equal, -127, 1, [[0, 128]])
        fw.op("dve", lambda e: e.tensor_scalar(out=self.maskneg[:, :], in0=self.U_b[:, :], scalar1=-1.0, scalar2=-NEG,
                                               op0=ALU.add, op1=ALU.mult), reads=cb, writes=cb)

    def weights_cast(self):
        fw = self.fw
        for l in range(self.NL):
            ds = fw.dsem()
            for name, rows in (("w_in", D), ("w_branch", D), ("w_out", D), ("ffn_up", D), ("ffn_down", DFF)):
                nsplit = 8
                rs = rows // nsplit
                for i in range(nsplit):
                    fw.dma("pool", ds, self.W[l][name][i * rs:(i + 1) * rs, :], self.p[name][l, i * rs:(i + 1) * rs, :],
                           writes=[self.Wb[l]])
        fw.barrier()

    def x_to_xT(self):
        nc, fw = self.nc, self.fw
        with contextlib.ExitStack() as st:
            xin = [self.sb(st, "xin%d" % i, [128, D], F32) for i in range(2)]
            xinb = [Buf() for _ in range(2)]
            xo = [self.sb(st, "xo%d" % i, [128, 4, 128], F32) for i in range(4)]
            xob = [Buf() for _ in range(4)]
            dsl = [fw.dsem() for _ in range(2)]
            dso = [fw.dsem() for _ in range(4)]
            k = 0
            for tt in range(T // 128):
                s = tt % 2
                fw.dma("sp", dsl[s], xin[s][:, :], self.x_in[tt * 128:(tt + 1) * 128, :], writes=[xinb[s]])
                for dg in range(4):
                    ps, psb = self.pst()
                    for j in range(4):
                        dc = dg * 4 + j
                        fw.op("pe", lambda e, ps=ps, j=j, dc=dc, s=s: e.transpose(out=ps[:, j * 128:(j + 1) * 128],
                              in_=xin[s][:, dc * 128:(dc + 1) * 128], identity=self.ident_f[:, :]),
                              reads=[xinb[s], self.b_const], writes=[psb])
                    o = k % 4
                    k += 1
                    if k % 2:
                        fw.op("dve", lambda e, ps=ps, o=o: e.tensor_copy(out=xo[o][:, :, :].rearrange("p a b -> p (a b)"), in_=ps[:, :]),
                              reads=[psb], writes=[xob[o]])
                    else:
                        fw.op("act", lambda e, ps=ps, o=o: e.copy(out=xo[o][:, :, :].rearrange("p a b -> p (a b)"), in_=ps[:, :]),
                              reads=[psb], writes=[xob[o]])
                    dst = self.xT[dg * 512:(dg + 1) * 512, tt * 128:(tt + 1) * 128].rearrange("(j p) t -> p j t", p=128)
                    fw.dma("pool", dso[o], dst, xo[o][:, :, :], reads=[xob[o]], writes=[self.b_xT])
        fw.barrier()

    def xT_to_y(self):
        nc, fw = self.nc, self.fw
        with contextlib.ExitStack() as st:
            xin = [self.sb(st, "yin%d" % i, [128, 4, 512], F32) for i in range(2)]
            xinb = [Buf() for _ in range(2)]
            xo = [self.sb(st, "yo%d" % i, [128, 512], F32) for i in range(4)]
            xob = [Buf() for _ in range(4)]
            dsl = [fw.dsem() for _ in range(2)]
            dso = [fw.dsem() for _ in range(4)]
            k = 0
            n = 0
            for dg in range(4):
                for tg in range(T // 512):
                    s = n % 2
                    n += 1
                    src = self.xT[dg * 512:(dg + 1) * 512, tg * 512:(tg + 1) * 512].rearrange("(j p) t -> p j t", p=128)
                    fw.dma("sp", dsl[s], xin[s][:, :, :], src, reads=[self.b_xT], writes=[xinb[s]])
                    for ti in range(4):
                        ps, psb = self.pst()
                        for j in range(4):
                            fw.op("pe", lambda e, ps=ps, j=j, ti=ti, s=s: e.transpose(out=ps[:, j * 128:(j + 1) * 128],
                                  in_=xin[s][:, j, ti * 128:(ti + 1) * 128], identity=self.ident_f[:, :]),
                                  reads=[xinb[s], self.b_const], writes=[psb])
                        o = k % 4
                        k += 1
                        if k % 2:
                            fw.op("dve", lambda e, ps=ps, o=o: e.tensor_copy(out=xo[o][:, :], in_=ps[:, :]), reads=[psb], writes=[xob[o]])
                        else:
                            fw.op("act", lambda e, ps=ps, o=o: e.copy(out=xo[o][:, :], in_=ps[:, :]), reads=[psb], writes=[xob[o]])
                        tok0 = tg * 512 + ti * 128
                        fw.dma("pool", dso[o], self.y_out[tok0:tok0 + 128, dg * 512:(dg + 1) * 512], xo[o][:, :],
                               reads=[xob[o]], writes=[self.b_y])

    def layer(self, l):
        fw = self.fw
        stop = self.stop
        for sg in range(NSG):
            with contextlib.ExitStack() as st:
                hT = self.sb(st, "hT", [128, 16, SG], BF16)
                hTb = Buf("hT")
                self.rmsnorm_T("norm1_g", l, sg * SG, SG, hT, hTb, 0)
                self.in_proj(l, sg, hT, hTb)
                fw.barrier()
        if stop == "in_proj":
            return
        if "skip_fox" not in self.debug:
            self.fox(l)
        if stop == "fox":
            return
        if "skip_conv" not in self.debug:
            self.conv(l)
        if stop == "conv":
            return
        if "skip_gla" not in self.debug:
            self.gla(l)
        if stop == "gla":
            return
        if "skip_pool" not in self.debug:
            self.pool(l)
        if stop == "pool":
            return
        for sb in range(T // 1024):
            self.branch_out(l, sb)
        if stop == "branch":
            return
        self.ffn(l)

    def rmsnorm_T(self, gname, l, tok0, ntok, hT, hTb, off):
        nc, fw = self.nc, self.fw
        with contextlib.ExitStack() as st:
            xg = [self.sb(st, "nx%d" % i, [128, 16, 512], F32) for i in range(2)]
            xgb = [Buf() for _ in range(2)]
            sq = [self.sb(st, "nsq%d" % i, [128, 512], F32) for i in range(2)]
            sqb = [Buf() for _ in range(2)]
            rs = self.sb(st, "nrs", [128, 512], F32)
            rsb = Buf()
            gcol = self.sb(st, "ngc", [128, 16], F32)
            gcb = Buf()
            dsl = [fw.dsem() for _ in range(2)]
            self.load_cols(gcol, gcb, self.p[gname][l, :], 16)
            for tg in range(ntok // 512):
                s = tg % 2
                t0 = tok0 + tg * 512
                for dq in range(4):
                    src = self.xT[dq * 512:(dq + 1) * 512, t0:t0 + 512].rearrange("(j p) t -> p j t", p=128)
                    fw.dma("sp", dsl[s], xg[s][:, dq * 4:(dq + 1) * 4, :], src, reads=[self.b_xT], writes=[xgb[s]])
                ps, psb = self.pst()
                for dc in range(16):
                    q = dc % 2
                    fw.op("act", lambda e, s=s, dc=dc, q=q: e.activation(out=sq[q][:, :], in_=xg[s][:, dc, :], func=AF.Square),
                          reads=[xgb[s]], writes=[sqb[q]])
                    fw.op("pe", lambda e, ps=ps, q=q, dc=dc: e.matmul(ps[:, :], lhsT=self.ones_f[:, :], rhs=sq[q][:, :],
                                                                     start=(dc == 0), stop=(dc == 15)),
                          reads=[sqb[q], self.b_const], writes=[psb])
                fw.op("act", lambda e, ps=ps: e.activation(out=rs[:, :], in_=ps[:, :], func=AF.Sqrt, scale=1.0 / D, bias=self.eps_col[:, 0:1]),
                      reads=[psb, self.b_const], writes=[rsb])
                fw.op("dve", lambda e: e.reciprocal(out=rs[:, :], in_=rs[:, :]), reads=[rsb], writes=[rsb])
                for dc in range(16):
                    fw.op("dve", lambda e, s=s, dc=dc, tg=tg: e.scalar_tensor_tensor(
                        out=hT[:, dc, off + tg * 512:off + (tg + 1) * 512], in0=xg[s][:, dc, :], scalar=gcol[:, dc:dc + 1], in1=rs[:, :],
                        op0=ALU.mult, op1=ALU.mult), reads=[xgb[s], rsb, gcb], writes=[hTb])
        fw.barrier()

    def in_proj(self, l, sg, hT, hTb):
        nc, fw = self.nc, self.fw
        T0 = sg * SG
        chunks = []

        def seg(c0, w, kind):
            for (o, n) in _slices(w, 128):
                chunks.append((c0 + o, n, kind))
        seg(C_FQ, 512, "raw"); seg(C_FK, 512, "raw"); chunks.append((C_FV, 512, "vtok0"))
        seg(C_FF, 8, "raw"); seg(C_CA, 512, "raw"); seg(C_CG, 512, "sigmoid")
        seg(C_GQ, 256, "raw"); seg(C_GK, 256, "raw"); chunks.append((C_GV, 512, "vtok1"))
        seg(C_GA, 16, "raw"); seg(C_GR, 512, "silu"); seg(C_PZ, 512, "raw"); seg(C_GT, 8192, "gate")
        blocks = []
        cur = []
        for ch in chunks:
            if cur and (ch[0] + ch[1] - cur[0][0] > 512):
                blocks.append(cur); cur = []
            cur.append(ch)
        if cur:
            blocks.append(cur)
        W = self.W[l]["w_in"]
        with contextlib.ExitStack() as st:
            wt = [self.sb(st, "ipw%d" % i, [128, 16, 512], BF16) for i in range(2)]
            wtb = [Buf() for _ in range(2)]
            wds = [fw.dsem() for _ in range(2)]
            stg = [self.sb(st, "ips%d" % i, [128, 512], F32) for i in range(4)]
            stgb = [Buf() for _ in range(4)]
            sds = [fw.dsem() for _ in range(4)]
            stgh = [self.sb(st, "iph%d" % i, [128, 512], BF16) for i in range(4)]
            stghb = [Buf() for _ in range(4)]
            hds = [fw.dsem() for _ in range(4)]
            gb = self.sb(st, "ipgb", [128, 64], F32)
            gbb = Buf()
            self.load_cols(gb, gbb, self.p["gate_b"][l, :], 64)
            k32 = 0
            k16 = 0
            for bi, blk in enumerate(blocks):
                s = bi % 2
                c0 = blk[0][0]
                ncol = blk[-1][0] + blk[-1][1] - c0
                self.load_w(W, self.Wb[l], c0, ncol, wt[s], wtb[s], wds[s])
                for (cc, n, kind) in blk:
                    o = cc - c0
                    if kind.startswith("vtok"):
                        vo = 0 if kind == "vtok0" else 512
                        for tt in range(SG // 128):
                            ps, psb = self.pst()
                            for dc in range(16):
                                fw.op("pe", lambda e, ps=ps, dc=dc, tt=tt, s=s, o=o: e.matmul(
                                    ps[:, :], lhsT=hT[:, dc, tt * 128:(tt + 1) * 128], rhs=wt[s][:, dc, o:o + 512],
                                    start=(dc == 0), stop=(dc == 15)), reads=[hTb, wtb[s]], writes=[psb])
                            q = k16 % 4
                            k16 += 1
                            if k16 % 2:
                                fw.op("dve", lambda e, ps=ps, q=q: e.tensor_copy(out=stgh[q][:, :], in_=ps[:, :]), reads=[psb], writes=[stghb[q]])
                            else:
                                fw.op("act", lambda e, ps=ps, q=q: e.copy(out=stgh[q][:, :], in_=ps[:, :]), reads=[psb], writes=[stghb[q]])
                            fw.dma("pool", hds[q], self.vtok[T0 + tt * 128:T0 + (tt + 1) * 128, vo:vo + 512], stgh[q][:, :],
                                   reads=[stghb[q]], writes=[self.b_vtok])
                        continue
                    for tg in range(SG // 512):
                        tk = T0 + tg * 512
                        ps, psb = self.pst()
                        for dc in range(16):
                            fw.op("pe", lambda e, ps=ps, dc=dc, tg=tg, s=s, o=o, n=n: e.matmul(
                                ps[0:n, :], lhsT=wt[s][:, dc, o:o + n], rhs=hT[:, dc, tg * 512:(tg + 1) * 512],
                                start=(dc == 0), stop=(dc == 15)), reads=[hTb, wtb[s]], writes=[psb])
                        if kind == "gate":
                            q = k16 % 4
                            k16 += 1
                            gi = (cc - C_GT) // 128
                            fw.op("act", lambda e, ps=ps, q=q, gi=gi: e.activation(out=stgh[q][:, :], in_=ps[:, :], func=AF.Sigmoid,
                                                                                   bias=gb[:, gi:gi + 1]),
                                  reads=[psb, gbb], writes=[stghb[q]])
                            fw.dma("pool", hds[q], self.gatesT[cc - C_GT:cc - C_GT + 128, tk:tk + 512], stgh[q][:, :],
                                   reads=[stghb[q]], writes=[self.b_gates])
                        else:
                            q = k32 % 4
                            k32 += 1
                            if kind == "sigmoid":
                                fw.op("act", lambda e, ps=ps, q=q, n=n: e.activation(out=stg[q][0:n, :], in_=ps[0:n, :], func=AF.Sigmoid),
                                      reads=[psb], writes=[stgb[q]])
                            elif kind == "silu":
                                fw.op("act", lambda e, ps=ps, q=q, n=n: e.activation(out=stg[q][0:n, :], in_=ps[0:n, :], func=AF.Silu),
                                      reads=[psb], writes=[stgb[q]])
                            else:
                                fw.op("dve", lambda e, ps=ps, q=q, n=n: e.tensor_copy(out=stg[q][0:n, :], in_=ps[0:n, :]),
                                      reads=[psb], writes=[stgb[q]])
                            fw.dma("pool", sds[q], self.zT[cc:cc + n, tk:tk + 512], stg[q][0:n, :],
                                   reads=[stgb[q]], writes=[self.b_zT])

    def fox(self, l):
        nc, fw = self.nc, self.fw
        NTI = T // 128
        NG = T // 512
        with contextlib.ExitStack() as st:
            cpT = self.sb(st, "cpT", [128, NTI, 8], F32)
            cpTb = Buf()
            refT = self.sb(st, "refT", [128, NG, 8], F32)
            refTb = Buf()
            with contextlib.ExitStack() as st1:
                sp = self.sb(st1, "fsp", [8, T], F32)
                cp = self.sb(st1, "fcp", [8, T], F32)
                rb = self.sb(st1, "frb", [8, T], BF16)
                nfb = self.sb(st1, "nfb", [8, 1], F32)
                b_sp, b_cp, b_rb, b_nfb = Buf(), Buf(), Buf(), Buf()
                ds = fw.dsem()
                fw.dma("sp", ds, sp[:, :], self.zT[C_FF:C_FF + 8, :], reads=[self.b_zT], writes=[b_sp])
                fw.dma("sp", ds, nfb[:, :], self.p["fox_fb"][l, :].rearrange("(h o) -> h o", o=1), writes=[b_nfb])
                fw.op("dve", lambda e: e.tensor_scalar(out=nfb[:, :], in0=nfb[:, :], scalar1=-1.0, scalar2=None, op0=ALU.mult),
                      reads=[b_nfb], writes=[b_nfb])
                for (o, n) in _slices(T, 2048):
                    fw.op("act", lambda e, o=o, n=n: e.activation(out=sp[:, o:o + n], in_=sp[:, o:o + n], func=AF.Exp, scale=-1.0, bias=nfb[:, 0:1]),
                          reads=[b_sp, b_nfb], writes=[b_sp])
                for (o, n) in _slices(T, 2048):
                    fw.op("act", lambda e, o=o, n=n: e.activation(out=sp[:, o:o + n], in_=sp[:, o:o + n], func=AF.Ln, bias=self.ones_f[0:8, 0:1]),
                          reads=[b_sp, self.b_const], writes=[b_sp])
                for i, (o, n) in enumerate(_slices(T, 2048)):
                    init = 0.0 if i == 0 else cp[:, o - 1:o]
                    self._cumsum(cp, sp, o, n, init, b_sp, b_cp)
                for G in range(NG):
                    ref = self.zero_col[0:8, 0:1] if G == 0 else cp[:, G * 512 - 1:G * 512]
                    fw.op("dve", lambda e, G=G, ref=ref: e.tensor_scalar(out=rb[:, G * 512:(G + 1) * 512], in0=cp[:, G * 512:(G + 1) * 512],
                                                                        scalar1=ref, scalar2=-1.0, op0=ALU.subtract, op1=ALU.mult),
                          reads=[b_cp, self.b_const], writes=[b_rb])
                ds2 = fw.dsem()
                fw.dma("pool", ds2, self.rrow[:, :], rb[:, :], reads=[b_rb], writes=[self.b_rrow])
                for t4 in range(NTI // 16):
                    ps, psb = self.pst()
                    for j in range(16):
                        ti = t4 * 16 + j
                        fw.op("pe", lambda e, ps=ps, j=j, ti=ti: e.transpose(out=ps[:, j * 8:(j + 1) * 8], in_=cp[:, ti * 128:(ti + 1) * 128],
                                                                            identity=self.ident_f[0:8, 0:8]),
                              reads=[b_cp, self.b_const], writes=[psb])
                    fw.op("dve", lambda e, ps=ps, t4=t4: e.tensor_copy(out=cpT[:, t4 * 16:(t4 + 1) * 16, :].rearrange("p a b -> p (a b)"),
                                                                        in_=ps[:, 0:128]), reads=[psb], writes=[cpTb])
                ps, psb = self.pst()
                fw.op("pe", lambda e, ps=ps: e.matmul(ps[:, 0:(NG - 1) * 8].rearrange("p (a b) -> p a b", b=8), lhsT=self.sel127[:, :],
                                                      rhs=cpT[:, :, :].rearrange("p (g f) h -> p g f h", f=4)[:, 0:NG - 1, 3, :], start=True, stop=True),
                      reads=[cpTb, self.b_const], writes=[psb])
                fw.op("dve", lambda e: e.memset(refT[:, 0, :], 0.0), writes=[refTb])
                fw.op("dve", lambda e, ps=ps: e.tensor_copy(out=refT[:, 1:NG, :].rearrange("p a b -> p (a b)"), in_=ps[:, 0:(NG - 1) * 8]),
                      reads=[psb], writes=[refTb])
            fw.barrier()
            V = self.sb(st, "fV", [128, NTI, 512], BF16)
            Vb = Buf()
            ds = fw.dsem()
            for q4 in range(4):
                n4 = NTI // 4
                fw.dma("sp", ds, V[:, q4 * n4:(q4 + 1) * n4, :],
                       self.vtok[q4 * n4 * 128:(q4 + 1) * n4 * 128, 0:512].rearrange("(n p) c -> p n c", p=128),
                       reads=[self.b_vtok], writes=[Vb])
            qpad = [self.sb(st, "qpad%d" % e, [128, T], BF16) for e in range(2)]
            kpad = [self.sb(st, "kpad%d" % e, [128, T], BF16) for e in range(2)]
            qpb = [Buf() for _ in range(2)]
            kpb = [Buf() for _ in range(2)]
            for e_ in range(2):
                fw.op("pool", lambda e, e_=e_: e.memset(qpad[e_][:, :], 0.0), writes=[qpb[e_]])
                fw.op("pool", lambda e, e_=e_: e.memset(kpad[e_][:, :], 0.0), writes=[kpb[e_]])
            fw.op("pool", lambda e: e.memset(kpad[0][64:65, :], 1.0), writes=[kpb[0]])
            fw.op("pool", lambda e: e.memset(kpad[1][0:1, :], 1.0), writes=[kpb[1]])
            gq = self.sb(st, "fgq", [128, 1], F32)
            gk = self.sb(st, "fgk", [128, 1], F32)
            gb_ = Buf()
            ds = fw.dsem()
            for half in range(2):
                fw.dma("sp", ds, gq[half * 64:(half + 1) * 64, :], self.p["fox_qg"][l, :].rearrange("(h o) -> h o", o=1), writes=[gb_])
                fw.dma("sp", ds, gk[half * 64:(half + 1) * 64, :], self.p["fox_kg"][l, :].rearrange("(h o) -> h o", o=1), writes=[gb_])
            fw.op("dve", lambda e: e.tensor_scalar(out=gq[:, :], in0=gq[:, :], scalar1=0.125, scalar2=None, op0=ALU.mult), reads=[gb_], writes=[gb_])
            raw = [self.sb(st, "fraw%d" % i, [128, 512], F32) for i in range(2)]
            rawb = [Buf() for _ in range(2)]
            rds = [fw.dsem() for _ in range(2)]
            sq = self.sb(st, "fsq", [128, 512], F32)
            sqb = Buf()
            rs = self.sb(st, "frs", [128, 512], F32)
            rsb = Buf()
            PT = [self.sb(st, "fPT%d" % i, [128, 512], BF16) for i in range(3)]
            PTb = [Buf() for _ in range(3)]
            bias = [self.sb(st, "fbias%d" % i, [128, NTI], F32) for i in range(2)]
            biasb = [Buf() for _ in range(2)]
            rden = self.sb(st, "frden", [64, 512], F32)
            rdenb = Buf()
            ost = [self.sb(st, "fost%d" % i, [64, 512], BF16) for i in range(2)]
            ostb = [Buf() for _ in range(2)]
            ods = [fw.dsem() for _ in range(2)]
            ads = fw.dsem()
            kraw = 0
            kpt = 0
            kb = 0
            ko = 0
            for j in range(4):
                for which in range(2):
                    row0 = (C_FQ if which == 0 else C_FK) + j * 128
                    gcol = gq if which == 0 else gk
                    dst = qpad if which == 0 else kpad
                    dstb = qpb if which == 0 else kpb
                    for G in range(NG):
                        s = kraw % 2
                        kraw += 1
                        fw.dma("sp", rds[s], raw[s][:, :], self.zT[row0:row0 + 128, G * 512:(G + 1) * 512], reads=[self.b_zT], writes=[rawb[s]])
                        fw.op("act", lambda e, s=s: e.activation(out=sq[:, :], in_=raw[s][:, :], func=AF.Square), reads=[rawb[s]], writes=[sqb])
                        ps, psb = self.pst()
                        fw.op("pe", lambda e, ps=ps: e.matmul(ps[:, :], lhsT=self.bo64[:, :], rhs=sq[:, :], start=True, stop=True),
                              reads=[sqb, self.b_const], writes=[psb])
                        fw.op("act", lambda e, ps=ps: e.activation(out=rs[:, :], in_=ps[:, :], func=AF.Sqrt, scale=1.0 / 64, bias=self.eps_col[:, 0:1]),
                              reads=[psb, self.b_const], writes=[rsb])
                        fw.op("dve", lambda e: e.reciprocal(out=rs[:, :], in_=rs[:, :]), reads=[rsb], writes=[rsb])
                        for e_ in range(2):
                            pr = slice(e_ * 64, (e_ + 1) * 64)
                            fw.op("dve", lambda e, s=s, G=G, e_=e_, pr=pr, dst=dst, gcol=gcol: e.scalar_tensor_tensor(
                                out=dst[e_][pr, G * 512:(G + 1) * 512], in0=raw[s][pr, :], scalar=gcol[pr, 0:1], in1=rs[pr, :],
                                op0=ALU.mult, op1=ALU.mult), reads=[rawb[s], rsb, gb_], writes=[dstb[e_]])
                fw.dma("sp", ads, qpad[0][64:65, :], self.rrow[2 * j:2 * j + 1, :], reads=[self.b_rrow], writes=[qpb[0]])
                fw.dma("sp", ads, qpad[1][0:1, :], self.rrow[2 * j + 1:2 * j + 2, :], reads=[self.b_rrow], writes=[qpb[1]])
                for e_ in range(2):
                    h = 2 * j + e_
                    for G in range(NG):
                        nt = 4 * G + 4
                        bs = kb % 2
                        kb += 1
                        fw.op("dve", lambda e, bs=bs, nt=nt, h=h, G=G: e.tensor_scalar(
                            out=bias[bs][:, 0:nt], in0=cpT[:, 0:nt, h], scalar1=refT[:, G, h:h + 1], scalar2=None, op0=ALU.subtract),
                            reads=[cpTb, refTb], writes=[biasb[bs]])
                        acc = 4 + 2 * (G % 2)
                        po, pob = self.ps[acc], self.psb[acc]
                        pd, pdb = self.ps[acc + 1], self.psb[acc + 1]
                        for ti in range(nt):
                            i = ti - 4 * G
                            c0 = max(i, 0) * 128
                            ps, psb = self.pst()
                            fw.op("pe", lambda e, ps=ps, ti=ti, c0=c0, e_=e_, G=G, i=i: e.matmul(
                                ps[:, c0:512], lhsT=kpad[e_][:, ti * 128:(ti + 1) * 128], rhs=qpad[e_][:, G * 512 + c0:(G + 1) * 512],
                                start=True, stop=(i < 0)), reads=[kpb[e_], qpb[e_]], writes=[psb])
                            if i >= 0:
                                fw.op("pe", lambda e, ps=ps, c0=c0: e.matmul(ps[:, c0:c0 + 128], lhsT=self.ident_b[:, :], rhs=self.maskneg[:, :],
                                                                             start=False, stop=True), reads=[self.b_const], writes=[psb])
                            p_ = kpt % 3
                            kpt += 1
                            fw.op("act", lambda e, ps=ps, p_=p_, c0=c0, bs=bs, ti=ti: e.activation(
                                out=PT[p_][:, c0:512], in_=ps[:, c0:512], func=AF.Exp, bias=bias[bs][:, ti:ti + 1]),
                                reads=[psb, biasb[bs]], writes=[PTb[p_]])
                            fw.op("pe", lambda e, po=po, p_=p_, c0=c0, ti=ti, h=h, nt=nt: e.matmul(
                                po[0:64, c0:512], lhsT=V[:, ti, h * 64:(h + 1) * 64], rhs=PT[p_][:, c0:512],
                                start=(ti == 0), stop=(ti == nt - 1)), reads=[Vb, PTb[p_]], writes=[pob])
                            fw.op("pe", lambda e, pd=pd, p_=p_, c0=c0, ti=ti, nt=nt: e.matmul(
                                pd[0:64, c0:512], lhsT=self.ones_b[:, 0:64], rhs=PT[p_][:, c0:512],
                                start=(ti == 0), stop=(ti == nt - 1)), reads=[self.b_const, PTb[p_]], writes=[pdb])
                        fw.op("dve", lambda e, pd=pd: e.reciprocal(out=rden[:, :], in_=pd[0:64, :]), reads=[pdb], writes=[rdenb])
                        o_ = ko % 2
                        ko += 1
                        fw.op("dve", lambda e, po=po, o_=o_: e.tensor_tensor(out=ost[o_][:, :], in0=po[0:64, :], in1=rden[:, :], op=ALU.mult),
                              reads=[pob, rdenb], writes=[ostb[o_]])
                        fw.dma("pool", ods[o_], self.oT[h * 64:(h + 1) * 64, G * 512:(G + 1) * 512], ost[o_][:, :],
                               reads=[ostb[o_]], writes=[self.b_oT])
        fw.barrier()

    def _cumsum(self, cp, sp, o, n, init, b_sp, b_cp):
        self.fw.op("dve", lambda e: e.tensor_tensor_scan(out=cp[:, o:o + n], data0=self.ones_big[0:8, 0:n], data1=sp[:, o:o + n],
                                                         initial=init, op0=ALU.mult, op1=ALU.add),
                   reads=[b_sp, b_cp, self.b_const], writes=[b_cp])

    def conv(self, l):
        nc, fw = self.nc, self.fw
        NG = T // 512
        with contextlib.ExitStack() as st:
            uT = self.sb(st, "cuT", [128, 4, 32 + T], BF16)
            uTb = Buf()
            diag = self.sb(st, "cdiag", [128, 31, 4, 128], BF16)
            diagb = Buf()
            wT = self.sb(st, "cwT", [128, 4, 32], F32)
            wTb = Buf()
            cols = self.sb(st, "ccols", [128, 3, 4], F32)
            colsb = Buf()
            with contextlib.ExitStack() as st1:
                ds = fw.dsem()
                for cc in range(4):
                    fw.dma("sp", ds, wT[:, cc, 0:31], self.p["conv_dw"][l, :, cc * 128:(cc + 1) * 128].rearrange("k p -> p k"), writes=[wTb],
                           allow_slow_non_contiguous=True)
                for i, nm in enumerate(("conv_db", "conv_ln_g", "conv_ln_b")):
                    fw.dma("sp", ds, cols[:, i, :], self.p[nm][l, :].rearrange("(k p) -> p k", p=128), writes=[colsb],
                           allow_slow_non_contiguous=True)
                for k in range(31):
                    for cc in range(4):
                        eng = "dve"
                        fw.op(eng, lambda e, k=k, cc=cc: e.tensor_scalar(out=diag[:, k, cc, :], in0=self.ident_b[:, :], scalar1=wT[:, cc, k:k + 1],
                                                                          scalar2=None, op0=ALU.mult), reads=[wTb, self.b_const], writes=[diagb])
                fw.op("pool", lambda e: e.memset(uT[:, :, 0:32], 0.0), writes=[uTb])
                a_t = [self.sb(st1, "ca%d" % i, [128, 2048], F32) for i in range(2)]
                g_t = [self.sb(st1, "cg%d" % i, [128, 2048], F32) for i in range(2)]
                atb = [Buf() for _ in range(2)]
                lds = [fw.dsem() for _ in range(2)]
                n = 0
                for cc in range(4):
                    for tb in range(T // 2048):
                        s = n % 2
                        n += 1
                        fw.dma("sp", lds[s], a_t[s][:, :], self.zT[C_CA + cc * 128:C_CA + (cc + 1) * 128, tb * 2048:(tb + 1) * 2048],
                               reads=[self.b_zT], writes=[atb[s]])
                        fw.dma("sp", lds[s], g_t[s][:, :], self.zT[C_CG + cc * 128:C_CG + (cc + 1) * 128, tb * 2048:(tb + 1) * 2048],
                               reads=[self.b_zT], writes=[atb[s]])
                        eng = "dve" if n % 2 else "pool"
                        fw.op(eng, lambda e, s=s, cc=cc, tb=tb: e.tensor_tensor(out=uT[:, cc, 32 + tb * 2048:32 + (tb + 1) * 2048], in0=a_t[s][:, :],
                                                                                in1=g_t[s][:, :], op=ALU.mult), reads=[atb[s]], writes=[uTb])
            self.dump("wT", wT[:, :, :], [128, 4, 32], F32, wTb)
            self.dump("diag", diag[:, 3, 1, :], [128, 128], BF16, diagb)
            self.dump("uT", uT[:, 0, 0:600], [128, 600], BF16, uTb)
            fw.barrier()
            y = [self.sb(st, "cy%d" % i, [128, 4, 512], F32) for i in range(2)]
            yb = [Buf() for _ in range(2)]
            sq = [self.sb(st, "csq%d" % i, [128, 512], F32) for i in range(2)]
            sqb = [Buf() for _ in range(2)]
            rs = self.sb(st, "crs", [128, 512], F32)
            rsb = Buf()
            ob = [self.sb(st, "cob%d" % i, [128, 512], BF16) for i in range(4)]
            obb = [Buf() for _ in range(4)]
            ods = [fw.dsem() for _ in range(4)]
            ko = 0
            for g in range(NG):
                s = g % 2
                for cc in range(4):
                    ps, psb = self.pst()
                    for k in range(31):
                        c0 = 2 + k + g * 512
                        fw.op("pe", lambda e, ps=ps, k=k, cc=cc, c0=c0: e.matmul(ps[:, :], lhsT=diag[:, k, cc, :], rhs=uT[:, cc, c0:c0 + 512],
                                                                                 start=(k == 0), stop=(k == 30)), reads=[diagb, uTb], writes=[psb])
                    fw.op("dve", lambda e, ps=ps, s=s, cc=cc: e.tensor_scalar(out=y[s][:, cc, :], in0=ps[:, :], scalar1=cols[:, 0, cc:cc + 1], scalar2=None, op0=ALU.add),
                          reads=[psb, colsb], writes=[yb[s]])
                if g == 0:
                    self.dump("ypre", y[s][:, :, :], [128, 4, 512], F32, yb[s])
                pm, pmb = self.pst()
                for cc in range(4):
                    fw.op("pe", lambda e, pm=pm, s=s, cc=cc: e.matmul(pm[:, :], lhsT=self.ones_f[:, :], rhs=y[s][:, cc, :], start=(cc == 0), stop=(cc == 3)),
                          reads=[yb[s], self.b_const], writes=[pmb])
                for cc in range(4):
                    fw.op("dve", lambda e, pm=pm, s=s, cc=cc: e.scalar_tensor_tensor(out=y[s][:, cc, :], in0=pm[:, :], scalar=-1.0 / 512, in1=y[s][:, cc, :],
                                                                                     op0=ALU.mult, op1=ALU.add), reads=[pmb, yb[s]], writes=[yb[s]])
                if g == 0:
                    self.dump("ymid", y[s][:, :, :], [128, 4, 512], F32, yb[s])
                pv, pvb = self.pst()
                for cc in range(4):
                    q = cc % 2
                    fw.op("act", lambda e, s=s, cc=cc, q=q: e.activation(out=sq[q][:, :], in_=y[s][:, cc, :], func=AF.Square), reads=[yb[s]], writes=[sqb[q]])
                    fw.op("pe", lambda e, pv=pv, q=q, cc=cc: e.matmul(pv[:, :], lhsT=self.ones_f[:, :], rhs=sq[q][:, :], start=(cc == 0), stop=(cc == 3)),
                          reads=[sqb[q], self.b_const], writes=[pvb])
                fw.op("act", lambda e, pv=pv: e.activation(out=rs[:, :], in_=pv[:, :], func=AF.Sqrt, scale=1.0 / 512, bias=self.eps_col[:, 0:1]),
                      reads=[pvb, self.b_const], writes=[rsb])
                fw.op("dve", lambda e: e.reciprocal(out=rs[:, :], in_=rs[:, :]), reads=[rsb], writes=[rsb])
                if g == 0:
                    self.dump("rs", rs[:, :], [128, 512], F32, rsb)
                for cc in range(4):
                    fw.op("dve", lambda e, s=s, cc=cc: e.tensor_tensor(out=y[s][:, cc, :], in0=y[s][:, cc, :], in1=rs[:, :], op=ALU.mult),
                          reads=[yb[s], rsb], writes=[yb[s]])
                    fw.op("dve", lambda e, s=s, cc=cc: e.tensor_scalar(out=y[s][:, cc, :], in0=y[s][:, cc, :], scalar1=cols[:, 1, cc:cc + 1],
                                                                        scalar2=cols[:, 2, cc:cc + 1], op0=ALU.mult, op1=ALU.add),
                          reads=[yb[s], colsb], writes=[yb[s]])
                    o = ko % 4
                    ko += 1
                    fw.op("act", lambda e, s=s, cc=cc, o=o: e.activation(out=ob[o][:, :], in_=y[s][:, cc, :], func=AF.Silu), reads=[yb[s]], writes=[obb[o]])
                    fw.dma("pool", ods[o], self.oT[512 + cc * 128:512 + (cc + 1) * 128, g * 512:(g + 1) * 512], ob[o][:, :],
                           reads=[obb[o]], writes=[self.b_oT])
        fw.barrier()

    def pool(self, l):
        nc, fw = self.nc, self.fw
        BK = 2048
        with contextlib.ExitStack() as st:
            pw32 = self.sb(st, "ppw32", [128, 4, 128], F32)
            pw = self.sb(st, "ppw", [128, 4, 128], BF16)
            pwb = Buf()
            scol = self.sb(st, "pscol", [128, 4], F32)
            inv = self.sb(st, "pinv", [128, 4, 16], F32)
            smb = Buf()
            ds = fw.dsem()
            fw.dma("sp", ds, pw32[:, :, :], self.p["pool_w"][l, :, :, :].rearrange("g c j -> c g j"), writes=[pwb])
            fw.dma("sp", ds, scol[:, :], self.p["pool_scale"][l, :].rearrange("(k p) -> p k", p=128), writes=[smb], allow_slow_non_contiguous=True)
            fw.dma("sp", ds, inv[:, :, :].rearrange("p a b -> p (a b)"), self.p["invcnt"][:, :].rearrange("a b -> (a b)").partition_broadcast(128), writes=[smb])
            fw.op("dve", lambda e: e.tensor_copy(out=pw[:, :, :], in_=pw32[:, :, :]), reads=[pwb], writes=[pwb])
            bufA = [self.sb(st, "pA%d" % i, [128, 16 + BK], F32) for i in range(2)]
            bufB = [self.sb(st, "pB%d" % i, [128, 16 + BK], F32) for i in range(2)]
            bufC = [self.sb(st, "pC%d" % i, [128, 16 + BK], F32) for i in range(2)]
            Ab = [Buf() for _ in range(2)]
            Bb = [Buf() for _ in range(2)]
            Cb = [Buf() for _ in range(2)]
            lds = [fw.dsem() for _ in range(2)]
            mx = [self.sb(st, "pmx%d" % i, [128, BK], BF16) for i in range(2)]
            mxb = [Buf() for _ in range(2)]
            ob = [self.sb(st, "pob%d" % i, [128, 512], BF16) for i in range(4)]
            obb = [Buf() for _ in range(4)]
            ods = [fw.dsem() for _ in range(4)]
            n = 0
            ko = 0
            for gi in range(4):
                w = 2 << gi
                row0 = C_PZ + gi * 128
                for bk in range(T // BK):
                    s = n % 2
                    n += 1
                    A, B, C = bufA[s], bufB[s], bufC[s]
                    fw.dma("sp", lds[s], A[:, 16:16 + BK], self.zT[row0:row0 + 128, bk * BK:(bk + 1) * BK], reads=[self.b_zT], writes=[Ab[s]])
                    if bk == 0:
                        fw.op("pool", lambda e, A=A: e.memset(A[:, 0:16], 0.0), writes=[Ab[s]])
                    else:
                        fw.dma("sp", lds[s], A[:, 0:16], self.zT[row0:row0 + 128, bk * BK - 16:bk * BK], reads=[self.b_zT], writes=[Ab[s]])
                    src, srcb = A, Ab[s]
                    dsts = [(B, Bb[s]), (C, Cb[s])]
                    d = 1
                    k = 0
                    while d < w:
                        dst, dstb = dsts[k % 2]
                        k += 1
                        fw.op("dve", lambda e, dst=dst, src=src, d=d: e.tensor_tensor(out=dst[:, d:16 + BK], in0=src[:, d:16 + BK], in1=src[:, 0:16 + BK - d], op=ALU.add),
                              reads=[srcb], writes=[dstb])
                        src, srcb = dst, dstb
                        d *= 2
                    fw.op("dve", lambda e, src=src, A=A, s=s, w=w: e.scalar_tensor_tensor(out=mx[s][:, :], in0=src[:, 16:16 + BK], scalar=1.0 / w, in1=A[:, 16:16 + BK],
                                                                                          op0=ALU.mult, op1=ALU.subtract), reads=[srcb, Ab[s]], writes=[mxb[s]])
                    if bk == 0:
                        fw.op("dve", lambda e, src=src, gi=gi: e.tensor_tensor(out=src[:, 16:32], in0=src[:, 16:32], in1=inv[:, gi, :], op=ALU.mult),
                              reads=[srcb, smb], writes=[srcb])
                        fw.op("dve", lambda e, src=src, A=A, s=s: e.tensor_tensor(out=mx[s][:, 0:16], in0=src[:, 16:32], in1=A[:, 16:32], op=ALU.subtract),
                              reads=[srcb, Ab[s]], writes=[mxb[s]])
                    for tg in range(BK // 512):
                        ps, psb = self.pst()
                        fw.op("pe", lambda e, ps=ps, gi=gi, s=s, tg=tg: e.matmul(ps[:, :], lhsT=pw[:, gi, :], rhs=mx[s][:, tg * 512:(tg + 1) * 512], start=True, stop=True),
                              reads=[pwb, mxb[s]], writes=[psb])
                        o = ko % 4
                        ko += 1
                        fw.op("act", lambda e, ps=ps, o=o, gi=gi: e.activation(out=ob[o][:, :], in_=ps[:, :], func=AF.Copy, scale=scol[:, gi:gi + 1]),
                              reads=[psb, smb], writes=[obb[o]])
                        t0 = bk * BK + tg * 512
                        fw.dma("pool", ods[o], self.oT[1536 + gi * 128:1536 + (gi + 1) * 128, t0:t0 + 512], ob[o][:, :], reads=[obb[o]], writes=[self.b_oT])
        fw.barrier()

    def gla(self, l):
        nc, fw = self.nc, self.fw
        BK = 2048
        NTB = BK // 128
        with contextlib.ExitStack() as st:
            wa = self.sb(st, "gwa", [16, 256], F32)
            nba = self.sb(st, "gnba", [128, 2], F32)
            og = self.sb(st, "gog", [128, 4], F32)
            smb = Buf()
            ds = fw.dsem()
            fw.dma("sp", ds, wa[:, :], self.p["gla_wa"][l, :, :], writes=[smb])
            fw.dma("sp", ds, nba[:, :], self.p["gla_ba"][l, :].rearrange("(k p) -> p k", p=128), writes=[smb], allow_slow_non_contiguous=True)
            fw.dma("sp", ds, og[:, :], self.p["gla_og"][l, :].rearrange("(k p) -> p k", p=128), writes=[smb], allow_slow_non_contiguous=True)
            fw.op("dve", lambda e: e.tensor_scalar(out=nba[:, :], in0=nba[:, :], scalar1=-1.0, scalar2=None, op0=ALU.mult), reads=[smb], writes=[smb])
            alow = self.sb(st, "galow", [16, BK], F32)
            alowb = Buf()
            la = self.sb(st, "gla", [128, BK], F32)
            bp = self.sb(st, "gbp", [128, BK], F32)
            et = self.sb(st, "get", [128, BK], F32)
            qr = self.sb(st, "gqr", [128, BK], F32)
            kr = self.sb(st, "gkr", [128, BK], F32)
            lab, bpb, etb, qrb, krb = Buf(), Buf(), Buf(), Buf(), Buf()
            nbl = self.sb(st, "gnbl", [128, NTB], F32)
            dec = self.sb(st, "gdec", [128, NTB], F32)
            nblb = Buf()
            qt = self.sb(st, "gqt", [128, BK], BF16)
            qtb = Buf()
            kt = [self.sb(st, "gkt%d" % e, [128, BK], BF16) for e in range(2)]
            ktb = [Buf() for _ in range(2)]
            kd = self.sb(st, "gkd", [128, BK], BF16)
            kdb = Buf()
            kdt = self.sb(st, "gkdt", [128, NTB, 128], BF16)
            kdtb = Buf()
            Vt = self.sb(st, "gV", [128, NTB, 256], BF16)
            Vtb = Buf()
            KV = self.sb(st, "gKV", [128, NTB, 128], F32)
            KVb = Buf()
            S = self.sb(st, "gS", [128, 128], F32)
            Sb = Buf()
            Sbf = [self.sb(st, "gSbf%d" % e, [128, NTB, 128], BF16) for e in range(2)]
            Sbfb = [Buf() for _ in range(2)]
            Asb = [self.sb(st, "gA%d" % i, [128, 128], BF16) for i in range(3)]
            Asbb = [Buf() for _ in range(3)]
            orw = [self.sb(st, "gor%d" % i, [128, 512], F32) for i in range(2)]
            orwb = [Buf() for _ in range(2)]
            sq = self.sb(st, "gsq", [128, 512], F32)
            sqb = Buf()
            rs = self.sb(st, "grs", [128, 512], F32)
            rsb = Buf()
            rg = [self.sb(st, "grg%d" % i, [128, 512], F32) for i in range(2)]
            rgb = [Buf() for _ in range(2)]
            rds = [fw.dsem() for _ in range(2)]
            ob = [self.sb(st, "gob%d" % i, [128, 512], BF16) for i in range(2)]
            obb = [Buf() for _ in range(2)]
            ods = [fw.dsem() for _ in range(2)]
            lds = fw.dsem()
            for e_ in range(2):
                fw.op("pool", lambda e, e_=e_: e.memset(kt[e_][:, :], 0.0), writes=[ktb[e_]])
                fw.op("pool", lambda e, e_=e_: e.memset(Sbf[e_][:, :, :], 0.0), writes=[Sbfb[e_]])
            ka = 0
            ko = 0
            krg = 0
            for j in range(2):
                fw.op("dve", lambda e: e.memset(S[:, :], 0.0), writes=[Sb])
                for bk in range(T // BK):
                    t0 = bk * BK
                    fw.dma("sp", lds, alow[:, :], self.zT[C_GA:C_GA + 16, t0:t0 + BK], reads=[self.b_zT], writes=[alowb])
                    fw.dma("sp", lds, qr[:, :], self.zT[C_GQ + j * 128:C_GQ + (j + 1) * 128, t0:t0 + BK], reads=[self.b_zT], writes=[qrb])
                    fw.dma("sp", lds, kr[:, :], self.zT[C_GK + j * 128:C_GK + (j + 1) * 128, t0:t0 + BK], reads=[self.b_zT], writes=[krb])
                    fw.dma("sp", lds, Vt[:, :, :], self.vtok[t0:t0 + BK, 512 + j * 256:512 + (j + 1) * 256].rearrange("(n p) c -> p n c", p=128),
                           reads=[self.b_vtok], writes=[Vtb])
                    for tg in range(BK // 512):
                        ps, psb = self.pst()
                        fw.op("pe", lambda e, ps=ps, tg=tg, j=j: e.matmul(ps[:, :], lhsT=wa[:, j * 128:(j + 1) * 128], rhs=alow[:, tg * 512:(tg + 1) * 512],
                                                                         start=True, stop=True), reads=[smb, alowb], writes=[psb])
                        fw.op("act", lambda e, ps=ps, tg=tg, j=j: e.activation(out=la[:, tg * 512:(tg + 1) * 512], in_=ps[:, :], func=AF.Exp, scale=-1.0,
                                                                              bias=nba[:, j:j + 1]), reads=[psb, smb], writes=[lab])
                    fw.op("act", lambda e: e.activation(out=la[:, :], in_=la[:, :], func=AF.Ln, bias=self.ones_f[:, 0:1]), reads=[lab, self.b_const], writes=[lab])
                    for n in range(NTB):
                        fw.op("dve", lambda e, n=n: e.tensor_tensor_scan(out=bp[:, n * 128:(n + 1) * 128], data0=self.ones_f[:, :], data1=la[:, n * 128:(n + 1) * 128],
                                                                         initial=0.0, op0=ALU.mult, op1=ALU.add), reads=[lab, self.b_const], writes=[bpb])
                    fw.op("dve", lambda e: e.tensor_scalar(out=nbl[:, :], in0=bp[:, :].rearrange("p (n t) -> p n t", t=128)[:, :, 127], scalar1=-1.0 / 16, scalar2=None,
                                                           op0=ALU.mult), reads=[bpb], writes=[nblb])
                    fw.op("act", lambda e: e.activation(out=dec[:, :], in_=nbl[:, :], func=AF.Exp), reads=[nblb], writes=[nblb])
                    fw.op("act", lambda e: e.activation(out=et[:, :], in_=bp[:, :], func=AF.Exp, scale=-1.0 / 16), reads=[bpb], writes=[etb])
                    fw.op("dve", lambda e: e.scalar_tensor_tensor(out=qt[:, :], in0=qr[:, :], scalar=0.125, in1=et[:, :], op0=ALU.mult, op1=ALU.mult),
                          reads=[qrb, etb], writes=[qtb])
                    fw.op("act", lambda e: e.activation(out=et[:, :], in_=bp[:, :], func=AF.Exp, scale=1.0 / 16), reads=[bpb, qtb], writes=[etb])
                    for e_ in range(2):
                        pr = slice(e_ * 64, (e_ + 1) * 64)
                        fw.op("dve", lambda e, e_=e_, pr=pr: e.tensor_tensor(out=kt[e_][pr, :], in0=kr[pr, :], in1=et[pr, :], op=ALU.mult),
                              reads=[krb, etb], writes=[ktb[e_]])
                    for n in range(NTB):
                        fw.op("act", lambda e, n=n: e.activation(out=et[:, n * 128:(n + 1) * 128], in_=bp[:, n * 128:(n + 1) * 128], func=AF.Exp, scale=1.0 / 16,
                                                                 bias=nbl[:, n:n + 1]), reads=[bpb, nblb, ktb[0], ktb[1]], writes=[etb])
                    fw.op("dve", lambda e: e.tensor_tensor(out=kd[:, :], in0=kr[:, :], in1=et[:, :], op=ALU.mult), reads=[krb, etb], writes=[kdb])
                    for n4 in range(NTB // 4):
                        ps, psb = self.pst()
                        psv = ps[:, 0:256].bitcast(BF16)
                        for i in range(4):
                            n = n4 * 4 + i
                            fw.op("pe", lambda e, psv=psv, i=i, n=n: e.transpose(out=psv[:, i * 128:(i + 1) * 128], in_=kd[:, n * 128:(n + 1) * 128],
                                                                                 identity=self.ident_b[:, :]), reads=[kdb, self.b_const], writes=[psb])
                        fw.op("dve", lambda e, psv=psv, n4=n4: e.tensor_copy(out=kdt[:, n4 * 4:(n4 + 1) * 4, :].rearrange("p a b -> p (a b)"), in_=psv[:, :]),
                              reads=[psb], writes=[kdtb])
                    for n in range(NTB):
                        ps, psb = self.pst()
                        for e_ in range(2):
                            fw.op("pe", lambda e, ps=ps, n=n, e_=e_: e.matmul(ps[:, e_ * 128:(e_ + 1) * 128], lhsT=kdt[:, n, :], rhs=Vt[:, n, e_ * 128:(e_ + 1) * 128],
                                                                             start=True, stop=True), reads=[kdtb, Vtb], writes=[psb])
                        for e_ in range(2):
                            pr = slice(e_ * 64, (e_ + 1) * 64)
                            fw.op("act", lambda e, ps=ps, n=n, e_=e_, pr=pr: e.copy(out=KV[pr, n, :], in_=ps[pr, e_ * 128:(e_ + 1) * 128]),
                                  reads=[psb], writes=[KVb])
                    for n in range(NTB):
                        for e_ in range(2):
                            pr = slice(e_ * 64, (e_ + 1) * 64)
                            fw.op("pool", lambda e, n=n, e_=e_, pr=pr: e.tensor_copy(out=Sbf[e_][pr, n, :], in_=S[pr, :]), reads=[Sb], writes=[Sbfb[e_]])
                        fw.op("dve", lambda e, n=n: e.scalar_tensor_tensor(out=S[:, :], in0=S[:, :], scalar=dec[:, n:n + 1], in1=KV[:, n, :],
                                                                           op0=ALU.mult, op1=ALU.add), reads=[Sb, nblb, KVb, Sbfb[0], Sbfb[1]], writes=[Sb])
                    for e_ in range(2):
                        h = 2 * j + e_
                        for tg in range(BK // 512):
                            po, pob = self.ps[4 + (ko % 2)], self.psb[4 + (ko % 2)]
                            for i in range(4):
                                n = tg * 4 + i
                                tsl = slice(n * 128, (n + 1) * 128)
                                ps, psb = self.pst()
                                fw.op("pe", lambda e, ps=ps, e_=e_, tsl=tsl: e.matmul(ps[:, 0:128], lhsT=kt[e_][:, tsl], rhs=qt[:, tsl], start=True, stop=True),
                                      reads=[ktb[e_], qtb], writes=[psb])
                                a_ = ka % 3
                                ka += 1
                                fw.op("dve", lambda e, ps=ps, a_=a_: e.tensor_tensor(out=Asb[a_][:, :], in0=ps[:, 0:128], in1=self.U_b[:, :], op=ALU.mult),
                                      reads=[psb, self.b_const], writes=[Asbb[a_]])
                                fw.op("pe", lambda e, po=po, i=i, n=n, e_=e_, a_=a_: e.matmul(po[:, i * 128:(i + 1) * 128], lhsT=Vt[:, n, e_ * 128:(e_ + 1) * 128], rhs=Asb[a_][:, :],
                                                                                             start=True, stop=False), reads=[Vtb, Asbb[a_]], writes=[pob])
                                fw.op("pe", lambda e, po=po, i=i, n=n, e_=e_, tsl=tsl: e.matmul(po[:, i * 128:(i + 1) * 128], lhsT=Sbf[e_][:, n, :], rhs=qt[:, tsl],
                                                                                                start=False, stop=True), reads=[Sbfb[e_], qtb], writes=[pob])
                            o_ = ko % 2
                            ko += 1
                            fw.op("act", lambda e, po=po, o_=o_: e.copy(out=orw[o_][:, :], in_=po[:, :]), reads=[pob], writes=[orwb[o_]])
                            fw.op("act", lambda e, o_=o_: e.activation(out=sq[:, :], in_=orw[o_][:, :], func=AF.Square), reads=[orwb[o_]], writes=[sqb])
                            pn, pnb = self.pst()
                            fw.op("pe", lambda e, pn=pn: e.matmul(pn[:, :], lhsT=self.ones_f[:, :], rhs=sq[:, :], start=True, stop=True), reads=[sqb, self.b_const], writes=[pnb])
                            fw.op("act", lambda e, pn=pn: e.activation(out=rs[:, :], in_=pn[:, :], func=AF.Sqrt, scale=1.0 / 128, bias=self.eps_col[:, 0:1]),
                                  reads=[pnb, self.b_const], writes=[rsb])
                            fw.op("dve", lambda e: e.reciprocal(out=rs[:, :], in_=rs[:, :]), reads=[rsb], writes=[rsb])
                            g_ = krg % 2
                            krg += 1
                            tk = t0 + tg * 512
                            fw.dma("sp", rds[g_], rg[g_][:, :], self.zT[C_GR + h * 128:C_GR + (h + 1) * 128, tk:tk + 512], reads=[self.b_zT], writes=[rgb[g_]])
                            fw.op("dve", lambda e, o_=o_, h=h: e.scalar_tensor_tensor(out=orw[o_][:, :], in0=orw[o_][:, :], scalar=og[:, h:h + 1], in1=rs[:, :],
                                                                                      op0=ALU.mult, op1=ALU.mult), reads=[orwb[o_], smb, rsb], writes=[orwb[o_]])
                            fw.op("dve", lambda e, o_=o_, g_=g_: e.tensor_tensor(out=ob[o_][:, :], in0=orw[o_][:, :], in1=rg[g_][:, :], op=ALU.mult),
                                  reads=[orwb[o_], rgb[g_]], writes=[obb[o_]])
                            fw.dma("pool", ods[o_], self.oT[1024 + h * 128:1024 + (h + 1) * 128, tk:tk + 512], ob[o_][:, :], reads=[obb[o_]], writes=[self.b_oT])
        fw.barrier()

    def branch_out(self, l, sb):
        nc, fw = self.nc, self.fw
        NT = 1024
        t0 = sb * NT
        Wbr, Wo = self.W[l]["w_branch"], self.W[l]["w_out"]
        gview = self.gatesT.rearrange("(n d) t -> d n t", n=4)
        with contextlib.ExitStack() as st:
            oTb = self.sb(st, "boT", [128, 16, NT], BF16)
            oTbb = Buf()
            yT = self.sb(st, "byT", [128, 16, NT], BF16)
            yTb = Buf()
            wt = [self.sb(st, "bw%d" % i, [128, 16, 512], BF16) for i in range(2)]
            wtb = [Buf() for _ in range(2)]
            wds = [fw.dsem() for _ in range(2)]
            gt = [self.sb(st, "bg%d" % i, [128, 4, 512], BF16) for i in range(2)]
            gtb = [Buf() for _ in range(2)]
            gds = [fw.dsem() for _ in range(2)]
            m = [self.sb(st, "bm%d" % i, [128, 512], F32) for i in range(4)]
            mb = [Buf() for _ in range(4)]
            xc = [self.sb(st, "bx%d" % i, [128, 512], F32) for i in range(3)]
            xcb = [Buf() for _ in range(3)]
            xds = [fw.dsem() for _ in range(3)]
            ds = fw.dsem()
            fw.dma("sp", ds, oTb[:, :, :], self.oT[:, t0:t0 + NT].rearrange("(k p) t -> p k t", p=128), reads=[self.b_oT], writes=[oTbb])
            kw = 0
            kg = 0
            kp = 0
            for db in range(4):
                s = kw % 2
                kw += 1
                self.load_w(Wbr, self.Wb[l], db * 512, 512, wt[s], wtb[s], wds[s])
                for sc in range(4):
                    dch = db * 4 + sc
                    for tg in range(NT // 512):
                        tk = t0 + tg * 512
                        g_ = kg % 2
                        kg += 1
                        fw.dma("sp", gds[g_], gt[g_][:, :, :], gview[dch * 128:(dch + 1) * 128, :, tk:tk + 512], reads=[self.b_gates], writes=[gtb[g_]])
                        for n in range(4):
                            pi = kp % 8
                            kp += 1
                            ps, psb = self.ps[pi], self.psb[pi]
                            for cc in range(4):
                                fw.op("pe", lambda e, ps=ps, n=n, cc=cc, s=s, sc=sc, tg=tg: e.matmul(
                                    ps[:, :], lhsT=wt[s][:, n * 4 + cc, sc * 128:(sc + 1) * 128], rhs=oTb[:, n * 4 + cc, tg * 512:(tg + 1) * 512],
                                    start=(cc == 0), stop=(cc == 3)), reads=[wtb[s], oTbb], writes=[psb])
                            fw.op("dve", lambda e, ps=ps, n=n, g_=g_: e.tensor_tensor(out=m[n][:, :], in0=ps[:, :], in1=gt[g_][:, n, :], op=ALU.mult),
                                  reads=[psb, gtb[g_]], writes=[mb[n]])
                        fw.op("pool", lambda e: e.tensor_tensor(out=m[0][:, :], in0=m[0][:, :], in1=m[1][:, :], op=ALU.add), reads=[mb[0], mb[1]], writes=[mb[0]])
                        fw.op("pool", lambda e: e.tensor_tensor(out=m[2][:, :], in0=m[2][:, :], in1=m[3][:, :], op=ALU.add), reads=[mb[2], mb[3]], writes=[mb[2]])
                        fw.op("pool", lambda e, dch=dch, tg=tg: e.tensor_tensor(out=yT[:, dch, tg * 512:(tg + 1) * 512], in0=m[0][:, :], in1=m[2][:, :], op=ALU.add),
                              reads=[mb[0], mb[2]], writes=[yTb])
            kx = 0
            for db in range(4):
                s = kw % 2
                kw += 1
                self.load_w(Wo, self.Wb[l], db * 512, 512, wt[s], wtb[s], wds[s])
                for sc in range(4):
                    dch = db * 4 + sc
                    for tg in range(NT // 512):
                        tk = t0 + tg * 512
                        x_ = kx % 3
                        kx += 1
                        xb_ = Buf()
                        fw.dma("sp", xds[x_], xc[x_][:, :], self.xT[dch * 128:(dch + 1) * 128, tk:tk + 512], reads=[xb_], writes=[xcb[x_]])
                        pi = kp % 8
                        kp += 1
                        ps, psb = self.ps[pi], self.psb[pi]
                        for dc in range(16):
                            fw.op("pe", lambda e, ps=ps, dc=dc, s=s, sc=sc, tg=tg: e.matmul(
                                ps[:, :], lhsT=wt[s][:, dc, sc * 128:(sc + 1) * 128], rhs=yT[:, dc, tg * 512:(tg + 1) * 512],
                                start=(dc == 0), stop=(dc == 15)), reads=[wtb[s], yTb], writes=[psb])
                        fw.op("dve", lambda e, ps=ps, x_=x_: e.tensor_tensor(out=xc[x_][:, :], in0=ps[:, :], in1=xc[x_][:, :], op=ALU.add),
                              reads=[psb, xcb[x_]], writes=[xcb[x_]])
                        fw.dma("pool", xds[x_], self.xT[dch * 128:(dch + 1) * 128, tk:tk + 512], xc[x_][:, :], reads=[xcb[x_]], writes=[xb_])
        fw.barrier()

    def ffn(self, l):
        nc, fw = self.nc, self.fw
        NT = 1024
        Wup, Wdn = self.W[l]["ffn_up"], self.W[l]["ffn_down"]
        NCH = 2 * DFF // 128
        NP_ = NCH // 2
        with contextlib.ExitStack() as st:
            hT = self.sb(st, "fhT", [128, 16, 2 + NT], BF16)
            hTb = Buf()
            hh = self.sb(st, "fhh", [128, 16, 2], BF16)
            hhb = Buf()
            dwc = self.sb(st, "fdwc", [128, 3, NCH], F32)
            dbc = self.sb(st, "fdbc", [128, NCH], F32)
            cb = Buf()
            ds = fw.dsem()
            for k in range(3):
                fw.dma("sp", ds, dwc[:, k, :], self.p["ffn_dw"][l, k, :].rearrange("(c p) -> p c", p=128), writes=[cb], allow_slow_non_contiguous=True)
            fw.dma("sp", ds, dbc[:, :], self.p["ffn_db"][l, :].rearrange("(c p) -> p c", p=128), writes=[cb], allow_slow_non_contiguous=True)
            fw.op("pool", lambda e: e.memset(hh[:, :, :], 0.0), writes=[hhb])
            for sg in range(T // NT):
                t0 = sg * NT
                fw.op("pool", lambda e: e.tensor_copy(out=hT[:, :, 0:2], in_=hh[:, :, :]), reads=[hhb], writes=[hTb])
                self.rmsnorm_T("norm2_g", l, t0, NT, hT, hTb, 2)
                fw.op("pool", lambda e: e.tensor_copy(out=hh[:, :, :], in_=hT[:, :, NT:NT + 2]), reads=[hTb], writes=[hhb])
                with contextlib.ExitStack() as s2:
                    gT = self.sb(s2, "fgT", [128, NP_, 512], BF16)
                    gTb = Buf()
                    wu = [self.sb(s2, "fwu%d" % i, [128, 16, 512], BF16) for i in range(2)]
                    wub = [Buf() for _ in range(2)]
                    uds = [fw.dsem() for _ in range(2)]
                    wd = [self.sb(s2, "fwd%d" % i, [128, NP_, 256], BF16) for i in range(2)]
                    wdb = [Buf() for _ in range(2)]
                    dds = [fw.dsem() for _ in range(2)]
                    ub = [self.sb(s2, "fub%d" % i, [128, 2 + 512], F32) for i in range(4)]
                    ubb = [Buf() for _ in range(4)]
                    ac = [self.sb(s2, "fac%d" % i, [128, 512], F32) for i in range(4)]
                    acb = [Buf() for _ in range(4)]
                    xc = [self.sb(s2, "fx%d" % i, [128, 512], F32) for i in range(3)]
                    xcb = [Buf() for _ in range(3)]
                    xds = [fw.dsem() for _ in range(3)]
                    kb = 0
                    ku = 0
                    kd = 0
                    kx = 0
                    for g in range(NT // 512):
                        tk = t0 + g * 512
                        for cp in range(NP_):
                            if cp % 2 == 0:
                                s = kb % 2
                                kb += 1
                                fw.dma("sp", uds[s], wu[s][:, :, 0:256], Wup[:, cp * 128:cp * 128 + 256].rearrange("(k p) c -> p k c", p=128),
                                       reads=[self.Wb[l]], writes=[wub[s]])
                                fw.dma("sp", uds[s], wu[s][:, :, 256:512], Wup[:, DFF + cp * 128:DFF + cp * 128 + 256].rearrange("(k p) c -> p k c", p=128),
                                       reads=[self.Wb[l]], writes=[wub[s]])
                            res = []
                            for which in range(2):
                                ch = cp + which * NP_
                                wo = which * 256 + (cp % 2) * 128
                                ps, psb = self.pst()
                                for dc in range(16):
                                    fw.op("pe", lambda e, ps=ps, dc=dc, s=s, wo=wo, g=g: e.matmul(
                                        ps[:, :], lhsT=wu[s][:, dc, wo:wo + 128], rhs=hT[:, dc, 2 + g * 512:2 + (g + 1) * 512],
                                        start=(dc == 0), stop=(dc == 15)), reads=[wub[s], hTb], writes=[psb])
                                ph, phb = self.ps[4 + (ku % 4)], self.psb[4 + (ku % 4)]
                                for dc in range(16):
                                    fw.op("pe", lambda e, ph=ph, dc=dc, s=s, wo=wo, g=g: e.matmul(
                                        ph[:, 0:2], lhsT=wu[s][:, dc, wo:wo + 128], rhs=hT[:, dc, g * 512:g * 512 + 2],
                                        start=(dc == 0), stop=(dc == 15)), reads=[wub[s], hTb], writes=[phb])
                                u_ = ku % 4
                                ku += 1
                                fw.op("act", lambda e, ps=ps, u_=u_: e.copy(out=ub[u_][:, 2:514], in_=ps[:, :]), reads=[psb], writes=[ubb[u_]])
                                fw.op("act", lambda e, ph=ph, u_=u_: e.copy(out=ub[u_][:, 0:2], in_=ph[:, 0:2]), reads=[phb], writes=[ubb[u_]])
                                fw.op("dve", lambda e, u_=u_, ch=ch: e.tensor_scalar(out=ac[u_][:, :], in0=ub[u_][:, 2:514], scalar1=dwc[:, 2, ch:ch + 1],
                                                                                    scalar2=dbc[:, ch:ch + 1], op0=ALU.mult, op1=ALU.add),
                                      reads=[ubb[u_], cb], writes=[acb[u_]])
                                fw.op("dve", lambda e, u_=u_, ch=ch: e.scalar_tensor_tensor(out=ac[u_][:, :], in0=ub[u_][:, 1:513], scalar=dwc[:, 1, ch:ch + 1],
                                                                                           in1=ac[u_][:, :], op0=ALU.mult, op1=ALU.add),
                                      reads=[ubb[u_], cb, acb[u_]], writes=[acb[u_]])
                                fw.op("dve", lambda e, u_=u_, ch=ch: e.scalar_tensor_tensor(out=ac[u_][:, :], in0=ub[u_][:, 0:512], scalar=dwc[:, 0, ch:ch + 1],
                                                                                           in1=ac[u_][:, :], op0=ALU.mult, op1=ALU.add),
                                      reads=[ubb[u_], cb, acb[u_]], writes=[acb[u_]])
                                res.append(u_)
                            ua, uv = res
                            fw.op("act", lambda e, ua=ua: e.activation(out=ac[ua][:, :], in_=ac[ua][:, :], func=AF.Silu), reads=[acb[ua]], writes=[acb[ua]])
                            fw.op("pool", lambda e, ua=ua, uv=uv, cp=cp: e.tensor_tensor(out=gT[:, cp, :], in0=ac[ua][:, :], in1=ac[uv][:, :], op=ALU.mult),
                                  reads=[acb[ua], acb[uv]], writes=[gTb])
                        for db8 in range(D // 256):
                            s = kd % 2
                            kd += 1
                            fw.dma("sp", dds[s], wd[s][:, :, :], Wdn[:, db8 * 256:(db8 + 1) * 256].rearrange("(k p) c -> p k c", p=128),
                                   reads=[self.Wb[l]], writes=[wdb[s]])
                            for sc in range(2):
                                dch = db8 * 2 + sc
                                x_ = kx % 3
                                kx += 1
                                xb_ = Buf()
                                fw.dma("sp", xds[x_], xc[x_][:, :], self.xT[dch * 128:(dch + 1) * 128, tk:tk + 512], reads=[xb_], writes=[xcb[x_]])
                                ps, psb = self.pst()
                                for fc in range(NP_):
                                    fw.op("pe", lambda e, ps=ps, fc=fc, s=s, sc=sc: e.matmul(
                                        ps[:, :], lhsT=wd[s][:, fc, sc * 128:(sc + 1) * 128], rhs=gT[:, fc, :],
                                        start=(fc == 0), stop=(fc == NP_ - 1)), reads=[wdb[s], gTb], writes=[psb])
                                fw.op("dve", lambda e, ps=ps, x_=x_: e.tensor_tensor(out=xc[x_][:, :], in0=ps[:, :], in1=xc[x_][:, :], op=ALU.add),
                                      reads=[psb, xcb[x_]], writes=[xcb[x_]])
                                fw.dma("pool", xds[x_], self.xT[dch * 128:(dch + 1) * 128, tk:tk + 512], xc[x_][:, :], reads=[xcb[x_]], writes=[xb_])
                fw.barrier()
        fw.barrier()


def _build_inputs(inputs, nl, ncores):
    ins = []
    x = np.asarray(inputs["x"], dtype=np.float32)
    inv = np.zeros((4, 16), np.float32)
    for gi, w in enumerate((2, 4, 8, 16)):
        inv[gi] = 1.0 / np.minimum(np.arange(16) + 1, w)
    shared = {}
    for name in ("w_in", "w_out", "ffn_up", "ffn_down", "norm1_g", "fox_fb", "fox_qg", "fox_kg", "conv_dw", "conv_db",
                 "conv_ln_g", "conv_ln_b", "gla_wa", "gla_ba", "gla_og", "pool_w", "pool_scale", "gate_b", "norm2_g",
                 "ffn_dw", "ffn_db"):
        shared[name] = np.ascontiguousarray(np.asarray(inputs[name][:nl], dtype=np.float32))
    shared["w_branch"] = np.ascontiguousarray(np.asarray(inputs["w_branch"][:nl], dtype=np.float32).reshape(nl, 4 * 512, D))
    shared["invcnt"] = inv
    per = ncores // 2
    for c in range(ncores):
        m = dict(shared)
        m["x"] = np.ascontiguousarray(x[c // per])
        ins.append(m)
    return ins


def _run(inputs, nl=DEPTH, debug=(), trace=False, ncores=NCORES, stop=None):
    k = Kern(nl, debug, stop)
    nc = k.build()
    ins = _build_inputs(inputs, nl, ncores)
    res = run_bass_kernel_spmd(nc, ins, core_ids=list(range(ncores)), **({"trace": True} if trace else {}))
    return res


def kernel(**inputs):
    res = _run(inputs)
    per = NCORES // 2
    return np.stack([np.asarray(res.results[0]["y"]), np.asarray(res.results[per]["y"])], axis=0).astype(np.float32)
```

```python
import contextlib
import numpy as np
import concourse.bass as bass
import concourse.mybir as mybir
from concourse.bass_utils import run_bass_kernel_spmd

F32 = mybir.dt.float32
BF16 = mybir.dt.bfloat16
ALU = mybir.AluOpType
AF = mybir.ActivationFunctionType

NCORES = 2
DEPTH = 4
D = 2048
T = 8192
SG = 2048
NSG = T // SG
D_IN = 12824
DFF = 5632
EPS = 1e-6
NEG = -30000.0

C_FQ, C_FK, C_FV, C_FF, C_CA, C_CG, C_GQ, C_GK, C_GV, C_GA, C_GR, C_PZ, C_GT = (
    0, 512, 1024, 1536, 1544, 2056, 2568, 2824, 3080, 3592, 3608, 4120, 4632)
ZROWS = 4632


class DSem:
    __slots__ = ("sem", "count", "key")

    def __init__(self, sem, key):
        self.sem = sem
        self.count = 0
        self.key = key


class Buf:
    __slots__ = ("name", "w", "r")

    def __init__(self, name=""):
        self.name = name
        self.w = {}
        self.r = {}


class Stream:
    def __init__(self, fw, key, eng, sem):
        self.fw = fw
        self.key = key
        self.eng = eng
        self.sem = sem
        self.cnt = 0
        self.seen = {}

    def need(self, deps):
        for (k, sem, val) in deps:
            if k == self.key and self.key == "pe":
                continue
            d = self.fw.dkey.get(k)
            if d is not None:
                val = max(val, d.count * 16)
            if val > self.seen.get(k, 0):
                self.seen[k] = val
                self.eng.wait_ge(sem, val)


class FW:
    def __init__(self, nc, esems, dsems):
        self.nc = nc
        self.s = {}
        for key, eng in (("pe", nc.tensor), ("act", nc.scalar), ("dve", nc.vector),
                         ("pool", nc.gpsimd), ("sp", nc.sync)):
            self.s[key] = Stream(self, key, eng, esems[key])
        self.dsems = [DSem(s, "d%d" % i) for i, s in enumerate(dsems)]
        self.dkey = {d.key: d for d in self.dsems}
        self.dnext = 0

    def dsem(self):
        d = self.dsems[self.dnext % len(self.dsems)]
        self.dnext += 1
        return d

    @staticmethod
    def _deps(reads, writes):
        deps = []
        for b in reads:
            deps.extend(b.w.values())
        for b in writes:
            deps.extend(b.w.values())
            deps.extend(b.r.values())
        return deps

    @staticmethod
    def _mark(tok, reads, writes):
        for b in reads:
            b.r[tok[0]] = tok
        for b in writes:
            b.w[tok[0]] = tok
            b.r = {}

    def op(self, key, fn, reads=(), writes=()):
        st = self.s[key]
        st.need(self._deps(reads, writes))
        inst = fn(st.eng)
        st.cnt += 1
        inst.then_inc(st.sem, 1)
        self._mark((key, st.sem, st.cnt), reads, writes)
        return inst

    def dma(self, qkey, ds, out, in_, reads=(), writes=(), **kw):
        st = self.s[qkey]
        deps = self._deps(reads, writes)
        st.need(deps)
        inst = st.eng.dma_start(out=out, in_=in_, **kw)
        ds.count += 1
        inst.then_inc(ds.sem, 16)
        self._mark((ds.key, ds.sem, ds.count * 16), reads, writes)
        return inst

    def barrier(self):
        toks = []
        for k, st in self.s.items():
            if st.cnt:
                toks.append((k, st.sem, st.cnt))
        for d in self.dsems:
            if d.count:
                toks.append((d.key, d.sem, d.count * 16))
        for k, st in self.s.items():
            st.need(toks)
        self.dnext = 0


def _slices(total, step):
    return [(i, min(step, total - i)) for i in range(0, total, step)]


class Kern:
    def __init__(self, nl=DEPTH, debug=(), stop=None):
        self.NL = nl
        self.debug = set(debug)
        self.stop = stop

    def dram(self, name, shape, dt, kind=None):
        if kind is None and name in self.debug:
            kind = "ExternalOutput"
        if kind:
            return self.nc.dram_tensor(name, list(shape), dt, kind=kind)
        return self.nc.dram_tensor(name, list(shape), dt)

    def sb(self, st, name, shape, dt):
        self.uid += 1
        return st.enter_context(self.nc.sbuf_tensor("%s_%d" % (name, self.uid), list(shape), dt))

    def psum_tiles(self, st):
        self.ps = [st.enter_context(self.nc.psum_tensor("ps%d" % i, [128, 512], F32)) for i in range(8)]
        self.psb = [Buf("ps%d" % i) for i in range(8)]
        self.psn = 0

    def pst(self):
        i = self.psn % 4
        self.psn += 1
        return self.ps[i], self.psb[i]

    def dump(self, name, ap, shape, dt, buf):
        if ("dump_" + name) not in self.debug:
            return
        t = self.nc.dram_tensor("dump_" + name, list(shape), dt, kind="ExternalOutput")
        self.fw.dma("sp", self.fw.dsem(), t[tuple(slice(None) for _ in shape)], ap, reads=[buf], writes=[Buf()])

    def load_w(self, W, wb, c0, n, wt, wtb, ds, q="sp"):
        src = W[:, c0:c0 + n].rearrange("(k p) c -> p k c", p=128)
        self.fw.dma(q, ds, wt[:, :, 0:n], src, reads=[wb], writes=[wtb])

    def load_cols(self, dst, dstb, src_vec, n, ds=None):
        fw = self.fw
        ds = ds or fw.dsem()
        fw.dma("sp", ds, dst[:, 0:n], src_vec.rearrange("(k p) -> p k", p=128), writes=[dstb],
               allow_slow_non_contiguous=True)

    def build(self):
        nc = bass.Bass("TRN2", target_bir_lowering=False)
        self.nc = nc
        self.uid = 0
        NL = self.NL
        inp = lambda name, shape: nc.dram_tensor(name, list(shape), F32, kind="ExternalInput")
        self.x_in = inp("x", [T, D])
        self.p = {}
        for name, shape in (("w_in", [NL, D, D_IN]), ("w_branch", [NL, D, D]), ("w_out", [NL, D, D]),
                            ("ffn_up", [NL, D, 2 * DFF]), ("ffn_down", [NL, DFF, D]),
                            ("norm1_g", [NL, D]), ("fox_fb", [NL, 8]), ("fox_qg", [NL, 64]), ("fox_kg", [NL, 64]),
                            ("conv_dw", [NL, 31, 512]), ("conv_db", [NL, 512]), ("conv_ln_g", [NL, 512]),
                            ("conv_ln_b", [NL, 512]), ("gla_wa", [NL, 16, 256]), ("gla_ba", [NL, 256]),
                            ("gla_og", [NL, 512]), ("pool_w", [NL, 4, 128, 128]), ("pool_scale", [NL, 512]),
                            ("gate_b", [NL, 8192]), ("norm2_g", [NL, D]), ("ffn_dw", [NL, 3, 2 * DFF]),
                            ("ffn_db", [NL, 2 * DFF]), ("invcnt", [4, 16])):
            self.p[name] = inp(name, shape)
        self.y_out = nc.dram_tensor("y", [T, D], F32, kind="ExternalOutput")

        self.W = []
        for l in range(NL):
            self.W.append({
                "w_in": self.dram("b_w_in%d" % l, [D, D_IN], BF16), "w_branch": self.dram("b_w_br%d" % l, [D, D], BF16),
                "w_out": self.dram("b_w_out%d" % l, [D, D], BF16), "ffn_up": self.dram("b_f_up%d" % l, [D, 2 * DFF], BF16),
                "ffn_down": self.dram("b_f_dn%d" % l, [DFF, D], BF16)})
        self.Wb = [Buf("W%d" % l) for l in range(NL)]
        self.xT = self.dram("xT", [D, T], F32)
        self.zT = self.dram("zT", [ZROWS, T], F32)
        self.vtok = self.dram("vtok", [T, 1024], BF16)
        self.gatesT = self.dram("gatesT", [8192, T], BF16)
        self.oT = self.dram("oT", [D, T], BF16)
        self.rrow = self.dram("rrow", [8, T], BF16)
        self.b_xT, self.b_zT, self.b_vtok, self.b_gates, self.b_oT, self.b_rrow, self.b_y = (
            Buf("xT"), Buf("zT"), Buf("vtok"), Buf("gatesT"), Buf("oT"), Buf("rrow"), Buf("y"))

        with contextlib.ExitStack() as st:
            esems = {k: st.enter_context(nc.semaphore("s_" + k)) for k in ("pe", "act", "dve", "pool", "sp")}
            dsems = [st.enter_context(nc.semaphore("d%d" % i)) for i in range(40)]
            self.fw = FW(nc, esems, dsems)
            fw = self.fw
            self.psum_tiles(st)
            self.consts(st)
            fw.barrier()
            self.weights_cast()
            self.x_to_xT()
            for l in range(NL):
                self.layer(l)
            self.xT_to_y()
            fw.barrier()
        return nc

    def consts(self, st):
        nc, fw = self.nc, self.fw
        self.ident_f = self.sb(st, "ident_f", [128, 128], F32)
        self.ident_b = self.sb(st, "ident_b", [128, 128], BF16)
        self.ones_f = self.sb(st, "ones_f", [128, 128], F32)
        self.ones_b = self.sb(st, "ones_b", [128, 128], BF16)
        self.bo64 = self.sb(st, "bo64", [128, 128], F32)
        self.sel127 = self.sb(st, "sel127", [128, 128], F32)
        self.U_b = self.sb(st, "U_b", [128, 128], BF16)
        self.maskneg = self.sb(st, "maskneg", [128, 128], BF16)
        self.eps_col = self.sb(st, "eps_col", [128, 1], F32)
        self.zero_col = self.sb(st, "zero_col", [128, 1], F32)
        self.ones_big = self.sb(st, "ones_big", [128, 2048], F32)
        self.b_const = Buf("const")
        cb = [self.b_const]
        fw.op("pool", lambda e: e.memset(self.ones_f[:, :], 1.0), writes=cb)
        fw.op("pool", lambda e: e.memset(self.ones_b[:, :], 1.0), writes=cb)
        fw.op("pool", lambda e: e.memset(self.eps_col[:, :], EPS), writes=cb)
        fw.op("pool", lambda e: e.memset(self.zero_col[:, :], 0.0), writes=cb)
        fw.op("pool", lambda e: e.memset(self.ones_big[:, :], 1.0), writes=cb)
        fw.op("pool", lambda e: e.memset(self.bo64[:, :], 0.0), writes=cb)
        fw.op("pool", lambda e: e.memset(self.bo64[0:64, 0:64], 1.0), writes=cb)
        fw.op("pool", lambda e: e.memset(self.bo64[64:128, 64:128], 1.0), writes=cb)
        sel = lambda out, in_, op, base, cm, pat: fw.op("pool", lambda e: e.affine_select(
            out=out, in_=in_, pattern=pat, compare_op=op, fill=0.0, base=base, channel_multiplier=cm), reads=cb, writes=cb)
        sel(self.ident_f[:, :], self.ones_f[:, :], ALU.is_equal, 0, -1, [[1, 128]])
        sel(self.ident_b[:, :], self.ones_b[:, :], ALU.is_equal, 0, -1, [[1, 128]])
        sel(self.U_b[:, :], self.ones_b[:, :], ALU.is_ge, 0, -1, [[1, 128]])
        sel(self.sel127[:, :], self.ones_f[:, :], ALU.is_equal, -127, 1, [[0, 128]])
        fw.op("dve", lambda e: e.tensor_scalar(out=self.maskneg[:, :], in0=self.U_b[:, :], scalar1=-1.0, scalar2=-NEG,
                                               op0=ALU.add, op1=ALU.mult), reads=cb, writes=cb)

    def weights_cast(self):
        fw = self.fw
        for l in range(self.NL):
            ds = fw.dsem()
            for name, rows in (("w_in", D), ("w_branch", D), ("w_out", D), ("ffn_up", D), ("ffn_down", DFF)):
                nsplit = 8
                rs = rows // nsplit
                for i in range(nsplit):
                    fw.dma("pool", ds, self.W[l][name][i * rs:(i + 1) * rs, :], self.p[name][l, i * rs:(i + 1) * rs, :],
                           writes=[self.Wb[l]])
        fw.barrier()

    def x_to_xT(self):
        nc, fw = self.nc, self.fw
        with contextlib.ExitStack() as st:
            xin = [self.sb(st, "xin%d" % i, [128, D], F32) for i in range(2)]
            xinb = [Buf() for _ in range(2)]
            xo = [self.sb(st, "xo%d" % i, [128, 4, 128], F32) for i in range(4)]
            xob = [Buf() for _ in range(4)]
            dsl = [fw.dsem() for _ in range(2)]
            dso = [fw.dsem() for _ in range(4)]
            k = 0
            for tt in range(T // 128):
                s = tt % 2
                fw.dma("sp", dsl[s], xin[s][:, :], self.x_in[tt * 128:(tt + 1) * 128, :], writes=[xinb[s]])
                for dg in range(4):
                    ps, psb = self.pst()
                    for j in range(4):
                        dc = dg * 4 + j
                        fw.op("pe", lambda e, ps=ps, j=j, dc=dc, s=s: e.transpose(out=ps[:, j * 128:(j + 1) * 128],
                              in_=xin[s][:, dc * 128:(dc + 1) * 128], identity=self.ident_f[:, :]),
                              reads=[xinb[s], self.b_const], writes=[psb])
                    o = k % 4
                    k += 1
                    if k % 2:
                        fw.op("dve", lambda e, ps=ps, o=o: e.tensor_copy(out=xo[o][:, :, :].rearrange("p a b -> p (a b)"), in_=ps[:, :]),
                              reads=[psb], writes=[xob[o]])
                    else:
                        fw.op("act", lambda e, ps=ps, o=o: e.copy(out=xo[o][:, :, :].rearrange("p a b -> p (a b)"), in_=ps[:, :]),
                              reads=[psb], writes=[xob[o]])
                    dst = self.xT[dg * 512:(dg + 1) * 512, tt * 128:(tt + 1) * 128].rearrange("(j p) t -> p j t", p=128)
                    fw.dma("pool", dso[o], dst, xo[o][:, :, :], reads=[xob[o]], writes=[self.b_xT])
        fw.barrier()

    def xT_to_y(self):
        nc, fw = self.nc, self.fw
        with contextlib.ExitStack() as st:
            xin = [self.sb(st, "yin%d" % i, [128, 4, 512], F32) for i in range(2)]
            xinb = [Buf() for _ in range(2)]
            xo = [self.sb(st, "yo%d" % i, [128, 512], F32) for i in range(4)]
            xob = [Buf() for _ in range(4)]
            dsl = [fw.dsem() for _ in range(2)]
            dso = [fw.dsem() for _ in range(4)]
            k = 0
            n = 0
            for dg in range(4):
                for tg in range(T // 512):
                    s = n % 2
                    n += 1
                    src = self.xT[dg * 512:(dg + 1) * 512, tg * 512:(tg + 1) * 512].rearrange("(j p) t -> p j t", p=128)
                    fw.dma("sp", dsl[s], xin[s][:, :, :], src, reads=[self.b_xT], writes=[xinb[s]])
                    for ti in range(4):
                        ps, psb = self.pst()
                        for j in range(4):
                            fw.op("pe", lambda e, ps=ps, j=j, ti=ti, s=s: e.transpose(out=ps[:, j * 128:(j + 1) * 128],
                                  in_=xin[s][:, j, ti * 128:(ti + 1) * 128], identity=self.ident_f[:, :]),
                                  reads=[xinb[s], self.b_const], writes=[psb])
                        o = k % 4
                        k += 1
                        if k % 2:
                            fw.op("dve", lambda e, ps=ps, o=o: e.tensor_copy(out=xo[o][:, :], in_=ps[:, :]), reads=[psb], writes=[xob[o]])
                        else:
                            fw.op("act", lambda e, ps=ps, o=o: e.copy(out=xo[o][:, :], in_=ps[:, :]), reads=[psb], writes=[xob[o]])
                        tok0 = tg * 512 + ti * 128
                        fw.dma("pool", dso[o], self.y_out[tok0:tok0 + 128, dg * 512:(dg + 1) * 512], xo[o][:, :],
                               reads=[xob[o]], writes=[self.b_y])

    def layer(self, l):
        fw = self.fw
        stop = self.stop
        for sg in range(NSG):
            with contextlib.ExitStack() as st:
                hT = self.sb(st, "hT", [128, 16, SG], BF16)
                hTb = Buf("hT")
                self.rmsnorm_T("norm1_g", l, sg * SG, SG, hT, hTb, 0)
                self.in_proj(l, sg, hT, hTb)
                fw.barrier()
        if stop == "in_proj":
            return
        if "skip_fox" not in self.debug:
            self.fox(l)
        if stop == "fox":
            return
        if "skip_conv" not in self.debug:
            self.conv(l)
        if stop == "conv":
            return
        if "skip_gla" not in self.debug:
            self.gla(l)
        if stop == "gla":
            return
        if "skip_pool" not in self.debug:
            self.pool(l)
        if stop == "pool":
            return
        for sb in range(T // 1024):
            self.branch_out(l, sb)
        if stop == "branch":
            return
        self.ffn(l)

    def rmsnorm_T(self, gname, l, tok0, ntok, hT, hTb, off):
        nc, fw = self.nc, self.fw
        with contextlib.ExitStack() as st:
            xg = [self.sb(st, "nx%d" % i, [128, 16, 512], F32) for i in range(2)]
            xgb = [Buf() for _ in range(2)]
            sq = [self.sb(st, "nsq%d" % i, [128, 512], F32) for i in range(2)]
            sqb = [Buf() for _ in range(2)]
            rs = self.sb(st, "nrs", [128, 512], F32)
            rsb = Buf()
            gcol = self.sb(st, "ngc", [128, 16], F32)
            gcb = Buf()
            dsl = [fw.dsem() for _ in range(2)]
            self.load_cols(gcol, gcb, self.p[gname][l, :], 16)
            for tg in range(ntok // 512):
                s = tg % 2
                t0 = tok0 + tg * 512
                for dq in range(4):
                    src = self.xT[dq * 512:(dq + 1) * 512, t0:t0 + 512].rearrange("(j p) t -> p j t", p=128)
                    fw.dma("sp", dsl[s], xg[s][:, dq * 4:(dq + 1) * 4, :], src, reads=[self.b_xT], writes=[xgb[s]])
                ps, psb = self.pst()
                for dc in range(16):
                    q = dc % 2
                    fw.op("act", lambda e, s=s, dc=dc, q=q: e.activation(out=sq[q][:, :], in_=xg[s][:, dc, :], func=AF.Square),
                          reads=[xgb[s]], writes=[sqb[q]])
                    fw.op("pe", lambda e, ps=ps, q=q, dc=dc: e.matmul(ps[:, :], lhsT=self.ones_f[:, :], rhs=sq[q][:, :],
                                                                     start=(dc == 0), stop=(dc == 15)),
                          reads=[sqb[q], self.b_const], writes=[psb])
                fw.op("act", lambda e, ps=ps: e.activation(out=rs[:, :], in_=ps[:, :], func=AF.Sqrt, scale=1.0 / D, bias=self.eps_col[:, 0:1]),
                      reads=[psb, self.b_const], writes=[rsb])
                fw.op("dve", lambda e: e.reciprocal(out=rs[:, :], in_=rs[:, :]), reads=[rsb], writes=[rsb])
                for dc in range(16):
                    fw.op("dve", lambda e, s=s, dc=dc, tg=tg: e.scalar_tensor_tensor(
                        out=hT[:, dc, off + tg * 512:off + (tg + 1) * 512], in0=xg[s][:, dc, :], scalar=gcol[:, dc:dc + 1], in1=rs[:, :],
                        op0=ALU.mult, op1=ALU.mult), reads=[xgb[s], rsb, gcb], writes=[hTb])
        fw.barrier()

    def in_proj(self, l, sg, hT, hTb):
        nc, fw = self.nc, self.fw
        T0 = sg * SG
        chunks = []

        def seg(c0, w, kind):
            for (o, n) in _slices(w, 128):
                chunks.append((c0 + o, n, kind))
        seg(C_FQ, 512, "raw"); seg(C_FK, 512, "raw"); chunks.append((C_FV, 512, "vtok0"))
        seg(C_FF, 8, "raw"); seg(C_CA, 512, "raw"); seg(C_CG, 512, "sigmoid")
        seg(C_GQ, 256, "raw"); seg(C_GK, 256, "raw"); chunks.append((C_GV, 512, "vtok1"))
        seg(C_GA, 16, "raw"); seg(C_GR, 512, "silu"); seg(C_PZ, 512, "raw"); seg(C_GT, 8192, "gate")
        blocks = []
        cur = []
        for ch in chunks:
            if cur and (ch[0] + ch[1] - cur[0][0] > 512):
                blocks.append(cur); cur = []
            cur.append(ch)
        if cur:
            blocks.append(cur)
        W = self.W[l]["w_in"]
        with contextlib.ExitStack() as st:
            wt = [self.sb(st, "ipw%d" % i, [128, 16, 512], BF16) for i in range(2)]
            wtb = [Buf() for _ in range(2)]
            wds = [fw.dsem() for _ in range(2)]
            stg = [self.sb(st, "ips%d" % i, [128, 512], F32) for i in range(4)]
            stgb = [Buf() for _ in range(4)]
            sds = [fw.dsem() for _ in range(4)]
            stgh = [self.sb(st, "iph%d" % i, [128, 512], BF16) for i in range(4)]
            stghb = [Buf() for _ in range(4)]
            hds = [fw.dsem() for _ in range(4)]
            gb = self.sb(st, "ipgb", [128, 64], F32)
            gbb = Buf()
            self.load_cols(gb, gbb, self.p["gate_b"][l, :], 64)
            k32 = 0
            k16 = 0
            for bi, blk in enumerate(blocks):
                s = bi % 2
                c0 = blk[0][0]
                ncol = blk[-1][0] + blk[-1][1] - c0
                self.load_w(W, self.Wb[l], c0, ncol, wt[s], wtb[s], wds[s])
                for (cc, n, kind) in blk:
                    o = cc - c0
                    if kind.startswith("vtok"):
                        vo = 0 if kind == "vtok0" else 512
                        for tt in range(SG // 128):
                            ps, psb = self.pst()
                            for dc in range(16):
                                fw.op("pe", lambda e, ps=ps, dc=dc, tt=tt, s=s, o=o: e.matmul(
                                    ps[:, :], lhsT=hT[:, dc, tt * 128:(tt + 1) * 128], rhs=wt[s][:, dc, o:o + 512],
                                    start=(dc == 0), stop=(dc == 15)), reads=[hTb, wtb[s]], writes=[psb])
                            q = k16 % 4
                            k16 += 1
                            if k16 % 2:
                                fw.op("dve", lambda e, ps=ps, q=q: e.tensor_copy(out=stgh[q][:, :], in_=ps[:, :]), reads=[psb], writes=[stghb[q]])
                            else:
                                fw.op("act", lambda e, ps=ps, q=q: e.copy(out=stgh[q][:, :], in_=ps[:, :]), reads=[psb], writes=[stghb[q]])
                            fw.dma("pool", hds[q], self.vtok[T0 + tt * 128:T0 + (tt + 1) * 128, vo:vo + 512], stgh[q][:, :],
                                   reads=[stghb[q]], writes=[self.b_vtok])
                        continue
                    for tg in range(SG // 512):
                        tk = T0 + tg * 512
                        ps, psb = self.pst()
                        for dc in range(16):
                            fw.op("pe", lambda e, ps=ps, dc=dc, tg=tg, s=s, o=o, n=n: e.matmul(
                                ps[0:n, :], lhsT=wt[s][:, dc, o:o + n], rhs=hT[:, dc, tg * 512:(tg + 1) * 512],
                                start=(dc == 0), stop=(dc == 15)), reads=[hTb, wtb[s]], writes=[psb])
                        if kind == "gate":
                            q = k16 % 4
                            k16 += 1
                            gi = (cc - C_GT) // 128
                            fw.op("act", lambda e, ps=ps, q=q, gi=gi: e.activation(out=stgh[q][:, :], in_=ps[:, :], func=AF.Sigmoid,
                                                                                   bias=gb[:, gi:gi + 1]),
                                  reads=[psb, gbb], writes=[stghb[q]])
                            fw.dma("pool", hds[q], self.gatesT[cc - C_GT:cc - C_GT + 128, tk:tk + 512], stgh[q][:, :],
                                   reads=[stghb[q]], writes=[self.b_gates])
                        else:
                            q = k32 % 4
                            k32 += 1
                            if kind == "sigmoid":
                                fw.op("act", lambda e, ps=ps, q=q, n=n: e.activation(out=stg[q][0:n, :], in_=ps[0:n, :], func=AF.Sigmoid),
                                      reads=[psb], writes=[stgb[q]])
                            elif kind == "silu":
                                fw.op("act", lambda e, ps=ps, q=q, n=n: e.activation(out=stg[q][0:n, :], in_=ps[0:n, :], func=AF.Silu),
                                      reads=[psb], writes=[stgb[q]])
                            else:
                                fw.op("dve", lambda e, ps=ps, q=q, n=n: e.tensor_copy(out=stg[q][0:n, :], in_=ps[0:n, :]),
                                      reads=[psb], writes=[stgb[q]])
                            fw.dma("pool", sds[q], self.zT[cc:cc + n, tk:tk + 512], stg[q][0:n, :],
                                   reads=[stgb[q]], writes=[self.b_zT])

    def fox(self, l):
        nc, fw = self.nc, self.fw
        NTI = T // 128
        NG = T // 512
        with contextlib.ExitStack() as st:
            cpT = self.sb(st, "cpT", [128, NTI, 8], F32)
            cpTb = Buf()
            refT = self.sb(st, "refT", [128, NG, 8], F32)
            refTb = Buf()
            with contextlib.ExitStack() as st1:
                sp = self.sb(st1, "fsp", [8, T], F32)
                cp = self.sb(st1, "fcp", [8, T], F32)
                rb = self.sb(st1, "frb", [8, T], BF16)
                nfb = self.sb(st1, "nfb", [8, 1], F32)
                b_sp, b_cp, b_rb, b_nfb = Buf(), Buf(), Buf(), Buf()
                ds = fw.dsem()
                fw.dma("sp", ds, sp[:, :], self.zT[C_FF:C_FF + 8, :], reads=[self.b_zT], writes=[b_sp])
                fw.dma("sp", ds, nfb[:, :], self.p["fox_fb"][l, :].rearrange("(h o) -> h o", o=1), writes=[b_nfb])
                fw.op("dve", lambda e: e.tensor_scalar(out=nfb[:, :], in0=nfb[:, :], scalar1=-1.0, scalar2=None, op0=ALU.mult),
                      reads=[b_nfb], writes=[b_nfb])
                for (o, n) in _slices(T, 2048):
                    fw.op("act", lambda e, o=o, n=n: e.activation(out=sp[:, o:o + n], in_=sp[:, o:o + n], func=AF.Exp, scale=-1.0, bias=nfb[:, 0:1]),
                          reads=[b_sp, b_nfb], writes=[b_sp])
                for (o, n) in _slices(T, 2048):
                    fw.op("act", lambda e, o=o, n=n: e.activation(out=sp[:, o:o + n], in_=sp[:, o:o + n], func=AF.Ln, bias=self.ones_f[0:8, 0:1]),
                          reads=[b_sp, self.b_const], writes=[b_sp])
                for i, (o, n) in enumerate(_slices(T, 2048)):
                    init = 0.0 if i == 0 else cp[:, o - 1:o]
                    self._cumsum(cp, sp, o, n, init, b_sp, b_cp)
                for G in range(NG):
                    ref = self.zero_col[0:8, 0:1] if G == 0 else cp[:, G * 512 - 1:G * 512]
                    fw.op("dve", lambda e, G=G, ref=ref: e.tensor_scalar(out=rb[:, G * 512:(G + 1) * 512], in0=cp[:, G * 512:(G + 1) * 512],
                                                                        scalar1=ref, scalar2=-1.0, op0=ALU.subtract, op1=ALU.mult),
                          reads=[b_cp, self.b_const], writes=[b_rb])
                ds2 = fw.dsem()
                fw.dma("pool", ds2, self.rrow[:, :], rb[:, :], reads=[b_rb], writes=[self.b_rrow])
                for t4 in range(NTI // 16):
                    ps, psb = self.pst()
                    for j in range(16):
                        ti = t4 * 16 + j
                        fw.op("pe", lambda e, ps=ps, j=j, ti=ti: e.transpose(out=ps[:, j * 8:(j + 1) * 8], in_=cp[:, ti * 128:(ti + 1) * 128],
                                                                            identity=self.ident_f[0:8, 0:8]),
                              reads=[b_cp, self.b_const], writes=[psb])
                    fw.op("dve", lambda e, ps=ps, t4=t4: e.tensor_copy(out=cpT[:, t4 * 16:(t4 + 1) * 16, :].rearrange("p a b -> p (a b)"),
                                                                        in_=ps[:, 0:128]), reads=[psb], writes=[cpTb])
                ps, psb = self.pst()
                fw.op("pe", lambda e, ps=ps: e.matmul(ps[:, 0:(NG - 1) * 8].rearrange("p (a b) -> p a b", b=8), lhsT=self.sel127[:, :],
                                                      rhs=cpT[:, :, :].rearrange("p (g f) h -> p g f h", f=4)[:, 0:NG - 1, 3, :], start=True, stop=True),
                      reads=[cpTb, self.b_const], writes=[psb])
                fw.op("dve", lambda e: e.memset(refT[:, 0, :], 0.0), writes=[refTb])
                fw.op("dve", lambda e, ps=ps: e.tensor_copy(out=refT[:, 1:NG, :].rearrange("p a b -> p (a b)"), in_=ps[:, 0:(NG - 1) * 8]),
                      reads=[psb], writes=[refTb])
            fw.barrier()
            V = self.sb(st, "fV", [128, NTI, 512], BF16)
            Vb = Buf()
            ds = fw.dsem()
            for q4 in range(4):
                n4 = NTI // 4
                fw.dma("sp", ds, V[:, q4 * n4:(q4 + 1) * n4, :],
                       self.vtok[q4 * n4 * 128:(q4 + 1) * n4 * 128, 0:512].rearrange("(n p) c -> p n c", p=128),
                       reads=[self.b_vtok], writes=[Vb])
            qpad = [self.sb(st, "qpad%d" % e, [128, T], BF16) for e in range(2)]
            kpad = [self.sb(st, "kpad%d" % e, [128, T], BF16) for e in range(2)]
            qpb = [Buf() for _ in range(2)]
            kpb = [Buf() for _ in range(2)]
            for e_ in range(2):
                fw.op("pool", lambda e, e_=e_: e.memset(qpad[e_][:, :], 0.0), writes=[qpb[e_]])
                fw.op("pool", lambda e, e_=e_: e.memset(kpad[e_][:, :], 0.0), writes=[kpb[e_]])
            fw.op("pool", lambda e: e.memset(kpad[0][64:65, :], 1.0), writes=[kpb[0]])
            fw.op("pool", lambda e: e.memset(kpad[1][0:1, :], 1.0), writes=[kpb[1]])
            gq = self.sb(st, "fgq", [128, 1], F32)
            gk = self.sb(st, "fgk", [128, 1], F32)
            gb_ = Buf()
            ds = fw.dsem()
            for half in range(2):
                fw.dma("sp", ds, gq[half * 64:(half + 1) * 64, :], self.p["fox_qg"][l, :].rearrange("(h o) -> h o", o=1), writes=[gb_])
                fw.dma("sp", ds, gk[half * 64:(half + 1) * 64, :], self.p["fox_kg"][l, :].rearrange("(h o) -> h o", o=1), writes=[gb_])
            fw.op("dve", lambda e: e.tensor_scalar(out=gq[:, :], in0=gq[:, :], scalar1=0.125, scalar2=None, op0=ALU.mult), reads=[gb_], writes=[gb_])
            raw = [self.sb(st, "fraw%d" % i, [128, 512], F32) for i in range(2)]
            rawb = [Buf() for _ in range(2)]
            rds = [fw.dsem() for _ in range(2)]
            sq = self.sb(st, "fsq", [128, 512], F32)
            sqb = Buf()
            rs = self.sb(st, "frs", [128, 512], F32)
            rsb = Buf()
            PT = [self.sb(st, "fPT%d" % i, [128, 512], BF16) for i in range(4)]
            PTb = [Buf() for _ in range(4)]
            bias = [self.sb(st, "fbias%d" % i, [128, NTI], F32) for i in range(2)]
            biasb = [Buf() for _ in range(2)]
            rden = self.sb(st, "frden", [64, 512], F32)
            rdenb = Buf()
            ost = [self.sb(st, "fost%d" % i, [64, 512], BF16) for i in range(2)]
            ostb = [Buf() for _ in range(2)]
            ods = [fw.dsem() for _ in range(2)]
            ads = fw.dsem()
            kraw = 0
            kpt = 0
            kb = 0
            ko = 0
            for j in range(4):
                for which in range(2):
                    row0 = (C_FQ if which == 0 else C_FK) + j * 128
                    gcol = gq if which == 0 else gk
                    dst = qpad if which == 0 else kpad
                    dstb = qpb if which == 0 else kpb
                    for G in range(NG):
                        s = kraw % 2
                        kraw += 1
                        fw.dma("sp", rds[s], raw[s][:, :], self.zT[row0:row0 + 128, G * 512:(G + 1) * 512], reads=[self.b_zT], writes=[rawb[s]])
                        fw.op("act", lambda e, s=s: e.activation(out=sq[:, :], in_=raw[s][:, :], func=AF.Square), reads=[rawb[s]], writes=[sqb])
                        ps, psb = self.pst()
                        fw.op("pe", lambda e, ps=ps: e.matmul(ps[:, :], lhsT=self.bo64[:, :], rhs=sq[:, :], start=True, stop=True),
                              reads=[sqb, self.b_const], writes=[psb])
                        fw.op("act", lambda e, ps=ps: e.activation(out=rs[:, :], in_=ps[:, :], func=AF.Sqrt, scale=1.0 / 64, bias=self.eps_col[:, 0:1]),
                              reads=[psb, self.b_const], writes=[rsb])
                        fw.op("dve", lambda e: e.reciprocal(out=rs[:, :], in_=rs[:, :]), reads=[rsb], writes=[rsb])
                        for e_ in range(2):
                            pr = slice(e_ * 64, (e_ + 1) * 64)
                            fw.op("dve", lambda e, s=s, G=G, e_=e_, pr=pr, dst=dst, gcol=gcol: e.scalar_tensor_tensor(
                                out=dst[e_][pr, G * 512:(G + 1) * 512], in0=raw[s][pr, :], scalar=gcol[pr, 0:1], in1=rs[pr, :],
                                op0=ALU.mult, op1=ALU.mult), reads=[rawb[s], rsb, gb_], writes=[dstb[e_]])
                fw.dma("sp", ads, qpad[0][64:65, :], self.rrow[2 * j:2 * j + 1, :], reads=[self.b_rrow], writes=[qpb[0]])
                fw.dma("sp", ads, qpad[1][0:1, :], self.rrow[2 * j + 1:2 * j + 2, :], reads=[self.b_rrow], writes=[qpb[1]])
                for e_ in range(2):
                    h = 2 * j + e_
                    for G in range(NG):
                        nt = 4 * G + 4
                        bs = kb % 2
                        kb += 1
                        fw.op("dve", lambda e, bs=bs, nt=nt, h=h, G=G: e.tensor_scalar(
                            out=bias[bs][:, 0:nt], in0=cpT[:, 0:nt, h], scalar1=refT[:, G, h:h + 1], scalar2=None, op0=ALU.subtract),
                            reads=[cpTb, refTb], writes=[biasb[bs]])
                        acc = 4 + 2 * (G % 2)
                        po, pob = self.ps[acc], self.psb[acc]
                        pd, pdb = self.ps[acc + 1], self.psb[acc + 1]
                        def front(ti, G=G, e_=e_, bs=bs):
                            nonlocal kpt
                            i = ti - 4 * G
                            c0 = max(i, 0) * 128
                            ps, psb = self.pst()
                            fw.op("pe", lambda e: e.matmul(
                                ps[:, c0:512], lhsT=kpad[e_][:, ti * 128:(ti + 1) * 128], rhs=qpad[e_][:, G * 512 + c0:(G + 1) * 512],
                                start=True, stop=(i < 0)), reads=[kpb[e_], qpb[e_]], writes=[psb])
                            if i >= 0:
                                fw.op("pe", lambda e: e.matmul(ps[:, c0:c0 + 128], lhsT=self.ident_b[:, :], rhs=self.maskneg[:, :],
                                                               start=False, stop=True), reads=[self.b_const], writes=[psb])
                            p_ = kpt % 4
                            kpt += 1
                            fw.op("act", lambda e: e.activation(
                                out=PT[p_][:, c0:512], in_=ps[:, c0:512], func=AF.Exp, bias=bias[bs][:, ti:ti + 1]),
                                reads=[psb, biasb[bs]], writes=[PTb[p_]])
                            return (ti, c0, p_)

                        def back(item, h=h, nt=nt, po=po, pob=pob, pd=pd, pdb=pdb):
                            ti, c0, p_ = item
                            fw.op("pe", lambda e: e.matmul(
                                po[0:64, c0:512], lhsT=V[:, ti, h * 64:(h + 1) * 64], rhs=PT[p_][:, c0:512],
                                start=(ti == 0), stop=(ti == nt - 1)), reads=[Vb, PTb[p_]], writes=[pob])
                            fw.op("pe", lambda e: e.matmul(
                                pd[0:64, c0:512], lhsT=self.ones_b[:, 0:64], rhs=PT[p_][:, c0:512],
                                start=(ti == 0), stop=(ti == nt - 1)), reads=[self.b_const, PTb[p_]], writes=[pdb])

                        pend = []
                        for ti in range(nt):
                            pend.append(front(ti))
                            if len(pend) > 2:
                                back(pend.pop(0))
                        for item in pend:
                            back(item)
                        fw.op("dve", lambda e, pd=pd: e.reciprocal(out=rden[:, :], in_=pd[0:64, :]), reads=[pdb], writes=[rdenb])
                        o_ = ko % 2
                        ko += 1
                        fw.op("dve", lambda e, po=po, o_=o_: e.tensor_tensor(out=ost[o_][:, :], in0=po[0:64, :], in1=rden[:, :], op=ALU.mult),
                              reads=[pob, rdenb], writes=[ostb[o_]])
                        fw.dma("pool", ods[o_], self.oT[h * 64:(h + 1) * 64, G * 512:(G + 1) * 512], ost[o_][:, :],
                               reads=[ostb[o_]], writes=[self.b_oT])
        fw.barrier()

    def _cumsum(self, cp, sp, o, n, init, b_sp, b_cp):
        self.fw.op("dve", lambda e: e.tensor_tensor_scan(out=cp[:, o:o + n], data0=self.ones_big[0:8, 0:n], data1=sp[:, o:o + n],
                                                         initial=init, op0=ALU.mult, op1=ALU.add),
                   reads=[b_sp, b_cp, self.b_const], writes=[b_cp])

    def conv(self, l):
        nc, fw = self.nc, self.fw
        NG = T // 512
        with contextlib.ExitStack() as st:
            uT = self.sb(st, "cuT", [128, 4, 32 + T], BF16)
            uTb = Buf()
            diag = self.sb(st, "cdiag", [128, 31, 4, 128], BF16)
            diagb = Buf()
            wT = self.sb(st, "cwT", [128, 4, 32], F32)
            wTb = Buf()
            cols = self.sb(st, "ccols", [128, 3, 4], F32)
            colsb = Buf()
            with contextlib.ExitStack() as st1:
                ds = fw.dsem()
                for cc in range(4):
                    fw.dma("sp", ds, wT[:, cc, 0:31], self.p["conv_dw"][l, :, cc * 128:(cc + 1) * 128].rearrange("k p -> p k"), writes=[wTb],
                           allow_slow_non_contiguous=True)
                for i, nm in enumerate(("conv_db", "conv_ln_g", "conv_ln_b")):
                    fw.dma("sp", ds, cols[:, i, :], self.p[nm][l, :].rearrange("(k p) -> p k", p=128), writes=[colsb],
                           allow_slow_non_contiguous=True)
                for k in range(31):
                    for cc in range(4):
                        eng = "dve"
                        fw.op(eng, lambda e, k=k, cc=cc: e.tensor_scalar(out=diag[:, k, cc, :], in0=self.ident_b[:, :], scalar1=wT[:, cc, k:k + 1],
                                                                          scalar2=None, op0=ALU.mult), reads=[wTb, self.b_const], writes=[diagb])
                fw.op("pool", lambda e: e.memset(uT[:, :, 0:32], 0.0), writes=[uTb])
                a_t = [self.sb(st1, "ca%d" % i, [128, 2048], F32) for i in range(2)]
                g_t = [self.sb(st1, "cg%d" % i, [128, 2048], F32) for i in range(2)]
                atb = [Buf() for _ in range(2)]
                lds = [fw.dsem() for _ in range(2)]
                n = 0
                for cc in range(4):
                    for tb in range(T // 2048):
                        s = n % 2
                        n += 1
                        fw.dma("sp", lds[s], a_t[s][:, :], self.zT[C_CA + cc * 128:C_CA + (cc + 1) * 128, tb * 2048:(tb + 1) * 2048],
                               reads=[self.b_zT], writes=[atb[s]])
                        fw.dma("sp", lds[s], g_t[s][:, :], self.zT[C_CG + cc * 128:C_CG + (cc + 1) * 128, tb * 2048:(tb + 1) * 2048],
                               reads=[self.b_zT], writes=[atb[s]])
                        eng = "dve" if n % 2 else "pool"
                        fw.op(eng, lambda e, s=s, cc=cc, tb=tb: e.tensor_tensor(out=uT[:, cc, 32 + tb * 2048:32 + (tb + 1) * 2048], in0=a_t[s][:, :],
                                                                                in1=g_t[s][:, :], op=ALU.mult), reads=[atb[s]], writes=[uTb])
            self.dump("wT", wT[:, :, :], [128, 4, 32], F32, wTb)
            self.dump("diag", diag[:, 3, 1, :], [128, 128], BF16, diagb)
            self.dump("uT", uT[:, 0, 0:600], [128, 600], BF16, uTb)
            fw.barrier()
            y = [self.sb(st, "cy%d" % i, [128, 4, 512], F32) for i in range(2)]
            yb = [Buf() for _ in range(2)]
            sq = [self.sb(st, "csq%d" % i, [128, 512], F32) for i in range(2)]
            sqb = [Buf() for _ in range(2)]
            rs = self.sb(st, "crs", [128, 512], F32)
            rsb = Buf()
            ob = [self.sb(st, "cob%d" % i, [128, 512], BF16) for i in range(4)]
            obb = [Buf() for _ in range(4)]
            ods = [fw.dsem() for _ in range(4)]
            ko = 0
            for g in range(NG):
                s = g % 2
                for cc in range(4):
                    ps, psb = self.pst()
                    for k in range(31):
                        c0 = 2 + k + g * 512
                        fw.op("pe", lambda e, ps=ps, k=k, cc=cc, c0=c0: e.matmul(ps[:, :], lhsT=diag[:, k, cc, :], rhs=uT[:, cc, c0:c0 + 512],
                                                                                 start=(k == 0), stop=(k == 30)), reads=[diagb, uTb], writes=[psb])
                    fw.op("dve", lambda e, ps=ps, s=s, cc=cc: e.tensor_scalar(out=y[s][:, cc, :], in0=ps[:, :], scalar1=cols[:, 0, cc:cc + 1], scalar2=None, op0=ALU.add),
                          reads=[psb, colsb], writes=[yb[s]])
                if g == 0:
                    self.dump("ypre", y[s][:, :, :], [128, 4, 512], F32, yb[s])
                pm, pmb = self.pst()
                for cc in range(4):
                    fw.op("pe", lambda e, pm=pm, s=s, cc=cc: e.matmul(pm[:, :], lhsT=self.ones_f[:, :], rhs=y[s][:, cc, :], start=(cc == 0), stop=(cc == 3)),
                          reads=[yb[s], self.b_const], writes=[pmb])
                for cc in range(4):
                    fw.op("dve", lambda e, pm=pm, s=s, cc=cc: e.scalar_tensor_tensor(out=y[s][:, cc, :], in0=pm[:, :], scalar=-1.0 / 512, in1=y[s][:, cc, :],
                                                                                     op0=ALU.mult, op1=ALU.add), reads=[pmb, yb[s]], writes=[yb[s]])
                if g == 0:
                    self.dump("ymid", y[s][:, :, :], [128, 4, 512], F32, yb[s])
                pv, pvb = self.pst()
                for cc in range(4):
                    q = cc % 2
                    fw.op("act", lambda e, s=s, cc=cc, q=q: e.activation(out=sq[q][:, :], in_=y[s][:, cc, :], func=AF.Square), reads=[yb[s]], writes=[sqb[q]])
                    fw.op("pe", lambda e, pv=pv, q=q, cc=cc: e.matmul(pv[:, :], lhsT=self.ones_f[:, :], rhs=sq[q][:, :], start=(cc == 0), stop=(cc == 3)),
                          reads=[sqb[q], self.b_const], writes=[pvb])
                fw.op("act", lambda e, pv=pv: e.activation(out=rs[:, :], in_=pv[:, :], func=AF.Sqrt, scale=1.0 / 512, bias=self.eps_col[:, 0:1]),
                      reads=[pvb, self.b_const], writes=[rsb])
                fw.op("dve", lambda e: e.reciprocal(out=rs[:, :], in_=rs[:, :]), reads=[rsb], writes=[rsb])
                if g == 0:
                    self.dump("rs", rs[:, :], [128, 512], F32, rsb)
                for cc in range(4):
                    fw.op("dve", lambda e, s=s, cc=cc: e.tensor_tensor(out=y[s][:, cc, :], in0=y[s][:, cc, :], in1=rs[:, :], op=ALU.mult),
                          reads=[yb[s], rsb], writes=[yb[s]])
                    fw.op("dve", lambda e, s=s, cc=cc: e.tensor_scalar(out=y[s][:, cc, :], in0=y[s][:, cc, :], scalar1=cols[:, 1, cc:cc + 1],
                                                                        scalar2=cols[:, 2, cc:cc + 1], op0=ALU.mult, op1=ALU.add),
                          reads=[yb[s], colsb], writes=[yb[s]])
                    o = ko % 4
                    ko += 1
                    fw.op("act", lambda e, s=s, cc=cc, o=o: e.activation(out=ob[o][:, :], in_=y[s][:, cc, :], func=AF.Silu), reads=[yb[s]], writes=[obb[o]])
                    fw.dma("pool", ods[o], self.oT[512 + cc * 128:512 + (cc + 1) * 128, g * 512:(g + 1) * 512], ob[o][:, :],
                           reads=[obb[o]], writes=[self.b_oT])
        fw.barrier()

    def pool(self, l):
        nc, fw = self.nc, self.fw
        BK = 2048
        with contextlib.ExitStack() as st:
            pw32 = self.sb(st, "ppw32", [128, 4, 128], F32)
            pw = self.sb(st, "ppw", [128, 4, 128], BF16)
            pwb = Buf()
            scol = self.sb(st, "pscol", [128, 4], F32)
            inv = self.sb(st, "pinv", [128, 4, 16], F32)
            smb = Buf()
            ds = fw.dsem()
            fw.dma("sp", ds, pw32[:, :, :], self.p["pool_w"][l, :, :, :].rearrange("g c j -> c g j"), writes=[pwb])
            fw.dma("sp", ds, scol[:, :], self.p["pool_scale"][l, :].rearrange("(k p) -> p k", p=128), writes=[smb], allow_slow_non_contiguous=True)
            fw.dma("sp", ds, inv[:, :, :].rearrange("p a b -> p (a b)"), self.p["invcnt"][:, :].rearrange("a b -> (a b)").partition_broadcast(128), writes=[smb])
            fw.op("dve", lambda e: e.tensor_copy(out=pw[:, :, :], in_=pw32[:, :, :]), reads=[pwb], writes=[pwb])
            bufA = [self.sb(st, "pA%d" % i, [128, 16 + BK], F32) for i in range(2)]
            bufB = [self.sb(st, "pB%d" % i, [128, 16 + BK], F32) for i in range(2)]
            bufC = [self.sb(st, "pC%d" % i, [128, 16 + BK], F32) for i in range(2)]
            Ab = [Buf() for _ in range(2)]
            Bb = [Buf() for _ in range(2)]
            Cb = [Buf() for _ in range(2)]
            lds = [fw.dsem() for _ in range(2)]
            mx = [self.sb(st, "pmx%d" % i, [128, BK], BF16) for i in range(2)]
            mxb = [Buf() for _ in range(2)]
            ob = [self.sb(st, "pob%d" % i, [128, 512], BF16) for i in range(4)]
            obb = [Buf() for _ in range(4)]
            ods = [fw.dsem() for _ in range(4)]
            n = 0
            ko = 0
            for gi in range(4):
                w = 2 << gi
                row0 = C_PZ + gi * 128
                for bk in range(T // BK):
                    s = n % 2
                    n += 1
                    A, B, C = bufA[s], bufB[s], bufC[s]
                    fw.dma("sp", lds[s], A[:, 16:16 + BK], self.zT[row0:row0 + 128, bk * BK:(bk + 1) * BK], reads=[self.b_zT], writes=[Ab[s]])
                    if bk == 0:
                        fw.op("pool", lambda e, A=A: e.memset(A[:, 0:16], 0.0), writes=[Ab[s]])
                    else:
                        fw.dma("sp", lds[s], A[:, 0:16], self.zT[row0:row0 + 128, bk * BK - 16:bk * BK], reads=[self.b_zT], writes=[Ab[s]])
                    src, srcb = A, Ab[s]
                    dsts = [(B, Bb[s]), (C, Cb[s])]
                    d = 1
                    k = 0
                    while d < w:
                        dst, dstb = dsts[k % 2]
                        k += 1
                        fw.op("dve", lambda e, dst=dst, src=src, d=d: e.tensor_tensor(out=dst[:, d:16 + BK], in0=src[:, d:16 + BK], in1=src[:, 0:16 + BK - d], op=ALU.add),
                              reads=[srcb], writes=[dstb])
                        src, srcb = dst, dstb
                        d *= 2
                    fw.op("dve", lambda e, src=src, A=A, s=s, w=w: e.scalar_tensor_tensor(out=mx[s][:, :], in0=src[:, 16:16 + BK], scalar=1.0 / w, in1=A[:, 16:16 + BK],
                                                                                          op0=ALU.mult, op1=ALU.subtract), reads=[srcb, Ab[s]], writes=[mxb[s]])
                    if bk == 0:
                        fw.op("dve", lambda e, src=src, gi=gi: e.tensor_tensor(out=src[:, 16:32], in0=src[:, 16:32], in1=inv[:, gi, :], op=ALU.mult),
                              reads=[srcb, smb], writes=[srcb])
                        fw.op("dve", lambda e, src=src, A=A, s=s: e.tensor_tensor(out=mx[s][:, 0:16], in0=src[:, 16:32], in1=A[:, 16:32], op=ALU.subtract),
                              reads=[srcb, Ab[s]], writes=[mxb[s]])
                    for tg in range(BK // 512):
                        ps, psb = self.pst()
                        fw.op("pe", lambda e, ps=ps, gi=gi, s=s, tg=tg: e.matmul(ps[:, :], lhsT=pw[:, gi, :], rhs=mx[s][:, tg * 512:(tg + 1) * 512], start=True, stop=True),
                              reads=[pwb, mxb[s]], writes=[psb])
                        o = ko % 4
                        ko += 1
                        fw.op("act", lambda e, ps=ps, o=o, gi=gi: e.activation(out=ob[o][:, :], in_=ps[:, :], func=AF.Copy, scale=scol[:, gi:gi + 1]),
                              reads=[psb, smb], writes=[obb[o]])
                        t0 = bk * BK + tg * 512
                        fw.dma("pool", ods[o], self.oT[1536 + gi * 128:1536 + (gi + 1) * 128, t0:t0 + 512], ob[o][:, :], reads=[obb[o]], writes=[self.b_oT])
        fw.barrier()

    def gla(self, l):
        nc, fw = self.nc, self.fw
        BK = 2048
        NTB = BK // 128
        with contextlib.ExitStack() as st:
            wa = self.sb(st, "gwa", [16, 256], F32)
            nba = self.sb(st, "gnba", [128, 2], F32)
            og = self.sb(st, "gog", [128, 4], F32)
            smb = Buf()
            ds = fw.dsem()
            fw.dma("sp", ds, wa[:, :], self.p["gla_wa"][l, :, :], writes=[smb])
            fw.dma("sp", ds, nba[:, :], self.p["gla_ba"][l, :].rearrange("(k p) -> p k", p=128), writes=[smb], allow_slow_non_contiguous=True)
            fw.dma("sp", ds, og[:, :], self.p["gla_og"][l, :].rearrange("(k p) -> p k", p=128), writes=[smb], allow_slow_non_contiguous=True)
            fw.op("dve", lambda e: e.tensor_scalar(out=nba[:, :], in0=nba[:, :], scalar1=-1.0, scalar2=None, op0=ALU.mult), reads=[smb], writes=[smb])
            alow = self.sb(st, "galow", [16, BK], F32)
            alowb = Buf()
            la = self.sb(st, "gla", [128, BK], F32)
            bp = self.sb(st, "gbp", [128, BK], F32)
            et = self.sb(st, "get", [128, BK], F32)
            qr = self.sb(st, "gqr", [128, BK], F32)
            kr = self.sb(st, "gkr", [128, BK], F32)
            lab, bpb, etb, qrb, krb = Buf(), Buf(), Buf(), Buf(), Buf()
            nbl = self.sb(st, "gnbl", [128, NTB], F32)
            dec = self.sb(st, "gdec", [128, NTB], F32)
            nblb = Buf()
            qt = self.sb(st, "gqt", [128, BK], BF16)
            qtb = Buf()
            kt = [self.sb(st, "gkt%d" % e, [128, BK], BF16) for e in range(2)]
            ktb = [Buf() for _ in range(2)]
            kd = self.sb(st, "gkd", [128, BK], BF16)
            kdb = Buf()
            kdt = self.sb(st, "gkdt", [128, NTB, 128], BF16)
            kdtb = Buf()
            Vt = self.sb(st, "gV", [128, NTB, 256], BF16)
            Vtb = Buf()
            KV = self.sb(st, "gKV", [128, NTB, 128], F32)
            KVb = Buf()
            S = self.sb(st, "gS", [128, 128], F32)
            Sb = Buf()
            Sbf = [self.sb(st, "gSbf%d" % e, [128, NTB, 128], BF16) for e in range(2)]
            Sbfb = [Buf() for _ in range(2)]
            Asb = [self.sb(st, "gA%d" % i, [128, 128], BF16) for i in range(3)]
            Asbb = [Buf() for _ in range(3)]
            orw = [self.sb(st, "gor%d" % i, [128, 512], F32) for i in range(2)]
            orwb = [Buf() for _ in range(2)]
            sq = self.sb(st, "gsq", [128, 512], F32)
            sqb = Buf()
            rs = self.sb(st, "grs", [128, 512], F32)
            rsb = Buf()
            rg = [self.sb(st, "grg%d" % i, [128, 512], F32) for i in range(2)]
            rgb = [Buf() for _ in range(2)]
            rds = [fw.dsem() for _ in range(2)]
            ob = [self.sb(st, "gob%d" % i, [128, 512], BF16) for i in range(2)]
            obb = [Buf() for _ in range(2)]
            ods = [fw.dsem() for _ in range(2)]
            lds = fw.dsem()
            for e_ in range(2):
                fw.op("pool", lambda e, e_=e_: e.memset(kt[e_][:, :], 0.0), writes=[ktb[e_]])
                fw.op("pool", lambda e, e_=e_: e.memset(Sbf[e_][:, :, :], 0.0), writes=[Sbfb[e_]])
            ka = 0
            ko = 0
            krg = 0
            for j in range(2):
                fw.op("dve", lambda e: e.memset(S[:, :], 0.0), writes=[Sb])
                for bk in range(T // BK):
                    t0 = bk * BK
                    fw.dma("sp", lds, alow[:, :], self.zT[C_GA:C_GA + 16, t0:t0 + BK], reads=[self.b_zT], writes=[alowb])
                    fw.dma("sp", lds, qr[:, :], self.zT[C_GQ + j * 128:C_GQ + (j + 1) * 128, t0:t0 + BK], reads=[self.b_zT], writes=[qrb])
                    fw.dma("sp", lds, kr[:, :], self.zT[C_GK + j * 128:C_GK + (j + 1) * 128, t0:t0 + BK], reads=[self.b_zT], writes=[krb])
                    fw.dma("sp", lds, Vt[:, :, :], self.vtok[t0:t0 + BK, 512 + j * 256:512 + (j + 1) * 256].rearrange("(n p) c -> p n c", p=128),
                           reads=[self.b_vtok], writes=[Vtb])
                    for tg in range(BK // 512):
                        ps, psb = self.pst()
                        fw.op("pe", lambda e, ps=ps, tg=tg, j=j: e.matmul(ps[:, :], lhsT=wa[:, j * 128:(j + 1) * 128], rhs=alow[:, tg * 512:(tg + 1) * 512],
                                                                         start=True, stop=True), reads=[smb, alowb], writes=[psb])
                        fw.op("act", lambda e, ps=ps, tg=tg, j=j: e.activation(out=la[:, tg * 512:(tg + 1) * 512], in_=ps[:, :], func=AF.Exp, scale=-1.0,
                                                                              bias=nba[:, j:j + 1]), reads=[psb, smb], writes=[lab])
                    fw.op("act", lambda e: e.activation(out=la[:, :], in_=la[:, :], func=AF.Ln, bias=self.ones_f[:, 0:1]), reads=[lab, self.b_const], writes=[lab])
                    for n in range(NTB):
                        fw.op("dve", lambda e, n=n: e.tensor_tensor_scan(out=bp[:, n * 128:(n + 1) * 128], data0=self.ones_f[:, :], data1=la[:, n * 128:(n + 1) * 128],
                                                                         initial=0.0, op0=ALU.mult, op1=ALU.add), reads=[lab, self.b_const], writes=[bpb])
                    fw.op("dve", lambda e: e.tensor_scalar(out=nbl[:, :], in0=bp[:, :].rearrange("p (n t) -> p n t", t=128)[:, :, 127], scalar1=-1.0 / 16, scalar2=None,
                                                           op0=ALU.mult), reads=[bpb], writes=[nblb])
                    fw.op("act", lambda e: e.activation(out=dec[:, :], in_=nbl[:, :], func=AF.Exp), reads=[nblb], writes=[nblb])
                    fw.op("act", lambda e: e.activation(out=et[:, :], in_=bp[:, :], func=AF.Exp, scale=-1.0 / 16), reads=[bpb], writes=[etb])
                    fw.op("dve", lambda e: e.scalar_tensor_tensor(out=qt[:, :], in0=qr[:, :], scalar=0.125, in1=et[:, :], op0=ALU.mult, op1=ALU.mult),
                          reads=[qrb, etb], writes=[qtb])
                    fw.op("act", lambda e: e.activation(out=et[:, :], in_=bp[:, :], func=AF.Exp, scale=1.0 / 16), reads=[bpb, qtb], writes=[etb])
                    for e_ in range(2):
                        pr = slice(e_ * 64, (e_ + 1) * 64)
                        fw.op("dve", lambda e, e_=e_, pr=pr: e.tensor_tensor(out=kt[e_][pr, :], in0=kr[pr, :], in1=et[pr, :], op=ALU.mult),
                              reads=[krb, etb], writes=[ktb[e_]])
                    for n in range(NTB):
                        fw.op("act", lambda e, n=n: e.activation(out=et[:, n * 128:(n + 1) * 128], in_=bp[:, n * 128:(n + 1) * 128], func=AF.Exp, scale=1.0 / 16,
                                                                 bias=nbl[:, n:n + 1]), reads=[bpb, nblb, ktb[0], ktb[1]], writes=[etb])
                    fw.op("dve", lambda e: e.tensor_tensor(out=kd[:, :], in0=kr[:, :], in1=et[:, :], op=ALU.mult), reads=[krb, etb], writes=[kdb])
                    for n4 in range(NTB // 4):
                        ps, psb = self.pst()
                        psv = ps[:, 0:256].bitcast(BF16)
                        for i in range(4):
                            n = n4 * 4 + i
                            fw.op("pe", lambda e, psv=psv, i=i, n=n: e.transpose(out=psv[:, i * 128:(i + 1) * 128], in_=kd[:, n * 128:(n + 1) * 128],
                                                                                 identity=self.ident_b[:, :]), reads=[kdb, self.b_const], writes=[psb])
                        fw.op("dve", lambda e, psv=psv, n4=n4: e.tensor_copy(out=kdt[:, n4 * 4:(n4 + 1) * 4, :].rearrange("p a b -> p (a b)"), in_=psv[:, :]),
                              reads=[psb], writes=[kdtb])
                    for n in range(NTB):
                        ps, psb = self.pst()
                        for e_ in range(2):
                            fw.op("pe", lambda e, ps=ps, n=n, e_=e_: e.matmul(ps[:, e_ * 128:(e_ + 1) * 128], lhsT=kdt[:, n, :], rhs=Vt[:, n, e_ * 128:(e_ + 1) * 128],
                                                                             start=True, stop=True), reads=[kdtb, Vtb], writes=[psb])
                        for e_ in range(2):
                            pr = slice(e_ * 64, (e_ + 1) * 64)
                            fw.op("act", lambda e, ps=ps, n=n, e_=e_, pr=pr: e.copy(out=KV[pr, n, :], in_=ps[pr, e_ * 128:(e_ + 1) * 128]),
                                  reads=[psb], writes=[KVb])
                    for n in range(NTB):
                        for e_ in range(2):
                            pr = slice(e_ * 64, (e_ + 1) * 64)
                            fw.op("pool", lambda e, n=n, e_=e_, pr=pr: e.tensor_copy(out=Sbf[e_][pr, n, :], in_=S[pr, :]), reads=[Sb], writes=[Sbfb[e_]])
                        fw.op("dve", lambda e, n=n: e.scalar_tensor_tensor(out=S[:, :], in0=S[:, :], scalar=dec[:, n:n + 1], in1=KV[:, n, :],
                                                                           op0=ALU.mult, op1=ALU.add), reads=[Sb, nblb, KVb, Sbfb[0], Sbfb[1]], writes=[Sb])
                    for e_ in range(2):
                        h = 2 * j + e_
                        for tg in range(BK // 512):
                            po, pob = self.ps[4 + (ko % 2)], self.psb[4 + (ko % 2)]
                            for i in range(4):
                                n = tg * 4 + i
                                tsl = slice(n * 128, (n + 1) * 128)
                                ps, psb = self.pst()
                                fw.op("pe", lambda e, ps=ps, e_=e_, tsl=tsl: e.matmul(ps[:, 0:128], lhsT=kt[e_][:, tsl], rhs=qt[:, tsl], start=True, stop=True),
                                      reads=[ktb[e_], qtb], writes=[psb])
                                a_ = ka % 3
                                ka += 1
                                fw.op("dve", lambda e, ps=ps, a_=a_: e.tensor_tensor(out=Asb[a_][:, :], in0=ps[:, 0:128], in1=self.U_b[:, :], op=ALU.mult),
                                      reads=[psb, self.b_const], writes=[Asbb[a_]])
                                fw.op("pe", lambda e, po=po, i=i, n=n, e_=e_, a_=a_: e.matmul(po[:, i * 128:(i + 1) * 128], lhsT=Vt[:, n, e_ * 128:(e_ + 1) * 128], rhs=Asb[a_][:, :],
                                                                                             start=True, stop=False), reads=[Vtb, Asbb[a_]], writes=[pob])
                                fw.op("pe", lambda e, po=po, i=i, n=n, e_=e_, tsl=tsl: e.matmul(po[:, i * 128:(i + 1) * 128], lhsT=Sbf[e_][:, n, :], rhs=qt[:, tsl],
                                                                                                start=False, stop=True), reads=[Sbfb[e_], qtb], writes=[pob])
                            o_ = ko % 2
                            ko += 1
                            fw.op("act", lambda e, po=po, o_=o_: e.copy(out=orw[o_][:, :], in_=po[:, :]), reads=[pob], writes=[orwb[o_]])
                            fw.op("act", lambda e, o_=o_: e.activation(out=sq[:, :], in_=orw[o_][:, :], func=AF.Square), reads=[orwb[o_]], writes=[sqb])
                            pn, pnb = self.pst()
                            fw.op("pe", lambda e, pn=pn: e.matmul(pn[:, :], lhsT=self.ones_f[:, :], rhs=sq[:, :], start=True, stop=True), reads=[sqb, self.b_const], writes=[pnb])
                            fw.op("act", lambda e, pn=pn: e.activation(out=rs[:, :], in_=pn[:, :], func=AF.Sqrt, scale=1.0 / 128, bias=self.eps_col[:, 0:1]),
                                  reads=[pnb, self.b_const], writes=[rsb])
                            fw.op("dve", lambda e: e.reciprocal(out=rs[:, :], in_=rs[:, :]), reads=[rsb], writes=[rsb])
                            g_ = krg % 2
                            krg += 1
                            tk = t0 + tg * 512
                            fw.dma("sp", rds[g_], rg[g_][:, :], self.zT[C_GR + h * 128:C_GR + (h + 1) * 128, tk:tk + 512], reads=[self.b_zT], writes=[rgb[g_]])
                            fw.op("dve", lambda e, o_=o_, h=h: e.scalar_tensor_tensor(out=orw[o_][:, :], in0=orw[o_][:, :], scalar=og[:, h:h + 1], in1=rs[:, :],
                                                                                      op0=ALU.mult, op1=ALU.mult), reads=[orwb[o_], smb, rsb], writes=[orwb[o_]])
                            fw.op("dve", lambda e, o_=o_, g_=g_: e.tensor_tensor(out=ob[o_][:, :], in0=orw[o_][:, :], in1=rg[g_][:, :], op=ALU.mult),
                                  reads=[orwb[o_], rgb[g_]], writes=[obb[o_]])
                            fw.dma("pool", ods[o_], self.oT[1024 + h * 128:1024 + (h + 1) * 128, tk:tk + 512], ob[o_][:, :], reads=[obb[o_]], writes=[self.b_oT])
        fw.barrier()

    def branch_out(self, l, sb):
        nc, fw = self.nc, self.fw
        NT = 1024
        t0 = sb * NT
        Wbr, Wo = self.W[l]["w_branch"], self.W[l]["w_out"]
        gview = self.gatesT.rearrange("(n d) t -> d n t", n=4)
        with contextlib.ExitStack() as st:
            oTb = self.sb(st, "boT", [128, 16, NT], BF16)
            oTbb = Buf()
            yT = self.sb(st, "byT", [128, 16, NT], BF16)
            yTb = Buf()
            wt = [self.sb(st, "bw%d" % i, [128, 16, 512], BF16) for i in range(2)]
            wtb = [Buf() for _ in range(2)]
            wds = [fw.dsem() for _ in range(2)]
            gt = [self.sb(st, "bg%d" % i, [128, 4, 512], BF16) for i in range(2)]
            gtb = [Buf() for _ in range(2)]
            gds = [fw.dsem() for _ in range(2)]
            m = [self.sb(st, "bm%d" % i, [128, 512], F32) for i in range(4)]
            mb = [Buf() for _ in range(4)]
            xc = [self.sb(st, "bx%d" % i, [128, 512], F32) for i in range(3)]
            xcb = [Buf() for _ in range(3)]
            xds = [fw.dsem() for _ in range(3)]
            ds = fw.dsem()
            fw.dma("sp", ds, oTb[:, :, :], self.oT[:, t0:t0 + NT].rearrange("(k p) t -> p k t", p=128), reads=[self.b_oT], writes=[oTbb])
            kw = 0
            kg = 0
            kp = 0
            for db in range(4):
                s = kw % 2
                kw += 1
                self.load_w(Wbr, self.Wb[l], db * 512, 512, wt[s], wtb[s], wds[s])
                for sc in range(4):
                    dch = db * 4 + sc
                    for tg in range(NT // 512):
                        tk = t0 + tg * 512
                        g_ = kg % 2
                        kg += 1
                        fw.dma("sp", gds[g_], gt[g_][:, :, :], gview[dch * 128:(dch + 1) * 128, :, tk:tk + 512], reads=[self.b_gates], writes=[gtb[g_]])
                        for n in range(4):
                            pi = kp % 8
                            kp += 1
                            ps, psb = self.ps[pi], self.psb[pi]
                            for cc in range(4):
                                fw.op("pe", lambda e, ps=ps, n=n, cc=cc, s=s, sc=sc, tg=tg: e.matmul(
                                    ps[:, :], lhsT=wt[s][:, n * 4 + cc, sc * 128:(sc + 1) * 128], rhs=oTb[:, n * 4 + cc, tg * 512:(tg + 1) * 512],
                                    start=(cc == 0), stop=(cc == 3)), reads=[wtb[s], oTbb], writes=[psb])
                            fw.op("dve", lambda e, ps=ps, n=n, g_=g_: e.tensor_tensor(out=m[n][:, :], in0=ps[:, :], in1=gt[g_][:, n, :], op=ALU.mult),
                                  reads=[psb, gtb[g_]], writes=[mb[n]])
                        fw.op("pool", lambda e: e.tensor_tensor(out=m[0][:, :], in0=m[0][:, :], in1=m[1][:, :], op=ALU.add), reads=[mb[0], mb[1]], writes=[mb[0]])
                        fw.op("pool", lambda e: e.tensor_tensor(out=m[2][:, :], in0=m[2][:, :], in1=m[3][:, :], op=ALU.add), reads=[mb[2], mb[3]], writes=[mb[2]])
                        fw.op("pool", lambda e, dch=dch, tg=tg: e.tensor_tensor(out=yT[:, dch, tg * 512:(tg + 1) * 512], in0=m[0][:, :], in1=m[2][:, :], op=ALU.add),
                              reads=[mb[0], mb[2]], writes=[yTb])
            kx = 0
            for db in range(4):
                s = kw % 2
                kw += 1
                self.load_w(Wo, self.Wb[l], db * 512, 512, wt[s], wtb[s], wds[s])
                for sc in range(4):
                    dch = db * 4 + sc
                    for tg in range(NT // 512):
                        tk = t0 + tg * 512
                        x_ = kx % 3
                        kx += 1
                        xb_ = Buf()
                        fw.dma("sp", xds[x_], xc[x_][:, :], self.xT[dch * 128:(dch + 1) * 128, tk:tk + 512], reads=[xb_], writes=[xcb[x_]])
                        pi = kp % 8
                        kp += 1
                        ps, psb = self.ps[pi], self.psb[pi]
                        for dc in range(16):
                            fw.op("pe", lambda e, ps=ps, dc=dc, s=s, sc=sc, tg=tg: e.matmul(
                                ps[:, :], lhsT=wt[s][:, dc, sc * 128:(sc + 1) * 128], rhs=yT[:, dc, tg * 512:(tg + 1) * 512],
                                start=(dc == 0), stop=(dc == 15)), reads=[wtb[s], yTb], writes=[psb])
                        fw.op("dve", lambda e, ps=ps, x_=x_: e.tensor_tensor(out=xc[x_][:, :], in0=ps[:, :], in1=xc[x_][:, :], op=ALU.add),
                              reads=[psb, xcb[x_]], writes=[xcb[x_]])
                        fw.dma("pool", xds[x_], self.xT[dch * 128:(dch + 1) * 128, tk:tk + 512], xc[x_][:, :], reads=[xcb[x_]], writes=[xb_])
        fw.barrier()

    def ffn(self, l):
        nc, fw = self.nc, self.fw
        NT = 1024
        Wup, Wdn = self.W[l]["ffn_up"], self.W[l]["ffn_down"]
        NCH = 2 * DFF // 128
        NP_ = NCH // 2
        with contextlib.ExitStack() as st:
            hT = self.sb(st, "fhT", [128, 16, 2 + NT], BF16)
            hTb = Buf()
            hh = self.sb(st, "fhh", [128, 16, 2], BF16)
            hhb = Buf()
            dwc = self.sb(st, "fdwc", [128, 3, NCH], F32)
            dbc = self.sb(st, "fdbc", [128, NCH], F32)
            cb = Buf()
            ds = fw.dsem()
            for k in range(3):
                fw.dma("sp", ds, dwc[:, k, :], self.p["ffn_dw"][l, k, :].rearrange("(c p) -> p c", p=128), writes=[cb], allow_slow_non_contiguous=True)
            fw.dma("sp", ds, dbc[:, :], self.p["ffn_db"][l, :].rearrange("(c p) -> p c", p=128), writes=[cb], allow_slow_non_contiguous=True)
            fw.op("pool", lambda e: e.memset(hh[:, :, :], 0.0), writes=[hhb])
            for sg in range(T // NT):
                t0 = sg * NT
                fw.op("pool", lambda e: e.tensor_copy(out=hT[:, :, 0:2], in_=hh[:, :, :]), reads=[hhb], writes=[hTb])
                self.rmsnorm_T("norm2_g", l, t0, NT, hT, hTb, 2)
                fw.op("pool", lambda e: e.tensor_copy(out=hh[:, :, :], in_=hT[:, :, NT:NT + 2]), reads=[hTb], writes=[hhb])
                with contextlib.ExitStack() as s2:
                    gT = self.sb(s2, "fgT", [128, NP_, 512], BF16)
                    gTb = Buf()
                    wu = [self.sb(s2, "fwu%d" % i, [128, 16, 512], BF16) for i in range(2)]
                    wub = [Buf() for _ in range(2)]
                    uds = [fw.dsem() for _ in range(2)]
                    wd = [self.sb(s2, "fwd%d" % i, [128, NP_, 256], BF16) for i in range(2)]
                    wdb = [Buf() for _ in range(2)]
                    dds = [fw.dsem() for _ in range(2)]
                    ub = [self.sb(s2, "fub%d" % i, [128, 2 + 512], F32) for i in range(4)]
                    ubb = [Buf() for _ in range(4)]
                    ac = [self.sb(s2, "fac%d" % i, [128, 512], F32) for i in range(4)]
                    acb = [Buf() for _ in range(4)]
                    xc = [self.sb(s2, "fx%d" % i, [128, 512], F32) for i in range(3)]
                    xcb = [Buf() for _ in range(3)]
                    xds = [fw.dsem() for _ in range(3)]
                    kb = 0
                    ku = 0
                    kd = 0
                    kx = 0
                    for g in range(NT // 512):
                        tk = t0 + g * 512
                        for cp in range(NP_):
                            if cp % 2 == 0:
                                s = kb % 2
                                kb += 1
                                fw.dma("sp", uds[s], wu[s][:, :, 0:256], Wup[:, cp * 128:cp * 128 + 256].rearrange("(k p) c -> p k c", p=128),
                                       reads=[self.Wb[l]], writes=[wub[s]])
                                fw.dma("sp", uds[s], wu[s][:, :, 256:512], Wup[:, DFF + cp * 128:DFF + cp * 128 + 256].rearrange("(k p) c -> p k c", p=128),
                                       reads=[self.Wb[l]], writes=[wub[s]])
                            res = []
                            for which in range(2):
                                ch = cp + which * NP_
                                wo = which * 256 + (cp % 2) * 128
                                ps, psb = self.pst()
                                for dc in range(16):
                                    fw.op("pe", lambda e, ps=ps, dc=dc, s=s, wo=wo, g=g: e.matmul(
                                        ps[:, :], lhsT=wu[s][:, dc, wo:wo + 128], rhs=hT[:, dc, 2 + g * 512:2 + (g + 1) * 512],
                                        start=(dc == 0), stop=(dc == 15)), reads=[wub[s], hTb], writes=[psb])
                                ph, phb = self.ps[4 + (ku % 4)], self.psb[4 + (ku % 4)]
                                for dc in range(16):
                                    fw.op("pe", lambda e, ph=ph, dc=dc, s=s, wo=wo, g=g: e.matmul(
                                        ph[:, 0:2], lhsT=wu[s][:, dc, wo:wo + 128], rhs=hT[:, dc, g * 512:g * 512 + 2],
                                        start=(dc == 0), stop=(dc == 15)), reads=[wub[s], hTb], writes=[phb])
                                u_ = ku % 4
                                ku += 1
                                fw.op("act", lambda e, ps=ps, u_=u_: e.copy(out=ub[u_][:, 2:514], in_=ps[:, :]), reads=[psb], writes=[ubb[u_]])
                                fw.op("act", lambda e, ph=ph, u_=u_: e.copy(out=ub[u_][:, 0:2], in_=ph[:, 0:2]), reads=[phb], writes=[ubb[u_]])
                                fw.op("dve", lambda e, u_=u_, ch=ch: e.tensor_scalar(out=ac[u_][:, :], in0=ub[u_][:, 2:514], scalar1=dwc[:, 2, ch:ch + 1],
                                                                                    scalar2=dbc[:, ch:ch + 1], op0=ALU.mult, op1=ALU.add),
                                      reads=[ubb[u_], cb], writes=[acb[u_]])
                                fw.op("dve", lambda e, u_=u_, ch=ch: e.scalar_tensor_tensor(out=ac[u_][:, :], in0=ub[u_][:, 1:513], scalar=dwc[:, 1, ch:ch + 1],
                                                                                           in1=ac[u_][:, :], op0=ALU.mult, op1=ALU.add),
                                      reads=[ubb[u_], cb, acb[u_]], writes=[acb[u_]])
                                fw.op("dve", lambda e, u_=u_, ch=ch: e.scalar_tensor_tensor(out=ac[u_][:, :], in0=ub[u_][:, 0:512], scalar=dwc[:, 0, ch:ch + 1],
                                                                                           in1=ac[u_][:, :], op0=ALU.mult, op1=ALU.add),
                                      reads=[ubb[u_], cb, acb[u_]], writes=[acb[u_]])
                                res.append(u_)
                            ua, uv = res
                            fw.op("act", lambda e, ua=ua: e.activation(out=ac[ua][:, :], in_=ac[ua][:, :], func=AF.Silu), reads=[acb[ua]], writes=[acb[ua]])
                            fw.op("pool", lambda e, ua=ua, uv=uv, cp=cp: e.tensor_tensor(out=gT[:, cp, :], in0=ac[ua][:, :], in1=ac[uv][:, :], op=ALU.mult),
                                  reads=[acb[ua], acb[uv]], writes=[gTb])
                        for db8 in range(D // 256):
                            s = kd % 2
                            kd += 1
                            fw.dma("sp", dds[s], wd[s][:, :, :], Wdn[:, db8 * 256:(db8 + 1) * 256].rearrange("(k p) c -> p k c", p=128),
                                   reads=[self.Wb[l]], writes=[wdb[s]])
                            for sc in range(2):
                                dch = db8 * 2 + sc
                                x_ = kx % 3
                                kx += 1
                                xb_ = Buf()
                                fw.dma("sp", xds[x_], xc[x_][:, :], self.xT[dch * 128:(dch + 1) * 128, tk:tk + 512], reads=[xb_], writes=[xcb[x_]])
                                ps, psb = self.pst()
                                for fc in range(NP_):
                                    fw.op("pe", lambda e, ps=ps, fc=fc, s=s, sc=sc: e.matmul(
                                        ps[:, :], lhsT=wd[s][:, fc, sc * 128:(sc + 1) * 128], rhs=gT[:, fc, :],
                                        start=(fc == 0), stop=(fc == NP_ - 1)), reads=[wdb[s], gTb], writes=[psb])
                                fw.op("dve", lambda e, ps=ps, x_=x_: e.tensor_tensor(out=xc[x_][:, :], in0=ps[:, :], in1=xc[x_][:, :], op=ALU.add),
                                      reads=[psb, xcb[x_]], writes=[xcb[x_]])
                                fw.dma("pool", xds[x_], self.xT[dch * 128:(dch + 1) * 128, tk:tk + 512], xc[x_][:, :], reads=[xcb[x_]], writes=[xb_])
                fw.barrier()
        fw.barrier()


def _build_inputs(inputs, nl, ncores):
    ins = []
    x = np.asarray(inputs["x"], dtype=np.float32)
    inv = np.zeros((4, 16), np.float32)
    for gi, w in enumerate((2, 4, 8, 16)):
        inv[gi] = 1.0 / np.minimum(np.arange(16) + 1, w)
    shared = {}
    for name in ("w_in", "w_out", "ffn_up", "ffn_down", "norm1_g", "fox_fb", "fox_qg", "fox_kg", "conv_dw", "conv_db",
                 "conv_ln_g", "conv_ln_b", "gla_wa", "gla_ba", "gla_og", "pool_w", "pool_scale", "gate_b", "norm2_g",
                 "ffn_dw", "ffn_db"):
        shared[name] = np.ascontiguousarray(np.asarray(inputs[name][:nl], dtype=np.float32))
    shared["w_branch"] = np.ascontiguousarray(np.asarray(inputs["w_branch"][:nl], dtype=np.float32).reshape(nl, 4 * 512, D))
    shared["invcnt"] = inv
    per = ncores // 2
    for c in range(ncores):
        m = dict(shared)
        m["x"] = np.ascontiguousarray(x[c // per])
        ins.append(m)
    return ins


def _run(inputs, nl=DEPTH, debug=(), trace=False, ncores=NCORES, stop=None):
    k = Kern(nl, debug, stop)
    nc = k.build()
    ins = _build_inputs(inputs, nl, ncores)
    res = run_bass_kernel_spmd(nc, ins, core_ids=list(range(ncores)), **({"trace": True} if trace else {}))
    return res


def kernel(**inputs):
    res = _run(inputs)
    per = NCORES // 2
    return np.stack([np.asarray(res.results[0]["y"]), np.asarray(res.results[per]["y"])], axis=0).astype(np.float32)
```

```python
import contextlib
import numpy as np
import concourse.bass as bass
import concourse.mybir as mybir
from concourse.bass_utils import run_bass_kernel_spmd

F32 = mybir.dt.float32
BF16 = mybir.dt.bfloat16
ALU = mybir.AluOpType
AF = mybir.ActivationFunctionType

NCORES = 2
DEPTH = 4
D = 2048
T = 8192
SG = 2048
NSG = T // SG
D_IN = 12824
DFF = 5632
EPS = 1e-6
NEG = -30000.0

C_FQ, C_FK, C_FV, C_FF, C_CA, C_CG, C_GQ, C_GK, C_GV, C_GA, C_GR, C_PZ, C_GT = (
    0, 512, 1024, 1536, 1544, 2056, 2568, 2824, 3080, 3592, 3608, 4120, 4632)
ZROWS = 4632


class DSem:
    __slots__ = ("sem", "count", "key")

    def __init__(self, sem, key):
        self.sem = sem
        self.count = 0
        self.key = key


class Buf:
    __slots__ = ("name", "w", "r")

    def __init__(self, name=""):
        self.name = name
        self.w = {}
        self.r = {}


class Stream:
    def __init__(self, fw, key, eng, sem):
        self.fw = fw
        self.key = key
        self.eng = eng
        self.sem = sem
        self.cnt = 0
        self.seen = {}

    def need(self, deps):
        for (k, sem, val) in deps:
            if k == self.key and self.key == "pe":
                continue
            d = self.fw.dkey.get(k)
            if d is not None:
                val = max(val, d.count * 16)
            if val > self.seen.get(k, 0):
                self.seen[k] = val
                self.eng.wait_ge(sem, val)


class FW:
    def __init__(self, nc, esems, dsems):
        self.nc = nc
        self.s = {}
        for key, eng in (("pe", nc.tensor), ("act", nc.scalar), ("dve", nc.vector),
                         ("pool", nc.gpsimd), ("sp", nc.sync)):
            self.s[key] = Stream(self, key, eng, esems[key])
        self.dsems = [DSem(s, "d%d" % i) for i, s in enumerate(dsems)]
        self.dkey = {d.key: d for d in self.dsems}
        self.dnext = 0

    def dsem(self):
        d = self.dsems[self.dnext % len(self.dsems)]
        self.dnext += 1
        return d

    @staticmethod
    def _deps(reads, writes):
        deps = []
        for b in reads:
            deps.extend(b.w.values())
        for b in writes:
            deps.extend(b.w.values())
            deps.extend(b.r.values())
        return deps

    @staticmethod
    def _mark(tok, reads, writes):
        for b in reads:
            b.r[tok[0]] = tok
        for b in writes:
            b.w[tok[0]] = tok
            b.r = {}

    def op(self, key, fn, reads=(), writes=()):
        st = self.s[key]
        st.need(self._deps(reads, writes))
        inst = fn(st.eng)
        st.cnt += 1
        inst.then_inc(st.sem, 1)
        self._mark((key, st.sem, st.cnt), reads, writes)
        return inst

    def dma(self, qkey, ds, out, in_, reads=(), writes=(), **kw):
        st = self.s[qkey]
        deps = self._deps(reads, writes)
        st.need(deps)
        inst = st.eng.dma_start(out=out, in_=in_, **kw)
        ds.count += 1
        inst.then_inc(ds.sem, 16)
        self._mark((ds.key, ds.sem, ds.count * 16), reads, writes)
        return inst

    def barrier(self):
        toks = []
        for k, st in self.s.items():
            if st.cnt:
                toks.append((k, st.sem, st.cnt))
        for d in self.dsems:
            if d.count:
                toks.append((d.key, d.sem, d.count * 16))
        for k, st in self.s.items():
            st.need(toks)
        self.dnext = 0


def _slices(total, step):
    return [(i, min(step, total - i)) for i in range(0, total, step)]


class Kern:
    def __init__(self, nl=DEPTH, debug=(), stop=None):
        self.NL = nl
        self.debug = set(debug)
        self.stop = stop

    def dram(self, name, shape, dt, kind=None):
        if kind is None and name in self.debug:
            kind = "ExternalOutput"
        if kind:
            return self.nc.dram_tensor(name, list(shape), dt, kind=kind)
        return self.nc.dram_tensor(name, list(shape), dt)

    def sb(self, st, name, shape, dt):
        self.uid += 1
        return st.enter_context(self.nc.sbuf_tensor("%s_%d" % (name, self.uid), list(shape), dt))

    def psum_tiles(self, st):
        self.ps = [st.enter_context(self.nc.psum_tensor("ps%d" % i, [128, 512], F32)) for i in range(8)]
        self.psb = [Buf("ps%d" % i) for i in range(8)]
        self.psn = 0

    def pst(self):
        i = self.psn % 4
        self.psn += 1
        return self.ps[i], self.psb[i]

    def dump(self, name, ap, shape, dt, buf):
        if ("dump_" + name) not in self.debug:
            return
        t = self.nc.dram_tensor("dump_" + name, list(shape), dt, kind="ExternalOutput")
        self.fw.dma("sp", self.fw.dsem(), t[tuple(slice(None) for _ in shape)], ap, reads=[buf], writes=[Buf()])

    def load_w(self, W, wb, c0, n, wt, wtb, ds, q="sp"):
        src = W[:, c0:c0 + n].rearrange("(k p) c -> p k c", p=128)
        self.fw.dma(q, ds, wt[:, :, 0:n], src, reads=[wb], writes=[wtb])

    def load_cols(self, dst, dstb, src_vec, n, ds=None):
        fw = self.fw
        ds = ds or fw.dsem()
        fw.dma("sp", ds, dst[:, 0:n], src_vec.rearrange("(k p) -> p k", p=128), writes=[dstb],
               allow_slow_non_contiguous=True)

    def build(self):
        nc = bass.Bass("TRN2", target_bir_lowering=False)
        self.nc = nc
        self.uid = 0
        NL = self.NL
        inp = lambda name, shape: nc.dram_tensor(name, list(shape), F32, kind="ExternalInput")
        self.x_in = inp("x", [T, D])
        self.p = {}
        for name, shape in (("w_in", [NL, D, D_IN]), ("w_branch", [NL, D, D]), ("w_out", [NL, D, D]),
                            ("ffn_up", [NL, D, 2 * DFF]), ("ffn_down", [NL, DFF, D]),
                            ("norm1_g", [NL, D]), ("fox_fb", [NL, 8]), ("fox_qg", [NL, 64]), ("fox_kg", [NL, 64]),
                            ("conv_dw", [NL, 31, 512]), ("conv_db", [NL, 512]), ("conv_ln_g", [NL, 512]),
                            ("conv_ln_b", [NL, 512]), ("gla_wa", [NL, 16, 256]), ("gla_ba", [NL, 256]),
                            ("gla_og", [NL, 512]), ("pool_w", [NL, 4, 128, 128]), ("pool_scale", [NL, 512]),
                            ("gate_b", [NL, 8192]), ("norm2_g", [NL, D]), ("ffn_dw", [NL, 3, 2 * DFF]),
                            ("ffn_db", [NL, 2 * DFF]), ("invcnt", [4, 16])):
            self.p[name] = inp(name, shape)
        self.y_out = nc.dram_tensor("y", [T, D], F32, kind="ExternalOutput")

        self.W = []
        for l in range(NL):
            self.W.append({
                "w_in": self.dram("b_w_in%d" % l, [D, D_IN], BF16), "w_branch": self.dram("b_w_br%d" % l, [D, D], BF16),
                "w_out": self.dram("b_w_out%d" % l, [D, D], BF16), "ffn_up": self.dram("b_f_up%d" % l, [D, 2 * DFF], BF16),
                "ffn_down": self.dram("b_f_dn%d" % l, [DFF, D], BF16)})
        self.Wb = [Buf("W%d" % l) for l in range(NL)]
        self.xT = self.dram("xT", [D, T], F32)
        self.zT = self.dram("zT", [ZROWS, T], F32)
        self.vtok = self.dram("vtok", [T, 1024], BF16)
        self.gatesT = self.dram("gatesT", [8192, T], BF16)
        self.oT = self.dram("oT", [D, T], BF16)
        self.rrow = self.dram("rrow", [8, T], BF16)
        self.b_xT, self.b_zT, self.b_vtok, self.b_gates, self.b_oT, self.b_rrow, self.b_y = (
            Buf("xT"), Buf("zT"), Buf("vtok"), Buf("gatesT"), Buf("oT"), Buf("rrow"), Buf("y"))

        with contextlib.ExitStack() as st:
            esems = {k: st.enter_context(nc.semaphore("s_" + k)) for k in ("pe", "act", "dve", "pool", "sp")}
            dsems = [st.enter_context(nc.semaphore("d%d" % i)) for i in range(40)]
            self.fw = FW(nc, esems, dsems)
            fw = self.fw
            self.csem = [DSem(st.enter_context(nc.semaphore("wc%d" % l)), "wc%d" % l) for l in range(NL)]
            for d in self.csem:
                fw.dkey[d.key] = d
            self.psum_tiles(st)
            self.consts(st)
            fw.barrier()
            self.weights_cast(0)
            self.x_to_xT()
            for l in range(NL):
                self.layer(l)
            self.xT_to_y()
            fw.barrier()
        return nc

    def consts(self, st):
        nc, fw = self.nc, self.fw
        self.ident_f = self.sb(st, "ident_f", [128, 128], F32)
        self.ident_b = self.sb(st, "ident_b", [128, 128], BF16)
        self.ones_f = self.sb(st, "ones_f", [128, 128], F32)
        self.ones_b = self.sb(st, "ones_b", [128, 128], BF16)
        self.bo64 = self.sb(st, "bo64", [128, 128], F32)
        self.sel127 = self.sb(st, "sel127", [128, 128], F32)
        self.U_b = self.sb(st, "U_b", [128, 128], BF16)
        self.maskneg = self.sb(st, "maskneg", [128, 128], BF16)
        self.eps_col = self.sb(st, "eps_col", [128, 1], F32)
        self.zero_col = self.sb(st, "zero_col", [128, 1], F32)
        self.ones_big = self.sb(st, "ones_big", [128, 2048], F32)
        self.b_const = Buf("const")
        cb = [self.b_const]
        fw.op("pool", lambda e: e.memset(self.ones_f[:, :], 1.0), writes=cb)
        fw.op("pool", lambda e: e.memset(self.ones_b[:, :], 1.0), writes=cb)
        fw.op("pool", lambda e: e.memset(self.eps_col[:, :], EPS), writes=cb)
        fw.op("pool", lambda e: e.memset(self.zero_col[:, :], 0.0), writes=cb)
        fw.op("pool", lambda e: e.memset(self.ones_big[:, :], 1.0), writes=cb)
        fw.op("pool", lambda e: e.memset(self.bo64[:, :], 0.0), writes=cb)
        fw.op("pool", lambda e: e.memset(self.bo64[0:64, 0:64], 1.0), writes=cb)
        fw.op("pool", lambda e: e.memset(self.bo64[64:128, 64:128], 1.0), writes=cb)
        sel = lambda out, in_, op, base, cm, pat: fw.op("pool", lambda e: e.affine_select(
            out=out, in_=in_, pattern=pat, compare_op=op, fill=0.0, base=base, channel_multiplier=cm), reads=cb, writes=cb)
        sel(self.ident_f[:, :], self.ones_f[:, :], ALU.is_equal, 0, -1, [[1, 128]])
        sel(self.ident_b[:, :], self.ones_b[:, :], ALU.is_equal, 0, -1, [[1, 128]])
        sel(self.U_b[:, :], self.ones_b[:, :], ALU.is_ge, 0, -1, [[1, 128]])
        sel(self.sel127[:, :], self.ones_f[:, :], ALU.is_equal, -127, 1, [[0, 128]])
        fw.op("dve", lambda e: e.tensor_scalar(out=self.maskneg[:, :], in0=self.U_b[:, :], scalar1=-1.0, scalar2=-NEG,
                                               op0=ALU.add, op1=ALU.mult), reads=cb, writes=cb)

    def weights_cast(self, l):
        fw = self.fw
        ds = self.csem[l]
        for name, rows in (("w_in", D), ("w_branch", D), ("w_out", D), ("ffn_up", D), ("ffn_down", DFF)):
            nsplit = 8
            rs = rows // nsplit
            for i in range(nsplit):
                fw.dma("pool", ds, self.W[l][name][i * rs:(i + 1) * rs, :], self.p[name][l, i * rs:(i + 1) * rs, :],
                       writes=[self.Wb[l]])

    def x_to_xT(self):
        nc, fw = self.nc, self.fw
        with contextlib.ExitStack() as st:
            xin = [self.sb(st, "xin%d" % i, [128, D], F32) for i in range(2)]
            xinb = [Buf() for _ in range(2)]
            xo = [self.sb(st, "xo%d" % i, [128, 4, 128], F32) for i in range(4)]
            xob = [Buf() for _ in range(4)]
            dsl = [fw.dsem() for _ in range(2)]
            dso = [fw.dsem() for _ in range(4)]
            k = 0
            for tt in range(T // 128):
                s = tt % 2
                fw.dma("sp", dsl[s], xin[s][:, :], self.x_in[tt * 128:(tt + 1) * 128, :], writes=[xinb[s]])
                for dg in range(4):
                    ps, psb = self.pst()
                    for j in range(4):
                        dc = dg * 4 + j
                        fw.op("pe", lambda e, ps=ps, j=j, dc=dc, s=s: e.transpose(out=ps[:, j * 128:(j + 1) * 128],
                              in_=xin[s][:, dc * 128:(dc + 1) * 128], identity=self.ident_f[:, :]),
                              reads=[xinb[s], self.b_const], writes=[psb])
                    o = k % 4
                    k += 1
                    if k % 2:
                        fw.op("dve", lambda e, ps=ps, o=o: e.tensor_copy(out=xo[o][:, :, :].rearrange("p a b -> p (a b)"), in_=ps[:, :]),
                              reads=[psb], writes=[xob[o]])
                    else:
                        fw.op("act", lambda e, ps=ps, o=o: e.copy(out=xo[o][:, :, :].rearrange("p a b -> p (a b)"), in_=ps[:, :]),
                              reads=[psb], writes=[xob[o]])
                    dst = self.xT[dg * 512:(dg + 1) * 512, tt * 128:(tt + 1) * 128].rearrange("(j p) t -> p j t", p=128)
                    fw.dma("pool", dso[o], dst, xo[o][:, :, :], reads=[xob[o]], writes=[self.b_xT])
        fw.barrier()

    def xT_to_y(self):
        nc, fw = self.nc, self.fw
        with contextlib.ExitStack() as st:
            xin = [self.sb(st, "yin%d" % i, [128, 4, 512], F32) for i in range(2)]
            xinb = [Buf() for _ in range(2)]
            xo = [self.sb(st, "yo%d" % i, [128, 512], F32) for i in range(4)]
            xob = [Buf() for _ in range(4)]
            dsl = [fw.dsem() for _ in range(2)]
            dso = [fw.dsem() for _ in range(4)]
            k = 0
            n = 0
            for dg in range(4):
                for tg in range(T // 512):
                    s = n % 2
                    n += 1
                    src = self.xT[dg * 512:(dg + 1) * 512, tg * 512:(tg + 1) * 512].rearrange("(j p) t -> p j t", p=128)
                    fw.dma("sp", dsl[s], xin[s][:, :, :], src, reads=[self.b_xT], writes=[xinb[s]])
                    for ti in range(4):
                        ps, psb = self.pst()
                        for j in range(4):
                            fw.op("pe", lambda e, ps=ps, j=j, ti=ti, s=s: e.transpose(out=ps[:, j * 128:(j + 1) * 128],
                                  in_=xin[s][:, j, ti * 128:(ti + 1) * 128], identity=self.ident_f[:, :]),
                                  reads=[xinb[s], self.b_const], writes=[psb])
                        o = k % 4
                        k += 1
                        if k % 2:
                            fw.op("dve", lambda e, ps=ps, o=o: e.tensor_copy(out=xo[o][:, :], in_=ps[:, :]), reads=[psb], writes=[xob[o]])
                        else:
                            fw.op("act", lambda e, ps=ps, o=o: e.copy(out=xo[o][:, :], in_=ps[:, :]), reads=[psb], writes=[xob[o]])
                        tok0 = tg * 512 + ti * 128
                        fw.dma("pool", dso[o], self.y_out[tok0:tok0 + 128, dg * 512:(dg + 1) * 512], xo[o][:, :],
                               reads=[xob[o]], writes=[self.b_y])

    def layer(self, l):
        fw = self.fw
        stop = self.stop
        for sg in range(NSG):
            with contextlib.ExitStack() as st:
                hT = self.sb(st, "hT", [128, 16, SG], BF16)
                hTb = Buf("hT")
                self.rmsnorm_T("norm1_g", l, sg * SG, SG, hT, hTb, 0)
                self.in_proj(l, sg, hT, hTb)
                fw.barrier()
        if stop == "in_proj":
            return
        if "skip_fox" not in self.debug:
            self.fox(l)
        if stop == "fox":
            return
        if "skip_conv" not in self.debug:
            self.conv(l)
        if stop == "conv":
            return
        if "skip_gla" not in self.debug:
            self.gla(l)
        if stop == "gla":
            return
        if "skip_pool" not in self.debug:
            self.pool(l)
        if stop == "pool":
            return
        for sb in range(T // 1024):
            self.branch_out(l, sb)
        if stop == "branch":
            return
        self.ffn(l)

    def rmsnorm_T(self, gname, l, tok0, ntok, hT, hTb, off):
        nc, fw = self.nc, self.fw
        with contextlib.ExitStack() as st:
            xg = [self.sb(st, "nx%d" % i, [128, 16, 512], F32) for i in range(2)]
            xgb = [Buf() for _ in range(2)]
            sq = [self.sb(st, "nsq%d" % i, [128, 512], F32) for i in range(2)]
            sqb = [Buf() for _ in range(2)]
            rs = self.sb(st, "nrs", [128, 512], F32)
            rsb = Buf()
            gcol = self.sb(st, "ngc", [128, 16], F32)
            gcb = Buf()
            dsl = [fw.dsem() for _ in range(2)]
            self.load_cols(gcol, gcb, self.p[gname][l, :], 16)
            for tg in range(ntok // 512):
                s = tg % 2
                t0 = tok0 + tg * 512
                for dq in range(4):
                    src = self.xT[dq * 512:(dq + 1) * 512, t0:t0 + 512].rearrange("(j p) t -> p j t", p=128)
                    fw.dma("sp", dsl[s], xg[s][:, dq * 4:(dq + 1) * 4, :], src, reads=[self.b_xT], writes=[xgb[s]])
                ps, psb = self.pst()
                for dc in range(16):
                    q = dc % 2
                    fw.op("act", lambda e, s=s, dc=dc, q=q: e.activation(out=sq[q][:, :], in_=xg[s][:, dc, :], func=AF.Square),
                          reads=[xgb[s]], writes=[sqb[q]])
                    fw.op("pe", lambda e, ps=ps, q=q, dc=dc: e.matmul(ps[:, :], lhsT=self.ones_f[:, :], rhs=sq[q][:, :],
                                                                     start=(dc == 0), stop=(dc == 15)),
                          reads=[sqb[q], self.b_const], writes=[psb])
                fw.op("act", lambda e, ps=ps: e.activation(out=rs[:, :], in_=ps[:, :], func=AF.Sqrt, scale=1.0 / D, bias=self.eps_col[:, 0:1]),
                      reads=[psb, self.b_const], writes=[rsb])
                fw.op("dve", lambda e: e.reciprocal(out=rs[:, :], in_=rs[:, :]), reads=[rsb], writes=[rsb])
                for dc in range(16):
                    fw.op("dve", lambda e, s=s, dc=dc, tg=tg: e.scalar_tensor_tensor(
                        out=hT[:, dc, off + tg * 512:off + (tg + 1) * 512], in0=xg[s][:, dc, :], scalar=gcol[:, dc:dc + 1], in1=rs[:, :],
                        op0=ALU.mult, op1=ALU.mult), reads=[xgb[s], rsb, gcb], writes=[hTb])
        fw.barrier()

    def in_proj(self, l, sg, hT, hTb):
        nc, fw = self.nc, self.fw
        T0 = sg * SG
        chunks = []

        def seg(c0, w, kind):
            for (o, n) in _slices(w, 128):
                chunks.append((c0 + o, n, kind))
        seg(C_FQ, 512, "raw"); seg(C_FK, 512, "raw"); chunks.append((C_FV, 512, "vtok0"))
        seg(C_FF, 8, "raw"); seg(C_CA, 512, "raw"); seg(C_CG, 512, "sigmoid")
        seg(C_GQ, 256, "raw"); seg(C_GK, 256, "raw"); chunks.append((C_GV, 512, "vtok1"))
        seg(C_GA, 16, "raw"); seg(C_GR, 512, "silu"); seg(C_PZ, 512, "raw"); seg(C_GT, 8192, "gate")
        blocks = []
        cur = []
        for ch in chunks:
            if cur and (ch[0] + ch[1] - cur[0][0] > 512):
                blocks.append(cur); cur = []
            cur.append(ch)
        if cur:
            blocks.append(cur)
        W = self.W[l]["w_in"]
        with contextlib.ExitStack() as st:
            wt = [self.sb(st, "ipw%d" % i, [128, 16, 512], BF16) for i in range(2)]
            wtb = [Buf() for _ in range(2)]
            wds = [fw.dsem() for _ in range(2)]
            stg = [self.sb(st, "ips%d" % i, [128, 512], F32) for i in range(4)]
            stgb = [Buf() for _ in range(4)]
            sds = [fw.dsem() for _ in range(4)]
            stgh = [self.sb(st, "iph%d" % i, [128, 512], BF16) for i in range(4)]
            stghb = [Buf() for _ in range(4)]
            hds = [fw.dsem() for _ in range(4)]
            gb = self.sb(st, "ipgb", [128, 64], F32)
            gbb = Buf()
            self.load_cols(gb, gbb, self.p["gate_b"][l, :], 64)
            k32 = 0
            k16 = 0
            for bi, blk in enumerate(blocks):
                s = bi % 2
                c0 = blk[0][0]
                ncol = blk[-1][0] + blk[-1][1] - c0
                self.load_w(W, self.Wb[l], c0, ncol, wt[s], wtb[s], wds[s])
                for (cc, n, kind) in blk:
                    o = cc - c0
                    if kind.startswith("vtok"):
                        vo = 0 if kind == "vtok0" else 512
                        for tt in range(SG // 128):
                            ps, psb = self.pst()
                            for dc in range(16):
                                fw.op("pe", lambda e, ps=ps, dc=dc, tt=tt, s=s, o=o: e.matmul(
                                    ps[:, :], lhsT=hT[:, dc, tt * 128:(tt + 1) * 128], rhs=wt[s][:, dc, o:o + 512],
                                    start=(dc == 0), stop=(dc == 15)), reads=[hTb, wtb[s]], writes=[psb])
                            q = k16 % 4
                            k16 += 1
                            if k16 % 2:
                                fw.op("dve", lambda e, ps=ps, q=q: e.tensor_copy(out=stgh[q][:, :], in_=ps[:, :]), reads=[psb], writes=[stghb[q]])
                            else:
                                fw.op("act", lambda e, ps=ps, q=q: e.copy(out=stgh[q][:, :], in_=ps[:, :]), reads=[psb], writes=[stghb[q]])
                            fw.dma("pool", hds[q], self.vtok[T0 + tt * 128:T0 + (tt + 1) * 128, vo:vo + 512], stgh[q][:, :],
                                   reads=[stghb[q]], writes=[self.b_vtok])
                        continue
                    for tg in range(SG // 512):
                        tk = T0 + tg * 512
                        ps, psb = self.pst()
                        for dc in range(16):
                            fw.op("pe", lambda e, ps=ps, dc=dc, tg=tg, s=s, o=o, n=n: e.matmul(
                                ps[0:n, :], lhsT=wt[s][:, dc, o:o + n], rhs=hT[:, dc, tg * 512:(tg + 1) * 512],
                                start=(dc == 0), stop=(dc == 15)), reads=[hTb, wtb[s]], writes=[psb])
                        if kind == "gate":
                            q = k16 % 4
                            k16 += 1
                            gi = (cc - C_GT) // 128
                            fw.op("act", lambda e, ps=ps, q=q, gi=gi: e.activation(out=stgh[q][:, :], in_=ps[:, :], func=AF.Sigmoid,
                                                                                   bias=gb[:, gi:gi + 1]),
                                  reads=[psb, gbb], writes=[stghb[q]])
                            fw.dma("pool", hds[q], self.gatesT[cc - C_GT:cc - C_GT + 128, tk:tk + 512], stgh[q][:, :],
                                   reads=[stghb[q]], writes=[self.b_gates])
                        else:
                            q = k32 % 4
                            k32 += 1
                            if kind == "sigmoid":
                                fw.op("act", lambda e, ps=ps, q=q, n=n: e.activation(out=stg[q][0:n, :], in_=ps[0:n, :], func=AF.Sigmoid),
                                      reads=[psb], writes=[stgb[q]])
                            elif kind == "silu":
                                fw.op("act", lambda e, ps=ps, q=q, n=n: e.activation(out=stg[q][0:n, :], in_=ps[0:n, :], func=AF.Silu),
                                      reads=[psb], writes=[stgb[q]])
                            else:
                                fw.op("dve", lambda e, ps=ps, q=q, n=n: e.tensor_copy(out=stg[q][0:n, :], in_=ps[0:n, :]),
                                      reads=[psb], writes=[stgb[q]])
                            fw.dma("pool", sds[q], self.zT[cc:cc + n, tk:tk + 512], stg[q][0:n, :],
                                   reads=[stgb[q]], writes=[self.b_zT])

    def fox(self, l):
        nc, fw = self.nc, self.fw
        NTI = T // 128
        NG = T // 512
        with contextlib.ExitStack() as st:
            cpT = self.sb(st, "cpT", [128, NTI, 8], F32)
            cpTb = Buf()
            refT = self.sb(st, "refT", [128, NG, 8], F32)
            refTb = Buf()
            with contextlib.ExitStack() as st1:
                sp = self.sb(st1, "fsp", [8, T], F32)
                cp = self.sb(st1, "fcp", [8, T], F32)
                rb = self.sb(st1, "frb", [8, T], BF16)
                nfb = self.sb(st1, "nfb", [8, 1], F32)
                b_sp, b_cp, b_rb, b_nfb = Buf(), Buf(), Buf(), Buf()
                ds = fw.dsem()
                fw.dma("sp", ds, sp[:, :], self.zT[C_FF:C_FF + 8, :], reads=[self.b_zT], writes=[b_sp])
                fw.dma("sp", ds, nfb[:, :], self.p["fox_fb"][l, :].rearrange("(h o) -> h o", o=1), writes=[b_nfb])
                fw.op("dve", lambda e: e.tensor_scalar(out=nfb[:, :], in0=nfb[:, :], scalar1=-1.0, scalar2=None, op0=ALU.mult),
                      reads=[b_nfb], writes=[b_nfb])
                for (o, n) in _slices(T, 2048):
                    fw.op("act", lambda e, o=o, n=n: e.activation(out=sp[:, o:o + n], in_=sp[:, o:o + n], func=AF.Exp, scale=-1.0, bias=nfb[:, 0:1]),
                          reads=[b_sp, b_nfb], writes=[b_sp])
                for (o, n) in _slices(T, 2048):
                    fw.op("act", lambda e, o=o, n=n: e.activation(out=sp[:, o:o + n], in_=sp[:, o:o + n], func=AF.Ln, bias=self.ones_f[0:8, 0:1]),
                          reads=[b_sp, self.b_const], writes=[b_sp])
                for i, (o, n) in enumerate(_slices(T, 2048)):
                    init = 0.0 if i == 0 else cp[:, o - 1:o]
                    self._cumsum(cp, sp, o, n, init, b_sp, b_cp)
                for G in range(NG):
                    ref = self.zero_col[0:8, 0:1] if G == 0 else cp[:, G * 512 - 1:G * 512]
                    fw.op("dve", lambda e, G=G, ref=ref: e.tensor_scalar(out=rb[:, G * 512:(G + 1) * 512], in0=cp[:, G * 512:(G + 1) * 512],
                                                                        scalar1=ref, scalar2=-1.0, op0=ALU.subtract, op1=ALU.mult),
                          reads=[b_cp, self.b_const], writes=[b_rb])
                ds2 = fw.dsem()
                fw.dma("pool", ds2, self.rrow[:, :], rb[:, :], reads=[b_rb], writes=[self.b_rrow])
                for t4 in range(NTI // 16):
                    ps, psb = self.pst()
                    for j in range(16):
                        ti = t4 * 16 + j
                        fw.op("pe", lambda e, ps=ps, j=j, ti=ti: e.transpose(out=ps[:, j * 8:(j + 1) * 8], in_=cp[:, ti * 128:(ti + 1) * 128],
                                                                            identity=self.ident_f[0:8, 0:8]),
                              reads=[b_cp, self.b_const], writes=[psb])
                    fw.op("dve", lambda e, ps=ps, t4=t4: e.tensor_copy(out=cpT[:, t4 * 16:(t4 + 1) * 16, :].rearrange("p a b -> p (a b)"),
                                                                        in_=ps[:, 0:128]), reads=[psb], writes=[cpTb])
                ps, psb = self.pst()
                fw.op("pe", lambda e, ps=ps: e.matmul(ps[:, 0:(NG - 1) * 8].rearrange("p (a b) -> p a b", b=8), lhsT=self.sel127[:, :],
                                                      rhs=cpT[:, :, :].rearrange("p (g f) h -> p g f h", f=4)[:, 0:NG - 1, 3, :], start=True, stop=True),
                      reads=[cpTb, self.b_const], writes=[psb])
                fw.op("dve", lambda e: e.memset(refT[:, 0, :], 0.0), writes=[refTb])
                fw.op("dve", lambda e, ps=ps: e.tensor_copy(out=refT[:, 1:NG, :].rearrange("p a b -> p (a b)"), in_=ps[:, 0:(NG - 1) * 8]),
                      reads=[psb], writes=[refTb])
            fw.barrier()
            V = self.sb(st, "fV", [128, NTI, 512], BF16)
            Vb = Buf()
            ds = fw.dsem()
            for q4 in range(4):
                n4 = NTI // 4
                fw.dma("sp", ds, V[:, q4 * n4:(q4 + 1) * n4, :],
                       self.vtok[q4 * n4 * 128:(q4 + 1) * n4 * 128, 0:512].rearrange("(n p) c -> p n c", p=128),
                       reads=[self.b_vtok], writes=[Vb])
            qpad = [self.sb(st, "qpad%d" % e, [128, T], BF16) for e in range(2)]
            kpad = [self.sb(st, "kpad%d" % e, [128, T], BF16) for e in range(2)]
            qpb = [Buf() for _ in range(2)]
            kpb = [Buf() for _ in range(2)]
            for e_ in range(2):
                fw.op("pool", lambda e, e_=e_: e.memset(qpad[e_][:, :], 0.0), writes=[qpb[e_]])
                fw.op("pool", lambda e, e_=e_: e.memset(kpad[e_][:, :], 0.0), writes=[kpb[e_]])
            fw.op("pool", lambda e: e.memset(kpad[0][64:65, :], 1.0), writes=[kpb[0]])
            fw.op("pool", lambda e: e.memset(kpad[1][0:1, :], 1.0), writes=[kpb[1]])
            if l + 1 < self.NL:
                self.weights_cast(l + 1)
            gq = self.sb(st, "fgq", [128, 1], F32)
            gk = self.sb(st, "fgk", [128, 1], F32)
            gb_ = Buf()
            ds = fw.dsem()
            for half in range(2):
                fw.dma("sp", ds, gq[half * 64:(half + 1) * 64, :], self.p["fox_qg"][l, :].rearrange("(h o) -> h o", o=1), writes=[gb_])
                fw.dma("sp", ds, gk[half * 64:(half + 1) * 64, :], self.p["fox_kg"][l, :].rearrange("(h o) -> h o", o=1), writes=[gb_])
            fw.op("dve", lambda e: e.tensor_scalar(out=gq[:, :], in0=gq[:, :], scalar1=0.125, scalar2=None, op0=ALU.mult), reads=[gb_], writes=[gb_])
            raw = [self.sb(st, "fraw%d" % i, [128, 512], F32) for i in range(2)]
            rawb = [Buf() for _ in range(2)]
            rds = [fw.dsem() for _ in range(2)]
            sq = self.sb(st, "fsq", [128, 512], F32)
            sqb = Buf()
            rs = self.sb(st, "frs", [128, 512], F32)
            rsb = Buf()
            PT = [self.sb(st, "fPT%d" % i, [128, 512], BF16) for i in range(4)]
            PTb = [Buf() for _ in range(4)]
            bias = [self.sb(st, "fbias%d" % i, [128, NTI], F32) for i in range(2)]
            biasb = [Buf() for _ in range(2)]
            rden = self.sb(st, "frden", [64, 512], F32)
            rdenb = Buf()
            ost = [self.sb(st, "fost%d" % i, [64, 512], BF16) for i in range(2)]
            ostb = [Buf() for _ in range(2)]
            ods = [fw.dsem() for _ in range(2)]
            ads = fw.dsem()
            kraw = 0
            kpt = 0
            kb = 0
            ko = 0
            for j in range(4):
                for which in range(2):
                    row0 = (C_FQ if which == 0 else C_FK) + j * 128
                    gcol = gq if which == 0 else gk
                    dst = qpad if which == 0 else kpad
                    dstb = qpb if which == 0 else kpb
                    for G in range(NG):
                        s = kraw % 2
                        kraw += 1
                        fw.dma("sp", rds[s], raw[s][:, :], self.zT[row0:row0 + 128, G * 512:(G + 1) * 512], reads=[self.b_zT], writes=[rawb[s]])
                        fw.op("act", lambda e, s=s: e.activation(out=sq[:, :], in_=raw[s][:, :], func=AF.Square), reads=[rawb[s]], writes=[sqb])
                        ps, psb = self.pst()
                        fw.op("pe", lambda e, ps=ps: e.matmul(ps[:, :], lhsT=self.bo64[:, :], rhs=sq[:, :], start=True, stop=True),
                              reads=[sqb, self.b_const], writes=[psb])
                        fw.op("act", lambda e, ps=ps: e.activation(out=rs[:, :], in_=ps[:, :], func=AF.Sqrt, scale=1.0 / 64, bias=self.eps_col[:, 0:1]),
                              reads=[psb, self.b_const], writes=[rsb])
                        fw.op("dve", lambda e: e.reciprocal(out=rs[:, :], in_=rs[:, :]), reads=[rsb], writes=[rsb])
                        for e_ in range(2):
                            pr = slice(e_ * 64, (e_ + 1) * 64)
                            fw.op("dve", lambda e, s=s, G=G, e_=e_, pr=pr, dst=dst, gcol=gcol: e.scalar_tensor_tensor(
                                out=dst[e_][pr, G * 512:(G + 1) * 512], in0=raw[s][pr, :], scalar=gcol[pr, 0:1], in1=rs[pr, :],
                                op0=ALU.mult, op1=ALU.mult), reads=[rawb[s], rsb, gb_], writes=[dstb[e_]])
                fw.dma("sp", ads, qpad[0][64:65, :], self.rrow[2 * j:2 * j + 1, :], reads=[self.b_rrow], writes=[qpb[0]])
                fw.dma("sp", ads, qpad[1][0:1, :], self.rrow[2 * j + 1:2 * j + 2, :], reads=[self.b_rrow], writes=[qpb[1]])
                for e_ in range(2):
                    h = 2 * j + e_
                    for G in range(NG):
                        nt = 4 * G + 4
                        bs = kb % 2
                        kb += 1
                        fw.op("dve", lambda e, bs=bs, nt=nt, h=h, G=G: e.tensor_scalar(
                            out=bias[bs][:, 0:nt], in0=cpT[:, 0:nt, h], scalar1=refT[:, G, h:h + 1], scalar2=None, op0=ALU.subtract),
                            reads=[cpTb, refTb], writes=[biasb[bs]])
                        acc = 4 + 2 * (G % 2)
                        po, pob = self.ps[acc], self.psb[acc]
                        pd, pdb = self.ps[acc + 1], self.psb[acc + 1]
                        def front(ti, G=G, e_=e_, bs=bs):
                            nonlocal kpt
                            i = ti - 4 * G
                            c0 = max(i, 0) * 128
                            ps, psb = self.pst()
                            fw.op("pe", lambda e: e.matmul(
                                ps[:, c0:512], lhsT=kpad[e_][:, ti * 128:(ti + 1) * 128], rhs=qpad[e_][:, G * 512 + c0:(G + 1) * 512],
                                start=True, stop=(i < 0)), reads=[kpb[e_], qpb[e_]], writes=[psb])
                            if i >= 0:
                                fw.op("pe", lambda e: e.matmul(ps[:, c0:c0 + 128], lhsT=self.ident_b[:, :], rhs=self.maskneg[:, :],
                                                               start=False, stop=True), reads=[self.b_const], writes=[psb])
                            p_ = kpt % 4
                            kpt += 1
                            fw.op("act", lambda e: e.activation(
                                out=PT[p_][:, c0:512], in_=ps[:, c0:512], func=AF.Exp, bias=bias[bs][:, ti:ti + 1]),
                                reads=[psb, biasb[bs]], writes=[PTb[p_]])
                            return (ti, c0, p_)

                        def back(item, h=h, nt=nt, po=po, pob=pob, pd=pd, pdb=pdb):
                            ti, c0, p_ = item
                            fw.op("pe", lambda e: e.matmul(
                                po[0:64, c0:512], lhsT=V[:, ti, h * 64:(h + 1) * 64], rhs=PT[p_][:, c0:512],
                                start=(ti == 0), stop=(ti == nt - 1)), reads=[Vb, PTb[p_]], writes=[pob])
                            fw.op("pe", lambda e: e.matmul(
                                pd[0:64, c0:512], lhsT=self.ones_b[:, 0:64], rhs=PT[p_][:, c0:512],
                                start=(ti == 0), stop=(ti == nt - 1)), reads=[self.b_const, PTb[p_]], writes=[pdb])

                        pend = []
                        for ti in range(nt):
                            pend.append(front(ti))
                            if len(pend) > 2:
                                back(pend.pop(0))
                        for item in pend:
                            back(item)
                        fw.op("dve", lambda e, pd=pd: e.reciprocal(out=rden[:, :], in_=pd[0:64, :]), reads=[pdb], writes=[rdenb])
                        o_ = ko % 2
                        ko += 1
                        fw.op("dve", lambda e, po=po, o_=o_: e.tensor_tensor(out=ost[o_][:, :], in0=po[0:64, :], in1=rden[:, :], op=ALU.mult),
                              reads=[pob, rdenb], writes=[ostb[o_]])
                        fw.dma("sp", ods[o_], self.oT[h * 64:(h + 1) * 64, G * 512:(G + 1) * 512], ost[o_][:, :],
                               reads=[ostb[o_]], writes=[self.b_oT])
        fw.barrier()

    def _cumsum(self, cp, sp, o, n, init, b_sp, b_cp):
        self.fw.op("dve", lambda e: e.tensor_tensor_scan(out=cp[:, o:o + n], data0=self.ones_big[0:8, 0:n], data1=sp[:, o:o + n],
                                                         initial=init, op0=ALU.mult, op1=ALU.add),
                   reads=[b_sp, b_cp, self.b_const], writes=[b_cp])

    def conv(self, l):
        nc, fw = self.nc, self.fw
        NG = T // 512
        with contextlib.ExitStack() as st:
            uT = self.sb(st, "cuT", [128, 4, 32 + T], BF16)
            uTb = Buf()
            diag = self.sb(st, "cdiag", [128, 31, 4, 128], BF16)
            diagb = Buf()
            wT = self.sb(st, "cwT", [128, 4, 32], F32)
            wTb = Buf()
            cols = self.sb(st, "ccols", [128, 3, 4], F32)
            colsb = Buf()
            with contextlib.ExitStack() as st1:
                ds = fw.dsem()
                for cc in range(4):
                    fw.dma("sp", ds, wT[:, cc, 0:31], self.p["conv_dw"][l, :, cc * 128:(cc + 1) * 128].rearrange("k p -> p k"), writes=[wTb],
                           allow_slow_non_contiguous=True)
                for i, nm in enumerate(("conv_db", "conv_ln_g", "conv_ln_b")):
                    fw.dma("sp", ds, cols[:, i, :], self.p[nm][l, :].rearrange("(k p) -> p k", p=128), writes=[colsb],
                           allow_slow_non_contiguous=True)
                for k in range(31):
                    for cc in range(4):
                        eng = "dve"
                        fw.op(eng, lambda e, k=k, cc=cc: e.tensor_scalar(out=diag[:, k, cc, :], in0=self.ident_b[:, :], scalar1=wT[:, cc, k:k + 1],
                                                                          scalar2=None, op0=ALU.mult), reads=[wTb, self.b_const], writes=[diagb])
                fw.op("pool", lambda e: e.memset(uT[:, :, 0:32], 0.0), writes=[uTb])
                a_t = [self.sb(st1, "ca%d" % i, [128, 2048], F32) for i in range(2)]
                g_t = [self.sb(st1, "cg%d" % i, [128, 2048], F32) for i in range(2)]
                atb = [Buf() for _ in range(2)]
                lds = [fw.dsem() for _ in range(2)]
                n = 0
                for cc in range(4):
                    for tb in range(T // 2048):
                        s = n % 2
                        n += 1
                        fw.dma("sp", lds[s], a_t[s][:, :], self.zT[C_CA + cc * 128:C_CA + (cc + 1) * 128, tb * 2048:(tb + 1) * 2048],
                               reads=[self.b_zT], writes=[atb[s]])
                        fw.dma("sp", lds[s], g_t[s][:, :], self.zT[C_CG + cc * 128:C_CG + (cc + 1) * 128, tb * 2048:(tb + 1) * 2048],
                               reads=[self.b_zT], writes=[atb[s]])
                        eng = "dve" if n % 2 else "pool"
                        fw.op(eng, lambda e, s=s, cc=cc, tb=tb: e.tensor_tensor(out=uT[:, cc, 32 + tb * 2048:32 + (tb + 1) * 2048], in0=a_t[s][:, :],
                                                                                in1=g_t[s][:, :], op=ALU.mult), reads=[atb[s]], writes=[uTb])
            self.dump("wT", wT[:, :, :], [128, 4, 32], F32, wTb)
            self.dump("diag", diag[:, 3, 1, :], [128, 128], BF16, diagb)
            self.dump("uT", uT[:, 0, 0:600], [128, 600], BF16, uTb)
            fw.barrier()
            y = [self.sb(st, "cy%d" % i, [128, 4, 512], F32) for i in range(2)]
            yb = [Buf() for _ in range(2)]
            sq = [self.sb(st, "csq%d" % i, [128, 512], F32) for i in range(2)]
            sqb = [Buf() for _ in range(2)]
            rs = self.sb(st, "crs", [128, 512], F32)
            rsb = Buf()
            ob = [self.sb(st, "cob%d" % i, [128, 512], BF16) for i in range(4)]
            obb = [Buf() for _ in range(4)]
            ods = [fw.dsem() for _ in range(4)]
            ko = 0
            for g in range(NG):
                s = g % 2
                for cc in range(4):
                    ps, psb = self.pst()
                    for k in range(31):
                        c0 = 2 + k + g * 512
                        fw.op("pe", lambda e, ps=ps, k=k, cc=cc, c0=c0: e.matmul(ps[:, :], lhsT=diag[:, k, cc, :], rhs=uT[:, cc, c0:c0 + 512],
                                                                                 start=(k == 0), stop=(k == 30)), reads=[diagb, uTb], writes=[psb])
                    fw.op("dve", lambda e, ps=ps, s=s, cc=cc: e.tensor_scalar(out=y[s][:, cc, :], in0=ps[:, :], scalar1=cols[:, 0, cc:cc + 1], scalar2=None, op0=ALU.add),
                          reads=[psb, colsb], writes=[yb[s]])
                if g == 0:
                    self.dump("ypre", y[s][:, :, :], [128, 4, 512], F32, yb[s])
                pm, pmb = self.pst()
                for cc in range(4):
                    fw.op("pe", lambda e, pm=pm, s=s, cc=cc: e.matmul(pm[:, :], lhsT=self.ones_f[:, :], rhs=y[s][:, cc, :], start=(cc == 0), stop=(cc == 3)),
                          reads=[yb[s], self.b_const], writes=[pmb])
                for cc in range(4):
                    fw.op("dve", lambda e, pm=pm, s=s, cc=cc: e.scalar_tensor_tensor(out=y[s][:, cc, :], in0=pm[:, :], scalar=-1.0 / 512, in1=y[s][:, cc, :],
                                                                                     op0=ALU.mult, op1=ALU.add), reads=[pmb, yb[s]], writes=[yb[s]])
                if g == 0:
                    self.dump("ymid", y[s][:, :, :], [128, 4, 512], F32, yb[s])
                pv, pvb = self.pst()
                for cc in range(4):
                    q = cc % 2
                    fw.op("act", lambda e, s=s, cc=cc, q=q: e.activation(out=sq[q][:, :], in_=y[s][:, cc, :], func=AF.Square), reads=[yb[s]], writes=[sqb[q]])
                    fw.op("pe", lambda e, pv=pv, q=q, cc=cc: e.matmul(pv[:, :], lhsT=self.ones_f[:, :], rhs=sq[q][:, :], start=(cc == 0), stop=(cc == 3)),
                          reads=[sqb[q], self.b_const], writes=[pvb])
                fw.op("act", lambda e, pv=pv: e.activation(out=rs[:, :], in_=pv[:, :], func=AF.Sqrt, scale=1.0 / 512, bias=self.eps_col[:, 0:1]),
                      reads=[pvb, self.b_const], writes=[rsb])
                fw.op("dve", lambda e: e.reciprocal(out=rs[:, :], in_=rs[:, :]), reads=[rsb], writes=[rsb])
                if g == 0:
                    self.dump("rs", rs[:, :], [128, 512], F32, rsb)
                for cc in range(4):
                    fw.op("dve", lambda e, s=s, cc=cc: e.tensor_tensor(out=y[s][:, cc, :], in0=y[s][:, cc, :], in1=rs[:, :], op=ALU.mult),
                          reads=[yb[s], rsb], writes=[yb[s]])
                    fw.op("dve", lambda e, s=s, cc=cc: e.tensor_scalar(out=y[s][:, cc, :], in0=y[s][:, cc, :], scalar1=cols[:, 1, cc:cc + 1],
                                                                        scalar2=cols[:, 2, cc:cc + 1], op0=ALU.mult, op1=ALU.add),
                          reads=[yb[s], colsb], writes=[yb[s]])
                    o = ko % 4
                    ko += 1
                    fw.op("act", lambda e, s=s, cc=cc, o=o: e.activation(out=ob[o][:, :], in_=y[s][:, cc, :], func=AF.Silu), reads=[yb[s]], writes=[obb[o]])
                    fw.dma("pool", ods[o], self.oT[512 + cc * 128:512 + (cc + 1) * 128, g * 512:(g + 1) * 512], ob[o][:, :],
                           reads=[obb[o]], writes=[self.b_oT])
        fw.barrier()

    def pool(self, l):
        nc, fw = self.nc, self.fw
        BK = 2048
        with contextlib.ExitStack() as st:
            pw32 = self.sb(st, "ppw32", [128, 4, 128], F32)
            pw = self.sb(st, "ppw", [128, 4, 128], BF16)
            pwb = Buf()
            scol = self.sb(st, "pscol", [128, 4], F32)
            inv = self.sb(st, "pinv", [128, 4, 16], F32)
            smb = Buf()
            ds = fw.dsem()
            fw.dma("sp", ds, pw32[:, :, :], self.p["pool_w"][l, :, :, :].rearrange("g c j -> c g j"), writes=[pwb])
            fw.dma("sp", ds, scol[:, :], self.p["pool_scale"][l, :].rearrange("(k p) -> p k", p=128), writes=[smb], allow_slow_non_contiguous=True)
            fw.dma("sp", ds, inv[:, :, :].rearrange("p a b -> p (a b)"), self.p["invcnt"][:, :].rearrange("a b -> (a b)").partition_broadcast(128), writes=[smb])
            fw.op("dve", lambda e: e.tensor_copy(out=pw[:, :, :], in_=pw32[:, :, :]), reads=[pwb], writes=[pwb])
            bufA = [self.sb(st, "pA%d" % i, [128, 16 + BK], F32) for i in range(2)]
            bufB = [self.sb(st, "pB%d" % i, [128, 16 + BK], F32) for i in range(2)]
            bufC = [self.sb(st, "pC%d" % i, [128, 16 + BK], F32) for i in range(2)]
            Ab = [Buf() for _ in range(2)]
            Bb = [Buf() for _ in range(2)]
            Cb = [Buf() for _ in range(2)]
            lds = [fw.dsem() for _ in range(2)]
            mx = [self.sb(st, "pmx%d" % i, [128, BK], BF16) for i in range(2)]
            mxb = [Buf() for _ in range(2)]
            ob = [self.sb(st, "pob%d" % i, [128, 512], BF16) for i in range(4)]
            obb = [Buf() for _ in range(4)]
            ods = [fw.dsem() for _ in range(4)]
            n = 0
            ko = 0
            for gi in range(4):
                w = 2 << gi
                row0 = C_PZ + gi * 128
                for bk in range(T // BK):
                    s = n % 2
                    n += 1
                    A, B, C = bufA[s], bufB[s], bufC[s]
                    fw.dma("sp", lds[s], A[:, 16:16 + BK], self.zT[row0:row0 + 128, bk * BK:(bk + 1) * BK], reads=[self.b_zT], writes=[Ab[s]])
                    if bk == 0:
                        fw.op("pool", lambda e, A=A: e.memset(A[:, 0:16], 0.0), writes=[Ab[s]])
                    else:
                        fw.dma("sp", lds[s], A[:, 0:16], self.zT[row0:row0 + 128, bk * BK - 16:bk * BK], reads=[self.b_zT], writes=[Ab[s]])
                    src, srcb = A, Ab[s]
                    dsts = [(B, Bb[s]), (C, Cb[s])]
                    d = 1
                    k = 0
                    while d < w:
                        dst, dstb = dsts[k % 2]
                        k += 1
                        fw.op("dve", lambda e, dst=dst, src=src, d=d: e.tensor_tensor(out=dst[:, d:16 + BK], in0=src[:, d:16 + BK], in1=src[:, 0:16 + BK - d], op=ALU.add),
                              reads=[srcb], writes=[dstb])
                        src, srcb = dst, dstb
                        d *= 2
                    fw.op("dve", lambda e, src=src, A=A, s=s, w=w: e.scalar_tensor_tensor(out=mx[s][:, :], in0=src[:, 16:16 + BK], scalar=1.0 / w, in1=A[:, 16:16 + BK],
                                                                                          op0=ALU.mult, op1=ALU.subtract), reads=[srcb, Ab[s]], writes=[mxb[s]])
                    if bk == 0:
                        fw.op("dve", lambda e, src=src, gi=gi: e.tensor_tensor(out=src[:, 16:32], in0=src[:, 16:32], in1=inv[:, gi, :], op=ALU.mult),
                              reads=[srcb, smb], writes=[srcb])
                        fw.op("dve", lambda e, src=src, A=A, s=s: e.tensor_tensor(out=mx[s][:, 0:16], in0=src[:, 16:32], in1=A[:, 16:32], op=ALU.subtract),
                              reads=[srcb, Ab[s]], writes=[mxb[s]])
                    for tg in range(BK // 512):
                        ps, psb = self.pst()
                        fw.op("pe", lambda e, ps=ps, gi=gi, s=s, tg=tg: e.matmul(ps[:, :], lhsT=pw[:, gi, :], rhs=mx[s][:, tg * 512:(tg + 1) * 512], start=True, stop=True),
                              reads=[pwb, mxb[s]], writes=[psb])
                        o = ko % 4
                        ko += 1
                        fw.op("act", lambda e, ps=ps, o=o, gi=gi: e.activation(out=ob[o][:, :], in_=ps[:, :], func=AF.Copy, scale=scol[:, gi:gi + 1]),
                              reads=[psb, smb], writes=[obb[o]])
                        t0 = bk * BK + tg * 512
                        fw.dma("pool", ods[o], self.oT[1536 + gi * 128:1536 + (gi + 1) * 128, t0:t0 + 512], ob[o][:, :], reads=[obb[o]], writes=[self.b_oT])
        fw.barrier()

    def gla(self, l):
        nc, fw = self.nc, self.fw
        BK = 2048
        NTB = BK // 128
        with contextlib.ExitStack() as st:
            wa = self.sb(st, "gwa", [16, 256], F32)
            nba = self.sb(st, "gnba", [128, 2], F32)
            og = self.sb(st, "gog", [128, 4], F32)
            smb = Buf()
            ds = fw.dsem()
            fw.dma("sp", ds, wa[:, :], self.p["gla_wa"][l, :, :], writes=[smb])
            fw.dma("sp", ds, nba[:, :], self.p["gla_ba"][l, :].rearrange("(k p) -> p k", p=128), writes=[smb], allow_slow_non_contiguous=True)
            fw.dma("sp", ds, og[:, :], self.p["gla_og"][l, :].rearrange("(k p) -> p k", p=128), writes=[smb], allow_slow_non_contiguous=True)
            fw.op("dve", lambda e: e.tensor_scalar(out=nba[:, :], in0=nba[:, :], scalar1=-1.0, scalar2=None, op0=ALU.mult), reads=[smb], writes=[smb])
            alow = self.sb(st, "galow", [16, BK], F32)
            alowb = Buf()
            la = self.sb(st, "gla", [128, BK], F32)
            bp = self.sb(st, "gbp", [128, BK], F32)
            et = self.sb(st, "get", [128, BK], F32)
            qr = self.sb(st, "gqr", [128, BK], F32)
            kr = self.sb(st, "gkr", [128, BK], F32)
            lab, bpb, etb, qrb, krb = Buf(), Buf(), Buf(), Buf(), Buf()
            nbl = self.sb(st, "gnbl", [128, NTB], F32)
            dec = self.sb(st, "gdec", [128, NTB], F32)
            nblb = Buf()
            qt = self.sb(st, "gqt", [128, BK], BF16)
            qtb = Buf()
            kt = [self.sb(st, "gkt%d" % e, [128, BK], BF16) for e in range(2)]
            ktb = [Buf() for _ in range(2)]
            kd = self.sb(st, "gkd", [128, BK], BF16)
            kdb = Buf()
            kdt = self.sb(st, "gkdt", [128, NTB, 128], BF16)
            kdtb = Buf()
            Vt = self.sb(st, "gV", [128, NTB, 256], BF16)
            Vtb = Buf()
            KV = self.sb(st, "gKV", [128, NTB, 128], F32)
            KVb = Buf()
            S = self.sb(st, "gS", [128, 128], F32)
            Sb = Buf()
            Sbf = [self.sb(st, "gSbf%d" % e, [128, NTB, 128], BF16) for e in range(2)]
            Sbfb = [Buf() for _ in range(2)]
            Asb = [self.sb(st, "gA%d" % i, [128, 128], BF16) for i in range(3)]
            Asbb = [Buf() for _ in range(3)]
            orw = [self.sb(st, "gor%d" % i, [128, 512], F32) for i in range(2)]
            orwb = [Buf() for _ in range(2)]
            sq = self.sb(st, "gsq", [128, 512], F32)
            sqb = Buf()
            rs = self.sb(st, "grs", [128, 512], F32)
            rsb = Buf()
            rg = [self.sb(st, "grg%d" % i, [128, 512], F32) for i in range(2)]
            rgb = [Buf() for _ in range(2)]
            rds = [fw.dsem() for _ in range(2)]
            ob = [self.sb(st, "gob%d" % i, [128, 512], BF16) for i in range(2)]
            obb = [Buf() for _ in range(2)]
            ods = [fw.dsem() for _ in range(2)]
            lds = fw.dsem()
            for e_ in range(2):
                fw.op("pool", lambda e, e_=e_: e.memset(kt[e_][:, :], 0.0), writes=[ktb[e_]])
                fw.op("pool", lambda e, e_=e_: e.memset(Sbf[e_][:, :, :], 0.0), writes=[Sbfb[e_]])
            ka = 0
            ko = 0
            krg = 0
            for j in range(2):
                fw.op("dve", lambda e: e.memset(S[:, :], 0.0), writes=[Sb])
                for bk in range(T // BK):
                    t0 = bk * BK
                    fw.dma("sp", lds, alow[:, :], self.zT[C_GA:C_GA + 16, t0:t0 + BK], reads=[self.b_zT], writes=[alowb])
                    fw.dma("sp", lds, qr[:, :], self.zT[C_GQ + j * 128:C_GQ + (j + 1) * 128, t0:t0 + BK], reads=[self.b_zT], writes=[qrb])
                    fw.dma("sp", lds, kr[:, :], self.zT[C_GK + j * 128:C_GK + (j + 1) * 128, t0:t0 + BK], reads=[self.b_zT], writes=[krb])
                    fw.dma("sp", lds, Vt[:, :, :], self.vtok[t0:t0 + BK, 512 + j * 256:512 + (j + 1) * 256].rearrange("(n p) c -> p n c", p=128),
                           reads=[self.b_vtok], writes=[Vtb])
                    for tg in range(BK // 512):
                        ps, psb = self.pst()
                        fw.op("pe", lambda e, ps=ps, tg=tg, j=j: e.matmul(ps[:, :], lhsT=wa[:, j * 128:(j + 1) * 128], rhs=alow[:, tg * 512:(tg + 1) * 512],
                                                                         start=True, stop=True), reads=[smb, alowb], writes=[psb])
                        fw.op("act", lambda e, ps=ps, tg=tg, j=j: e.activation(out=la[:, tg * 512:(tg + 1) * 512], in_=ps[:, :], func=AF.Exp, scale=-1.0,
                                                                              bias=nba[:, j:j + 1]), reads=[psb, smb], writes=[lab])
                    fw.op("act", lambda e: e.activation(out=la[:, :], in_=la[:, :], func=AF.Ln, bias=self.ones_f[:, 0:1]), reads=[lab, self.b_const], writes=[lab])
                    for n in range(NTB):
                        fw.op("dve", lambda e, n=n: e.tensor_tensor_scan(out=bp[:, n * 128:(n + 1) * 128], data0=self.ones_f[:, :], data1=la[:, n * 128:(n + 1) * 128],
                                                                         initial=0.0, op0=ALU.mult, op1=ALU.add), reads=[lab, self.b_const], writes=[bpb])
                    fw.op("dve", lambda e: e.tensor_scalar(out=nbl[:, :], in0=bp[:, :].rearrange("p (n t) -> p n t", t=128)[:, :, 127], scalar1=-1.0 / 16, scalar2=None,
                                                           op0=ALU.mult), reads=[bpb], writes=[nblb])
                    fw.op("act", lambda e: e.activation(out=dec[:, :], in_=nbl[:, :], func=AF.Exp), reads=[nblb], writes=[nblb])
                    fw.op("act", lambda e: e.activation(out=et[:, :], in_=bp[:, :], func=AF.Exp, scale=-1.0 / 16), reads=[bpb], writes=[etb])
                    fw.op("dve", lambda e: e.scalar_tensor_tensor(out=qt[:, :], in0=qr[:, :], scalar=0.125, in1=et[:, :], op0=ALU.mult, op1=ALU.mult),
                          reads=[qrb, etb], writes=[qtb])
                    fw.op("act", lambda e: e.activation(out=et[:, :], in_=bp[:, :], func=AF.Exp, scale=1.0 / 16), reads=[bpb, qtb], writes=[etb])
                    for e_ in range(2):
                        pr = slice(e_ * 64, (e_ + 1) * 64)
                        fw.op("dve", lambda e, e_=e_, pr=pr: e.tensor_tensor(out=kt[e_][pr, :], in0=kr[pr, :], in1=et[pr, :], op=ALU.mult),
                              reads=[krb, etb], writes=[ktb[e_]])
                    for n in range(NTB):
                        fw.op("act", lambda e, n=n: e.activation(out=et[:, n * 128:(n + 1) * 128], in_=bp[:, n * 128:(n + 1) * 128], func=AF.Exp, scale=1.0 / 16,
                                                                 bias=nbl[:, n:n + 1]), reads=[bpb, nblb, ktb[0], ktb[1]], writes=[etb])
                    fw.op("dve", lambda e: e.tensor_tensor(out=kd[:, :], in0=kr[:, :], in1=et[:, :], op=ALU.mult), reads=[krb, etb], writes=[kdb])
                    for n4 in range(NTB // 4):
                        ps, psb = self.pst()
                        psv = ps[:, 0:256].bitcast(BF16)
                        for i in range(4):
                            n = n4 * 4 + i
                            fw.op("pe", lambda e, psv=psv, i=i, n=n: e.transpose(out=psv[:, i * 128:(i + 1) * 128], in_=kd[:, n * 128:(n + 1) * 128],
                                                                                 identity=self.ident_b[:, :]), reads=[kdb, self.b_const], writes=[psb])
                        fw.op("dve", lambda e, psv=psv, n4=n4: e.tensor_copy(out=kdt[:, n4 * 4:(n4 + 1) * 4, :].rearrange("p a b -> p (a b)"), in_=psv[:, :]),
                              reads=[psb], writes=[kdtb])
                    for n in range(NTB):
                        ps, psb = self.pst()
                        for e_ in range(2):
                            fw.op("pe", lambda e, ps=ps, n=n, e_=e_: e.matmul(ps[:, e_ * 128:(e_ + 1) * 128], lhsT=kdt[:, n, :], rhs=Vt[:, n, e_ * 128:(e_ + 1) * 128],
                                                                             start=True, stop=True), reads=[kdtb, Vtb], writes=[psb])
                        for e_ in range(2):
                            pr = slice(e_ * 64, (e_ + 1) * 64)
                            fw.op("act", lambda e, ps=ps, n=n, e_=e_, pr=pr: e.copy(out=KV[pr, n, :], in_=ps[pr, e_ * 128:(e_ + 1) * 128]),
                                  reads=[psb], writes=[KVb])
                    for n in range(NTB):
                        for e_ in range(2):
                            pr = slice(e_ * 64, (e_ + 1) * 64)
                            fw.op("pool", lambda e, n=n, e_=e_, pr=pr: e.tensor_copy(out=Sbf[e_][pr, n, :], in_=S[pr, :]), reads=[Sb], writes=[Sbfb[e_]])
                        fw.op("dve", lambda e, n=n: e.scalar_tensor_tensor(out=S[:, :], in0=S[:, :], scalar=dec[:, n:n + 1], in1=KV[:, n, :],
                                                                           op0=ALU.mult, op1=ALU.add), reads=[Sb, nblb, KVb, Sbfb[0], Sbfb[1]], writes=[Sb])
                    for e_ in range(2):
                        h = 2 * j + e_
                        for tg in range(BK // 512):
                            po, pob = self.ps[4 + (ko % 2)], self.psb[4 + (ko % 2)]
                            for i in range(4):
                                n = tg * 4 + i
                                tsl = slice(n * 128, (n + 1) * 128)
                                ps, psb = self.pst()
                                fw.op("pe", lambda e, ps=ps, e_=e_, tsl=tsl: e.matmul(ps[:, 0:128], lhsT=kt[e_][:, tsl], rhs=qt[:, tsl], start=True, stop=True),
                                      reads=[ktb[e_], qtb], writes=[psb])
                                a_ = ka % 3
                                ka += 1
                                fw.op("dve", lambda e, ps=ps, a_=a_: e.tensor_tensor(out=Asb[a_][:, :], in0=ps[:, 0:128], in1=self.U_b[:, :], op=ALU.mult),
                                      reads=[psb, self.b_const], writes=[Asbb[a_]])
                                fw.op("pe", lambda e, po=po, i=i, n=n, e_=e_, a_=a_: e.matmul(po[:, i * 128:(i + 1) * 128], lhsT=Vt[:, n, e_ * 128:(e_ + 1) * 128], rhs=Asb[a_][:, :],
                                                                                             start=True, stop=False), reads=[Vtb, Asbb[a_]], writes=[pob])
                                fw.op("pe", lambda e, po=po, i=i, n=n, e_=e_, tsl=tsl: e.matmul(po[:, i * 128:(i + 1) * 128], lhsT=Sbf[e_][:, n, :], rhs=qt[:, tsl],
                                                                                                start=False, stop=True), reads=[Sbfb[e_], qtb], writes=[pob])
                            o_ = ko % 2
                            ko += 1
                            fw.op("act", lambda e, po=po, o_=o_: e.copy(out=orw[o_][:, :], in_=po[:, :]), reads=[pob], writes=[orwb[o_]])
                            fw.op("act", lambda e, o_=o_: e.activation(out=sq[:, :], in_=orw[o_][:, :], func=AF.Square), reads=[orwb[o_]], writes=[sqb])
                            pn, pnb = self.pst()
                            fw.op("pe", lambda e, pn=pn: e.matmul(pn[:, :], lhsT=self.ones_f[:, :], rhs=sq[:, :], start=True, stop=True), reads=[sqb, self.b_const], writes=[pnb])
                            fw.op("act", lambda e, pn=pn: e.activation(out=rs[:, :], in_=pn[:, :], func=AF.Sqrt, scale=1.0 / 128, bias=self.eps_col[:, 0:1]),
                                  reads=[pnb, self.b_const], writes=[rsb])
                            fw.op("dve", lambda e: e.reciprocal(out=rs[:, :], in_=rs[:, :]), reads=[rsb], writes=[rsb])
                            g_ = krg % 2
                            krg += 1
                            tk = t0 + tg * 512
                            fw.dma("sp", rds[g_], rg[g_][:, :], self.zT[C_GR + h * 128:C_GR + (h + 1) * 128, tk:tk + 512], reads=[self.b_zT], writes=[rgb[g_]])
                            fw.op("dve", lambda e, o_=o_, h=h: e.scalar_tensor_tensor(out=orw[o_][:, :], in0=orw[o_][:, :], scalar=og[:, h:h + 1], in1=rs[:, :],
                                                                                      op0=ALU.mult, op1=ALU.mult), reads=[orwb[o_], smb, rsb], writes=[orwb[o_]])
                            fw.op("dve", lambda e, o_=o_, g_=g_: e.tensor_tensor(out=ob[o_][:, :], in0=orw[o_][:, :], in1=rg[g_][:, :], op=ALU.mult),
                                  reads=[orwb[o_], rgb[g_]], writes=[obb[o_]])
                            fw.dma("pool", ods[o_], self.oT[1024 + h * 128:1024 + (h + 1) * 128, tk:tk + 512], ob[o_][:, :], reads=[obb[o_]], writes=[self.b_oT])
        fw.barrier()

    def branch_out(self, l, sb):
        nc, fw = self.nc, self.fw
        NT = 1024
        t0 = sb * NT
        Wbr, Wo = self.W[l]["w_branch"], self.W[l]["w_out"]
        gview = self.gatesT.rearrange("(n d) t -> d n t", n=4)
        with contextlib.ExitStack() as st:
            oTb = self.sb(st, "boT", [128, 16, NT], BF16)
            oTbb = Buf()
            yT = self.sb(st, "byT", [128, 16, NT], BF16)
            yTb = Buf()
            wt = [self.sb(st, "bw%d" % i, [128, 16, 512], BF16) for i in range(2)]
            wtb = [Buf() for _ in range(2)]
            wds = [fw.dsem() for _ in range(2)]
            gt = [self.sb(st, "bg%d" % i, [128, 4, 512], BF16) for i in range(2)]
            gtb = [Buf() for _ in range(2)]
            gds = [fw.dsem() for _ in range(2)]
            m = [self.sb(st, "bm%d" % i, [128, 512], F32) for i in range(4)]
            mb = [Buf() for _ in range(4)]
            xc = [self.sb(st, "bx%d" % i, [128, 512], F32) for i in range(3)]
            xcb = [Buf() for _ in range(3)]
            xds = [fw.dsem() for _ in range(3)]
            ds = fw.dsem()
            fw.dma("sp", ds, oTb[:, :, :], self.oT[:, t0:t0 + NT].rearrange("(k p) t -> p k t", p=128), reads=[self.b_oT], writes=[oTbb])
            kw = 0
            kg = 0
            kp = 0
            for db in range(4):
                s = kw % 2
                kw += 1
                self.load_w(Wbr, self.Wb[l], db * 512, 512, wt[s], wtb[s], wds[s])
                for sc in range(4):
                    dch = db * 4 + sc
                    for tg in range(NT // 512):
                        tk = t0 + tg * 512
                        g_ = kg % 2
                        kg += 1
                        fw.dma("sp", gds[g_], gt[g_][:, :, :], gview[dch * 128:(dch + 1) * 128, :, tk:tk + 512], reads=[self.b_gates], writes=[gtb[g_]])
                        for n in range(4):
                            pi = kp % 8
                            kp += 1
                            ps, psb = self.ps[pi], self.psb[pi]
                            for cc in range(4):
                                fw.op("pe", lambda e, ps=ps, n=n, cc=cc, s=s, sc=sc, tg=tg: e.matmul(
                                    ps[:, :], lhsT=wt[s][:, n * 4 + cc, sc * 128:(sc + 1) * 128], rhs=oTb[:, n * 4 + cc, tg * 512:(tg + 1) * 512],
                                    start=(cc == 0), stop=(cc == 3)), reads=[wtb[s], oTbb], writes=[psb])
                            fw.op("dve", lambda e, ps=ps, n=n, g_=g_: e.tensor_tensor(out=m[n][:, :], in0=ps[:, :], in1=gt[g_][:, n, :], op=ALU.mult),
                                  reads=[psb, gtb[g_]], writes=[mb[n]])
                        fw.op("pool", lambda e: e.tensor_tensor(out=m[0][:, :], in0=m[0][:, :], in1=m[1][:, :], op=ALU.add), reads=[mb[0], mb[1]], writes=[mb[0]])
                        fw.op("pool", lambda e: e.tensor_tensor(out=m[2][:, :], in0=m[2][:, :], in1=m[3][:, :], op=ALU.add), reads=[mb[2], mb[3]], writes=[mb[2]])
                        fw.op("pool", lambda e, dch=dch, tg=tg: e.tensor_tensor(out=yT[:, dch, tg * 512:(tg + 1) * 512], in0=m[0][:, :], in1=m[2][:, :], op=ALU.add),
                              reads=[mb[0], mb[2]], writes=[yTb])
            kx = 0
            for db in range(4):
                s = kw % 2
                kw += 1
                self.load_w(Wo, self.Wb[l], db * 512, 512, wt[s], wtb[s], wds[s])
                for sc in range(4):
                    dch = db * 4 + sc
                    for tg in range(NT // 512):
                        tk = t0 + tg * 512
                        x_ = kx % 3
                        kx += 1
                        xb_ = Buf()
                        fw.dma("sp", xds[x_], xc[x_][:, :], self.xT[dch * 128:(dch + 1) * 128, tk:tk + 512], reads=[xb_], writes=[xcb[x_]])
                        pi = kp % 8
                        kp += 1
                        ps, psb = self.ps[pi], self.psb[pi]
                        for dc in range(16):
                            fw.op("pe", lambda e, ps=ps, dc=dc, s=s, sc=sc, tg=tg: e.matmul(
                                ps[:, :], lhsT=wt[s][:, dc, sc * 128:(sc + 1) * 128], rhs=yT[:, dc, tg * 512:(tg + 1) * 512],
                                start=(dc == 0), stop=(dc == 15)), reads=[wtb[s], yTb], writes=[psb])
                        fw.op("dve", lambda e, ps=ps, x_=x_: e.tensor_tensor(out=xc[x_][:, :], in0=ps[:, :], in1=xc[x_][:, :], op=ALU.add),
                              reads=[psb, xcb[x_]], writes=[xcb[x_]])
                        fw.dma("pool", xds[x_], self.xT[dch * 128:(dch + 1) * 128, tk:tk + 512], xc[x_][:, :], reads=[xcb[x_]], writes=[xb_])
        fw.barrier()

    def ffn(self, l):
        nc, fw = self.nc, self.fw
        NT = 1024
        Wup, Wdn = self.W[l]["ffn_up"], self.W[l]["ffn_down"]
        NCH = 2 * DFF // 128
        NP_ = NCH // 2
        with contextlib.ExitStack() as st:
            hT = self.sb(st, "fhT", [128, 16, 2 + NT], BF16)
            hTb = Buf()
            hh = self.sb(st, "fhh", [128, 16, 2], BF16)
            hhb = Buf()
            dwc = self.sb(st, "fdwc", [128, 3, NCH], F32)
            dbc = self.sb(st, "fdbc", [128, NCH], F32)
            cb = Buf()
            ds = fw.dsem()
            for k in range(3):
                fw.dma("sp", ds, dwc[:, k, :], self.p["ffn_dw"][l, k, :].rearrange("(c p) -> p c", p=128), writes=[cb], allow_slow_non_contiguous=True)
            fw.dma("sp", ds, dbc[:, :], self.p["ffn_db"][l, :].rearrange("(c p) -> p c", p=128), writes=[cb], allow_slow_non_contiguous=True)
            fw.op("pool", lambda e: e.memset(hh[:, :, :], 0.0), writes=[hhb])
            for sg in range(T // NT):
                t0 = sg * NT
                fw.op("pool", lambda e: e.tensor_copy(out=hT[:, :, 0:2], in_=hh[:, :, :]), reads=[hhb], writes=[hTb])
                self.rmsnorm_T("norm2_g", l, t0, NT, hT, hTb, 2)
                fw.op("pool", lambda e: e.tensor_copy(out=hh[:, :, :], in_=hT[:, :, NT:NT + 2]), reads=[hTb], writes=[hhb])
                with contextlib.ExitStack() as s2:
                    gT = self.sb(s2, "fgT", [128, NP_, 512], BF16)
                    gTb = Buf()
                    wu = [self.sb(s2, "fwu%d" % i, [128, 16, 512], BF16) for i in range(2)]
                    wub = [Buf() for _ in range(2)]
                    uds = [fw.dsem() for _ in range(2)]
                    wd = [self.sb(s2, "fwd%d" % i, [128, NP_, 256], BF16) for i in range(2)]
                    wdb = [Buf() for _ in range(2)]
                    dds = [fw.dsem() for _ in range(2)]
                    ub = [self.sb(s2, "fub%d" % i, [128, 2 + 512], F32) for i in range(4)]
                    ubb = [Buf() for _ in range(4)]
                    ac = [self.sb(s2, "fac%d" % i, [128, 512], F32) for i in range(4)]
                    acb = [Buf() for _ in range(4)]
                    xc = [self.sb(s2, "fx%d" % i, [128, 512], F32) for i in range(3)]
                    xcb = [Buf() for _ in range(3)]
                    xds = [fw.dsem() for _ in range(3)]
                    kb = 0
                    ku = 0
                    kd = 0
                    kx = 0
                    for g in range(NT // 512):
                        tk = t0 + g * 512
                        for cp in range(NP_):
                            if cp % 2 == 0:
                                s = kb % 2
                                kb += 1
                                fw.dma("sp", uds[s], wu[s][:, :, 0:256], Wup[:, cp * 128:cp * 128 + 256].rearrange("(k p) c -> p k c", p=128),
                                       reads=[self.Wb[l]], writes=[wub[s]])
                                fw.dma("sp", uds[s], wu[s][:, :, 256:512], Wup[:, DFF + cp * 128:DFF + cp * 128 + 256].rearrange("(k p) c -> p k c", p=128),
                                       reads=[self.Wb[l]], writes=[wub[s]])
                            res = []
                            for which in range(2):
                                ch = cp + which * NP_
                                wo = which * 256 + (cp % 2) * 128
                                ps, psb = self.pst()
                                for dc in range(16):
                                    fw.op("pe", lambda e, ps=ps, dc=dc, s=s, wo=wo, g=g: e.matmul(
                                        ps[:, :], lhsT=wu[s][:, dc, wo:wo + 128], rhs=hT[:, dc, 2 + g * 512:2 + (g + 1) * 512],
                                        start=(dc == 0), stop=(dc == 15)), reads=[wub[s], hTb], writes=[psb])
                                ph, phb = self.ps[4 + (ku % 4)], self.psb[4 + (ku % 4)]
                                for dc in range(16):
                                    fw.op("pe", lambda e, ph=ph, dc=dc, s=s, wo=wo, g=g: e.matmul(
                                        ph[:, 0:2], lhsT=wu[s][:, dc, wo:wo + 128], rhs=hT[:, dc, g * 512:g * 512 + 2],
                                        start=(dc == 0), stop=(dc == 15)), reads=[wub[s], hTb], writes=[phb])
                                u_ = ku % 4
                                ku += 1
                                fw.op("act", lambda e, ps=ps, u_=u_: e.copy(out=ub[u_][:, 2:514], in_=ps[:, :]), reads=[psb], writes=[ubb[u_]])
                                fw.op("act", lambda e, ph=ph, u_=u_: e.copy(out=ub[u_][:, 0:2], in_=ph[:, 0:2]), reads=[phb], writes=[ubb[u_]])
                                fw.op("dve", lambda e, u_=u_, ch=ch: e.tensor_scalar(out=ac[u_][:, :], in0=ub[u_][:, 2:514], scalar1=dwc[:, 2, ch:ch + 1],
                                                                                    scalar2=dbc[:, ch:ch + 1], op0=ALU.mult, op1=ALU.add),
                                      reads=[ubb[u_], cb], writes=[acb[u_]])
                                fw.op("dve", lambda e, u_=u_, ch=ch: e.scalar_tensor_tensor(out=ac[u_][:, :], in0=ub[u_][:, 1:513], scalar=dwc[:, 1, ch:ch + 1],
                                                                                           in1=ac[u_][:, :], op0=ALU.mult, op1=ALU.add),
                                      reads=[ubb[u_], cb, acb[u_]], writes=[acb[u_]])
                                fw.op("dve", lambda e, u_=u_, ch=ch: e.scalar_tensor_tensor(out=ac[u_][:, :], in0=ub[u_][:, 0:512], scalar=dwc[:, 0, ch:ch + 1],
                                                                                           in1=ac[u_][:, :], op0=ALU.mult, op1=ALU.add),
                                      reads=[ubb[u_], cb, acb[u_]], writes=[acb[u_]])
                                res.append(u_)
                            ua, uv = res
                            fw.op("act", lambda e, ua=ua: e.activation(out=ac[ua][:, :], in_=ac[ua][:, :], func=AF.Silu), reads=[acb[ua]], writes=[acb[ua]])
                            fw.op("pool", lambda e, ua=ua, uv=uv, cp=cp: e.tensor_tensor(out=gT[:, cp, :], in0=ac[ua][:, :], in1=ac[uv][:, :], op=ALU.mult),
                                  reads=[acb[ua], acb[uv]], writes=[gTb])
                        for db8 in range(D // 256):
                            s = kd % 2
                            kd += 1
                            fw.dma("sp", dds[s], wd[s][:, :, :], Wdn[:, db8 * 256:(db8 + 1) * 256].rearrange("(k p) c -> p k c", p=128),
                                   reads=[self.Wb[l]], writes=[wdb[s]])
                            for sc in range(2):
                                dch = db8 * 2 + sc
                                x_ = kx % 3
                                kx += 1
                                xb_ = Buf()
                                fw.dma("sp", xds[x_], xc[x_][:, :], self.xT[dch * 128:(dch + 1) * 128, tk:tk + 512], reads=[xb_], writes=[xcb[x_]])
                                ps, psb = self.pst()
                                for fc in range(NP_):
                                    fw.op("pe", lambda e, ps=ps, fc=fc, s=s, sc=sc: e.matmul(
                                        ps[:, :], lhsT=wd[s][:, fc, sc * 128:(sc + 1) * 128], rhs=gT[:, fc, :],
                                        start=(fc == 0), stop=(fc == NP_ - 1)), reads=[wdb[s], gTb], writes=[psb])
                                fw.op("dve", lambda e, ps=ps, x_=x_: e.tensor_tensor(out=xc[x_][:, :], in0=ps[:, :], in1=xc[x_][:, :], op=ALU.add),
                                      reads=[psb, xcb[x_]], writes=[xcb[x_]])
                                fw.dma("pool", xds[x_], self.xT[dch * 128:(dch + 1) * 128, tk:tk + 512], xc[x_][:, :], reads=[xcb[x_]], writes=[xb_])
                fw.barrier()
        fw.barrier()


def _build_inputs(inputs, nl, ncores):
    ins = []
    x = np.asarray(inputs["x"], dtype=np.float32)
    inv = np.zeros((4, 16), np.float32)
    for gi, w in enumerate((2, 4, 8, 16)):
        inv[gi] = 1.0 / np.minimum(np.arange(16) + 1, w)
    shared = {}
    for name in ("w_in", "w_out", "ffn_up", "ffn_down", "norm1_g", "fox_fb", "fox_qg", "fox_kg", "conv_dw", "conv_db",
                 "conv_ln_g", "conv_ln_b", "gla_wa", "gla_ba", "gla_og", "pool_w", "pool_scale", "gate_b", "norm2_g",
                 "ffn_dw", "ffn_db"):
        shared[name] = np.ascontiguousarray(np.asarray(inputs[name][:nl], dtype=np.float32))
    shared["w_branch"] = np.ascontiguousarray(np.asarray(inputs["w_branch"][:nl], dtype=np.float32).reshape(nl, 4 * 512, D))
    shared["invcnt"] = inv
    per = ncores // 2
    for c in range(ncores):
        m = dict(shared)
        m["x"] = np.ascontiguousarray(x[c // per])
        ins.append(m)
    return ins


def _run(inputs, nl=DEPTH, debug=(), trace=False, ncores=NCORES, stop=None):
    k = Kern(nl, debug, stop)
    nc = k.build()
    ins = _build_inputs(inputs, nl, ncores)
    res = run_bass_kernel_spmd(nc, ins, core_ids=list(range(ncores)), **({"trace": True} if trace else {}))
    return res


def kernel(**inputs):
    res = _run(inputs)
    per = NCORES // 2
    return np.stack([np.asarray(res.results[0]["y"]), np.asarray(res.results[per]["y"])], axis=0).astype(np.float32)
```
